# Optimizing a Trainium2 kernel written in Bass

```python
import jax, jax.numpy as jnp
from jax import lax
import numpy as np

D_MODEL = 1024
BATCH = 8
SEQ = 8192
DEPTH = 1
DEC_BATCH = 8
DEC_SEQ = 64
PAST_LEN = 1024

CHUNK = 64
Q_BLOCK = 128
PLE_DIM = 256
N_HEADS = 8
QK_NOPE = 64
QK_ROPE = 32
V_DIM = 64
Q_LORA = 384
KV_LORA = 256
ROPE_BASE = 10000.0
CONV_CH = 512
CONV_K = 31
ATTN_W = N_HEADS * V_DIM
MIX_W = ATTN_W + CONV_CH
D_FF = 4 * D_MODEL
IN_W = Q_LORA + KV_LORA + QK_ROPE + 2 * CONV_CH
QK_DIM = QK_NOPE + QK_ROPE
EPS = 1e-6
LN_EPS = 1e-5
NEG = -1e30

kernel_name = 'hybrid_mla_conformer_stream_step'


def rms_norm(x, g):
    xf = x.astype(jnp.float32)
    y = xf * lax.rsqrt(jnp.mean(xf * xf, axis=-1, keepdims=True) + EPS)
    return (y * g.astype(jnp.float32)).astype(x.dtype)


def layer_norm(x, g, b):
    xf = x.astype(jnp.float32)
    mu = jnp.mean(xf, axis=-1, keepdims=True)
    var = jnp.mean(jnp.square(xf - mu), axis=-1, keepdims=True)
    y = (xf - mu) * lax.rsqrt(var + LN_EPS)
    return (y * g.astype(jnp.float32) + b.astype(jnp.float32)).astype(x.dtype)


def rope(x, pos):
    half = QK_ROPE // 2
    inv = ROPE_BASE ** (-jnp.arange(half, dtype=jnp.float32) / half)
    ang = pos.astype(jnp.float32)[:, None] * inv[None, :]
    ang = ang.reshape(ang.shape[:1] + (1,) * (x.ndim - 3) + ang.shape[1:])
    cos, sin = jnp.cos(ang), jnp.sin(ang)
    xf = x.astype(jnp.float32)
    x1, x2 = xf[..., :half], xf[..., half:]
    return jnp.concatenate([x1 * cos - x2 * sin, x1 * sin + x2 * cos], axis=-1).astype(x.dtype)


def expand_kv(c_kv, k_r, w_ukv):
    kv = jnp.einsum('btc,chd->bthd', c_kv, w_ukv.reshape(KV_LORA, N_HEADS, QK_NOPE + V_DIM))
    k_nope, v = kv[..., :QK_NOPE], kv[..., QK_NOPE:]
    k_rope = jnp.broadcast_to(k_r[:, :, None, :], k_nope.shape[:3] + (QK_ROPE,))
    return jnp.concatenate([k_nope, k_rope], axis=-1), v


def attend_chunk_causal(q, k, v):
    b, s = q.shape[:2]
    n_blk = s // Q_BLOCK
    scale = QK_DIM ** -0.5
    q_blocks = q.reshape(b, n_blk, Q_BLOCK, N_HEADS, QK_DIM).swapaxes(0, 1)
    k_chunk = jnp.arange(s) // CHUNK

    def one_block(args):
        q_blk, blk = args
        q_chunk = (blk * Q_BLOCK + jnp.arange(Q_BLOCK)) // CHUNK
        mask = k_chunk[None, :] <= q_chunk[:, None]
        sc = jnp.einsum('bqhd,bkhd->bhqk', q_blk, k).astype(jnp.float32) * scale
        pr = jax.nn.softmax(jnp.where(mask, sc, NEG), axis=-1).astype(v.dtype)
        return jnp.einsum('bhqk,bkhd->bqhd', pr, v)

    out = lax.map(one_block, (q_blocks, jnp.arange(n_blk)))
    return out.swapaxes(0, 1).reshape(b, s, ATTN_W)


def attend_open(q, k, v):
    b, s = q.shape[:2]
    sc = jnp.einsum('bqhd,bkhd->bhqk', q, k).astype(jnp.float32) * (QK_DIM ** -0.5)
    pr = jax.nn.softmax(sc, axis=-1).astype(v.dtype)
    return jnp.einsum('bhqk,bkhd->bqhd', pr, v).reshape(b, s, ATTN_W)


def causal_depthwise_conv(u, past, w, bias):
    padded = jnp.concatenate([past, u], axis=1)
    y = lax.conv_general_dilated(padded, w[:, None, :], window_strides=(1,), padding='VALID',
                                 dimension_numbers=('NWC', 'WIO', 'NWC'),
                                 feature_group_count=CONV_CH)
    return y + bias, padded[:, -(CONV_K - 1):]


def layer(x, p, pos, past, lp):
    b, s, _ = x.shape
    h = rms_norm(x, lp['norm_mix_g'])
    z = h @ lp['w_in']
    c_q, c_kv, k_r, g_in = jnp.split(z, [Q_LORA, Q_LORA + KV_LORA, Q_LORA + KV_LORA + QK_ROPE], axis=-1)
    q = (rms_norm(c_q, lp['q_norm_g']) @ lp['w_uq']).reshape(b, s, N_HEADS, QK_DIM)
    q = jnp.concatenate([q[..., :QK_NOPE], rope(q[..., QK_NOPE:], pos)], axis=-1)
    c_kv = rms_norm(c_kv, lp['kv_norm_g'])
    k_r = rope(k_r, pos)
    u = g_in[..., :CONV_CH] * jax.nn.sigmoid(g_in[..., CONV_CH:])
    if past is None:
        ckv_all, kr_all = c_kv, k_r
        conv_past = jnp.zeros((b, CONV_K - 1, CONV_CH), u.dtype)
    else:
        ckv_all = jnp.concatenate([past[0], c_kv], axis=1)
        kr_all = jnp.concatenate([past[1], k_r], axis=1)
        conv_past = past[2]
    k, v = expand_kv(ckv_all, kr_all, lp['w_ukv'])
    attn = attend_chunk_causal(q, k, v) if past is None else attend_open(q, k, v)
    conv, conv_new = causal_depthwise_conv(u, conv_past, lp['conv_w'], lp['conv_b'])
    conv = jax.nn.silu(layer_norm(conv, lp['conv_ln_g'], lp['conv_ln_b']))
    mixed = jnp.concatenate([rms_norm(attn, lp['attn_out_g']), rms_norm(conv, lp['conv_out_g'])], axis=-1)
    x = x + mixed @ lp['w_out']
    f = rms_norm(x, lp['norm_ffn_g']) @ lp['w_ff_up']
    x = x + jnp.square(jax.nn.relu(f)) @ lp['w_ff_down']
    x = x + jax.nn.sigmoid(rms_norm(x, lp['norm_ple_g']) @ lp['w_ple_gate']) * (p @ lp['w_ple_proj'])
    return x, c_kv, k_r, conv_new


def setup_inputs(seed: int = 0) -> dict:
    key = jax.random.key(seed)
    ks = iter(jax.random.split(key, 40))
    f32 = jnp.float32

    def nrm(shape, scale):
        return jax.random.normal(next(ks), shape, f32) * scale

    def gain(dim):
        return 1.0 + nrm((DEPTH, dim), 0.02)

    return {
        'x_prompt': nrm((BATCH, SEQ, D_MODEL), 1.0),
        'x_sample': nrm((DEC_BATCH, DEC_SEQ, D_MODEL), 1.0),
        'cache_ckv': nrm((DEPTH, DEC_BATCH, PAST_LEN, KV_LORA), 1.0),
        'cache_krope': nrm((DEPTH, DEC_BATCH, PAST_LEN, QK_ROPE), 1.0),
        'state_conv': nrm((DEPTH, DEC_BATCH, CONV_K - 1, CONV_CH), 0.5),
        'p_prompt': nrm((DEPTH, BATCH, SEQ, PLE_DIM), 1.0),
        'p_sample': nrm((DEPTH, DEC_BATCH, DEC_SEQ, PLE_DIM), 1.0),
        'norm_mix_g': gain(D_MODEL),
        'w_in': nrm((DEPTH, D_MODEL, IN_W), D_MODEL ** -0.5),
        'q_norm_g': gain(Q_LORA),
        'w_uq': nrm((DEPTH, Q_LORA, N_HEADS * QK_DIM), Q_LORA ** -0.5),
        'kv_norm_g': gain(KV_LORA),
        'w_ukv': nrm((DEPTH, KV_LORA, N_HEADS * (QK_NOPE + V_DIM)), KV_LORA ** -0.5),
        'conv_w': nrm((DEPTH, CONV_K, CONV_CH), CONV_K ** -0.5),
        'conv_b': nrm((DEPTH, CONV_CH), 0.02),
        'conv_ln_g': gain(CONV_CH),
        'conv_ln_b': nrm((DEPTH, CONV_CH), 0.02),
        'attn_out_g': gain(ATTN_W),
        'conv_out_g': gain(CONV_CH),
        'w_out': nrm((DEPTH, MIX_W, D_MODEL), MIX_W ** -0.5),
        'norm_ffn_g': gain(D_MODEL),
        'w_ff_up': nrm((DEPTH, D_MODEL, D_FF), D_MODEL ** -0.5),
        'w_ff_down': nrm((DEPTH, D_FF, D_MODEL), D_FF ** -0.5),
        'norm_ple_g': gain(D_MODEL),
        'w_ple_gate': nrm((DEPTH, D_MODEL, D_MODEL), D_MODEL ** -0.5),
        'w_ple_proj': nrm((DEPTH, PLE_DIM, D_MODEL), PLE_DIM ** -0.5),
        'norm_final_g': 1.0 + nrm((D_MODEL,), 0.02),
    }


def reference(x_prompt, x_sample, cache_ckv, cache_krope, state_conv, p_prompt, p_sample,
              norm_mix_g, w_in, q_norm_g, w_uq, kv_norm_g, w_ukv, conv_w, conv_b, conv_ln_g, conv_ln_b,
              attn_out_g, conv_out_g, w_out, norm_ffn_g, w_ff_up, w_ff_down, norm_ple_g, w_ple_gate,
              w_ple_proj, norm_final_g):
    s_prompt = x_prompt.shape[1]
    s_sample = x_sample.shape[1]
    pos_prompt = jnp.arange(s_prompt)
    pos_sample = PAST_LEN + jnp.arange(s_sample)
    hp, hs = x_prompt, x_sample
    ckv_p, kr_p, conv_p, ckv_s, kr_s, conv_s = [], [], [], [], [], []
    for i in range(DEPTH):
        lp = {
            'norm_mix_g': norm_mix_g[i], 'w_in': w_in[i], 'q_norm_g': q_norm_g[i], 'w_uq': w_uq[i],
            'kv_norm_g': kv_norm_g[i], 'w_ukv': w_ukv[i], 'conv_w': conv_w[i], 'conv_b': conv_b[i],
            'conv_ln_g': conv_ln_g[i], 'conv_ln_b': conv_ln_b[i], 'attn_out_g': attn_out_g[i],
            'conv_out_g': conv_out_g[i], 'w_out': w_out[i], 'norm_ffn_g': norm_ffn_g[i],
            'w_ff_up': w_ff_up[i], 'w_ff_down': w_ff_down[i], 'norm_ple_g': norm_ple_g[i],
            'w_ple_gate': w_ple_gate[i], 'w_ple_proj': w_ple_proj[i],
        }
        hp, a, b, c = layer(hp, p_prompt[i], pos_prompt, None, lp)
        ckv_p.append(a); kr_p.append(b); conv_p.append(c)
        hs, a, b, c = layer(hs, p_sample[i], pos_sample, (cache_ckv[i], cache_krope[i], state_conv[i]), lp)
        ckv_s.append(a); kr_s.append(b); conv_s.append(c)
    y_prompt = rms_norm(hp, norm_final_g)
    y_sample = rms_norm(hs, norm_final_g)
    new_ckv_prompt = jnp.stack(ckv_p)
    new_krope_prompt = jnp.stack(kr_p)
    new_conv_prompt = jnp.stack(conv_p)
    new_ckv_sample = jnp.stack(ckv_s)
    new_krope_sample = jnp.stack(kr_s)
    new_conv_sample = jnp.stack(conv_s)
    return (y_prompt, y_sample, new_ckv_prompt, new_krope_prompt, new_conv_prompt,
            new_ckv_sample, new_krope_sample, new_conv_sample)
```

```python
import os
import numpy as np
from contextlib import ExitStack
import concourse.bass as bass
import concourse.mybir as mybir
from concourse.bass_utils import run_bass_kernel_spmd

F32 = mybir.dt.float32
BF16 = mybir.dt.bfloat16
ALU = mybir.AluOpType
AF = mybir.ActivationFunctionType

D = 1024
NH = 8
QLORA = 384
KVL = 256
ROPE = 32
CCH = 512
CK = 31
DFF = 4096
PLE = 256
PAST = 1024
DEC = 64
SEQ = 8192
NCORES = 8
WIN_COLS = 704 + 1024
WUQ_COLS = NH * 192
SC_ATT = 96 ** -0.5
STAGE = int(os.environ.get('KSTAGE', '99'))
WSAMPLE = int(os.environ.get('KSAMPLE', '1'))
KSUB = int(os.environ.get('KSUB', '99'))


class Buf:
    def __init__(self, name, ap):
        self.name = name
        self.ap = ap
        self.last_write = None
        self.readers = {}
        self.dsem = None
        self.dcount = 0
        self.psum = False


class Eng:
    def __init__(self, name, sem, is_pe=False):
        self.name = name
        self.sem = sem
        self.count = 0
        self.waited = {}
        self.ops = []
        self.is_pe = is_pe


class Rot:
    def __init__(self, bufs):
        self.bufs = bufs
        self.i = 0

    def next(self):
        b = self.bufs[self.i % len(self.bufs)]
        self.i += 1
        return b


class FW:
    def __init__(self, nc, stack):
        self.nc = nc
        self.stack = stack
        self.engs = {}
        self.sems = {}
        for n in ["pe", "act", "dve", "pool", "sp"]:
            sem = stack.enter_context(nc.semaphore("s_" + n))
            self.engs[n] = Eng(n, sem, is_pe=(n == "pe"))
            self.sems[id(sem)] = sem
        self.dbufs = []
        self.sbuf_bytes = 0

    def tile(self, name, shape, dt):
        t = self.stack.enter_context(self.nc.sbuf_tensor(name, shape, dt))
        n = 1
        for s in shape[1:]:
            n *= s
        self.sbuf_bytes += n * (4 if dt == F32 else 2)
        return t

    def sbuf(self, name, shape, dt):
        return Buf(name, self.tile(name, shape, dt))

    def _deps(self, eng, reads, writes):
        deps = {}

        def add(k, v):
            if deps.get(k, 0) < v:
                deps[k] = v
        for b in reads:
            if b.last_write is not None:
                add(*b.last_write)
            if b.psum:
                for k, v in b.readers.items():
                    if k != id(eng.sem):
                        add(k, v)
        for b in writes:
            if b.last_write is not None:
                add(*b.last_write)
            for k, v in b.readers.items():
                add(k, v)
        out = []
        for k, v in deps.items():
            if eng.is_pe and k == id(eng.sem):
                continue
            if eng.waited.get(k, 0) >= v:
                continue
            eng.waited[k] = v
            out.append((self.sems[k], v))
        return out

    def _record(self, ev, reads, writes):
        for b in reads:
            if b.readers.get(ev[0], 0) < ev[1]:
                b.readers[ev[0]] = ev[1]
        for b in writes:
            b.last_write = ev
            b.readers = {}

    def I(self, engname, method, reads, writes, *args, **kw):
        eng = self.engs[engname]
        waits = self._deps(eng, reads, writes)
        eng.count += 1
        ev = (id(eng.sem), eng.count)
        sem = eng.sem

        def emit(e):
            for s, v in waits:
                e.wait_ge(s, v)
            getattr(e, method)(*args, **kw).then_inc(sem, 1)
        eng.ops.append(emit)
        self._record(ev, reads, writes)

    def dma(self, qname, out_ap, in_ap, reads, writes, owner, **kw):
        eng = self.engs[qname]
        if owner.dsem is None:
            owner.dsem = {}
        if qname not in owner.dsem:
            sem_ = self.stack.enter_context(self.nc.semaphore("d%s_%s" % (qname, owner.name)))
            owner.dsem[qname] = [sem_, 0]
            self.sems[id(sem_)] = sem_
            self.dbufs.append(owner.dsem[qname])
        waits = self._deps(eng, reads, writes)
        owner.dsem[qname][1] += 16
        sem = owner.dsem[qname][0]
        ev = (id(sem), owner.dsem[qname][1])

        def emit(e):
            for s, v in waits:
                e.wait_ge(s, v)
            e.dma_start(out=out_ap, in_=in_ap, **kw).then_inc(sem, 16)
        eng.ops.append(emit)
        self._record(ev, reads, writes)

    def _all_events(self):
        final = {}
        for e in self.engs.values():
            if e.count:
                final[id(e.sem)] = e.count
        for sem_, cnt_ in self.dbufs:
            final[id(sem_)] = cnt_
        return final

    def barrier(self, engines=("pe", "act", "dve", "pool", "sp")):
        final = self._all_events()
        for n in engines:
            eng = self.engs[n]
            ws = []
            for k, v in final.items():
                if eng.waited.get(k, 0) >= v:
                    continue
                if k == id(eng.sem) and eng.is_pe:
                    continue
                eng.waited[k] = v
                ws.append((self.sems[k], v))

            def emit(e, ws=ws):
                for s, v in ws:
                    e.wait_ge(s, v)
            eng.ops.append(emit)

    def finish(self):
        self.barrier(engines=("sp",))
        nc = self.nc
        engs = self.engs
        with nc.Block() as block:
            @block.tensor
            def _(e):
                for f in engs["pe"].ops:
                    f(e)

            @block.scalar
            def _(e):
                for f in engs["act"].ops:
                    f(e)

            @block.vector
            def _(e):
                for f in engs["dve"].ops:
                    f(e)

            @block.gpsimd
            def _(e):
                for f in engs["pool"].ops:
                    f(e)

            @block.sync
            def _(e):
                for f in engs["sp"].ops:
                    f(e)


def build_nc(S=SEQ, with_sample=True, dbg=False):
    nc = bass.Bass("TRN2", target_bir_lowering=False)
    NT = S // 512

    def din(name, shape, dt=F32):
        return nc.dram_tensor(name, list(shape), dt, kind="ExternalInput").ap()

    def dout(name, shape, dt=F32):
        return nc.dram_tensor(name, list(shape), dt, kind="ExternalOutput").ap()

    def dscr(name, shape, dt=BF16):
        return nc.dram_tensor(name, list(shape), dt, kind="Internal").ap()

    x_p = din("x_p", [S, D])
    p_p = din("p_p", [S, PLE])
    x_s = din("x_s", [DEC, D])
    p_s = din("p_s", [DEC, PLE])
    c_ckv = din("c_ckv", [PAST, KVL])
    c_kr = din("c_kr", [PAST, ROPE])
    s_conv = din("s_conv", [CK - 1, CCH])
    w_in = din("w_in", [D, WIN_COLS])
    w_uq = din("w_uq", [QLORA, WUQ_COLS])
    w_uk = din("w_uk", [KVL, 512])
    w_uv = din("w_uv", [KVL, 512])
    w_out = din("w_out", [D, D])
    w_up = din("w_up", [D, DFF])
    w_down = din("w_down", [DFF, D])
    w_gate = din("w_gate", [D, D])
    w_pp = din("w_pp", [PLE, D])
    pp_in = din("pp", [128, 171])
    bc_in = din("bc", [128, 256 + 1024])
    cs_p = din("cs_p", [S, 64])
    cs_s = din("cs_s", [DEC, 64])
    cst_p = din("cst_p", [96, 2, S])
    cst_s = din("cst_s", [96, 2, DEC])

    y_p = dout("y_p", [S, D])
    y_s = dout("y_s", [DEC, D])
    ckv_p = dout("ckv_p", [S, KVL])
    kr_p = dout("kr_p", [S, ROPE])
    conv_p = dout("conv_p", [CK - 1, CCH])
    ckv_s = dout("ckv_s", [DEC, KVL])
    kr_s = dout("kr_s", [DEC, ROPE])
    conv_s = dout("conv_s", [CK - 1, CCH])

    wos = dscr("wos", [D, D])
    wus = dscr("wus", [D, DFF])
    wds = dscr("wds", [DFF, D])
    wgs = dscr("wgs", [D, D])
    wcs = dscr("wcs", [D, 1024])
    kts_p = dscr("kts_p", [NH, 96, S])
    vs_p = dscr("vs_p", [NH, 128, S // 128, 65])
    SK = PAST + 128
    kts_s = dscr("kts_s", [NH, 96, SK])
    vs_s = dscr("vs_s", [NH, 128, SK // 128, 65])

    st = ExitStack()
    with st:
        fw = FW(nc, st)
        I = fw.I

        XT = fw.tile("X", [128, 4, D], F32)
        X = [Buf("X%d" % s, XT[:, s, :]) for s in range(4)]
        HT = fw.sbuf("HT", [128, 8, 512], BF16)
        XS = Rot([fw.sbuf("XS%d" % i, [128, D], BF16) for i in range(2)])
        XF = Rot([fw.sbuf("XF%d" % i, [128, D], F32) for i in range(2)])
        HTPT = fw.tile("HTP", [128, 8, 512], BF16)
        HTP = [Buf("HTP%d" % s_, HTPT[:, :, s_ * 128:(s_ + 1) * 128]) for s_ in range(4)]
        KRT = fw.sbuf("KRT", [96, 512], BF16)
        STT = fw.tile("STT", [128, 64], F32)
        STATS = Rot([Buf("st%d" % i, STT[:, 2 * i:2 * i + 2]) for i in range(16)])
        SGRP = Rot([([Buf("sg%d_%d" % (g, j), STT[:, 32 + 8 * g + j:32 + 8 * g + j + 1]) for j in range(4)],
                     Buf("sr%d" % g, STT[:, 32 + 8 * g + 4:32 + 8 * g + 8])) for g in range(4)])
        SMT = fw.tile("SMT", [128, 4, 32], F32)
        SMALL = Rot([Buf("sm%d" % i, SMT[:, i, :]) for i in range(4)])
        CKVO = Rot([fw.sbuf("CKVO%d" % i, [128, KVL], F32) for i in range(2)])
        KRO = Rot([fw.sbuf("KRO%d" % i, [128, ROPE], F32) for i in range(2)])
        CKVB = Rot([fw.sbuf("CKVB%d" % i, [128, KVL], BF16) for i in range(2)])
        KRB = Rot([fw.sbuf("KRB%d" % i, [128, 96], BF16) for i in range(2)])
        CQB = Rot([fw.sbuf("CQB%d" % i, [128, QLORA], BF16) for i in range(2)])
        CKVT = fw.sbuf("CKVT", [128, 2, 512], BF16)
        CQT = fw.sbuf("CQT", [128, 3, 512], BF16)
        CS = fw.sbuf("CS", [128, 4, 64], F32)
        CST = fw.sbuf("CST", [96, 2, 512], F32)
        KTNT = fw.tile("KTN", [128, 8, 512], BF16)
        AT = [Buf("AT%d" % i, KTNT[:, 4 * i:4 * i + 4, :]) for i in range(2)]
        ATR = Rot(AT)
        VN = fw.sbuf("VN", [128, 8, 4, 65], BF16)
        QTT = fw.tile("QT", [96, 8, 512], BF16)
        QT = [Buf("QT%d" % h, QTT[:, h, :]) for h in range(NH)]
        UC = Rot([fw.sbuf("UC%d" % i, [128, 512 + CK - 1], F32) for i in range(2)])
        UT = fw.sbuf("UT", [128, 4, CK - 1], F32)
        ACCT = fw.tile("ACC", [128, 4, 512], F32)
        ACC = [Buf("ACC%d" % k, ACCT[:, k, :]) for k in range(4)]
        YNT = fw.tile("YN", [128, 4, 512], BF16)
        YN = [Buf("YN%d" % k, YNT[:, k, :]) for k in range(4)]
        TMP = Rot([fw.sbuf("TMP%d" % i, [128, 512], F32) for i in range(4)])
        STAT = Rot([fw.sbuf("STAT%d" % i, [128, 512], F32) for i in range(2)])
        KH = Rot([fw.sbuf("KH%d" % i, [96, 1024], BF16) for i in range(3)])
        VH = Rot([fw.sbuf("VH%d" % i, [128, 8, 65], BF16) for i in range(3)])
        PTS = Rot([fw.sbuf("PTS%d" % i, [128, 1024], BF16) for i in range(3)])
        RL = fw.sbuf("RL", [65, 512], F32)
        RLH = fw.sbuf("RLH", [65, 512], BF16)
        RLL = fw.sbuf("RLL", [65, 512], BF16)
        ATTT = fw.tile("ATT", [128, 4, 512], BF16)
        ATT = [Buf("ATT%d" % j, ATTT[:, j, :]) for j in range(4)]
        SQB = Rot([fw.sbuf("SQB%d" % i, [128, 512], BF16) for i in range(2)])
        WPC = Rot([fw.sbuf("WPC%d" % i, [128, 4096], BF16) for i in range(4)])
        PB = fw.sbuf("PB", [128, 4, PLE], BF16)
        PTR = fw.sbuf("PTR", [128, 2, 512], BF16)
        WIN = fw.sbuf("WIN", [128, 8, 704], BF16)
        WUQ = fw.sbuf("WUQ", [128, 3, WUQ_COLS], BF16)
        WUK = fw.sbuf("WUK", [128, 2, 512], BF16)
        WUV = fw.sbuf("WUV", [128, 2, 512], BF16)
        WP = fw.sbuf("WP", [128, 2, D], BF16)
        PP = fw.sbuf("PP", [128, 171], F32)
        BC = fw.sbuf("BC", [128, 256 + 1024], F32)
        IDB = fw.sbuf("IDB", [128, 128], BF16)
        IDF = fw.sbuf("IDF", [128, 128], F32)
        ONB = fw.sbuf("ONB", [128, 128], BF16)
        ONF = fw.sbuf("ONF", [128, 128], F32)
        psG = st.enter_context(nc.psum_tensor("psG", [128, 2048], F32))
        psb = [psG[:, i * 512:(i + 1) * 512] for i in range(4)] + \
              [st.enter_context(nc.psum_tensor("ps%d" % i, [128, 512], F32)) for i in range(4, 8)]
        G = Rot([Buf("G%d" % i, psb[i]) for i in range(4)])
        OB = Rot([Buf("O%d" % i, psb[4 + i]) for i in range(2)])
        TB = Rot([Buf("T%d" % i, psb[6 + i]) for i in range(2)])
        for b_ in G.bufs + OB.bufs + TB.bufs:
            b_.psum = True
        GH = Rot(TB.bufs)

        def tbv(b):
            return b.ap[:, :].bitcast(BF16).rearrange("p (c t) -> p c t", t=128)

        GMIX, GQ, GFFN, GPLE, GOUT, CB, LNG, LNB, CW = 0, 8, 11, 19, 27, 35, 39, 43, 47

        def ppc(c):
            return PP.ap[:, c:c + 1]

        fw.dma("sp", PP.ap[:, :], pp_in, [], [PP], PP)
        fw.dma("sp", BC.ap[:, :], bc_in, [], [BC], BC)
        I("pool", "memset", [], [IDF], IDF.ap[:, :], 0.0)
        I("pool", "affine_select", [IDF], [IDF], out=IDF.ap[:, :], in_=IDF.ap[:, :], pattern=[[-1, 128]],
          compare_op=ALU.not_equal, fill=1.0, base=0, channel_multiplier=1)
        I("dve", "tensor_copy", [IDF], [IDB], out=IDB.ap[:, :], in_=IDF.ap[:, :])
        I("pool", "memset", [], [ONB], ONB.ap[:, :], 1.0)
        I("pool", "memset", [], [ONF], ONF.ap[:, :], 1.0)
        I("pool", "memset", [], [VN], VN.ap[:, :, :, :], 1.0)
        for kb in KRB.bufs:
            I("pool", "memset", [], [kb], kb.ap[:, :], 0.0)
        fw.dma("pool", WUK.ap[:, :, :], w_uk.rearrange("(c p) n -> p c n", p=128), [], [WUK], WUK)
        fw.dma("pool", WUV.ap[:, :, :], w_uv.rearrange("(c p) n -> p c n", p=128), [], [WUV], WUV)
        fw.dma("pool", WP.ap[:, :, :], w_pp.rearrange("(c p) n -> p c n", p=128), [], [WP], WP)
        I("pool", "tensor_scalar", [WP], [WP], out=WP.ap[:, :, :], in0=WP.ap[:, :, :], scalar1=0.5, scalar2=None, op0=ALU.mult)
        XFLAT = XT[:, :, :].rearrange("p s d -> p (s d)")
        STG = Rot([Buf("STG%d" % i, XFLAT[:, 2048 * i:2048 * (i + 1)]) for i in range(2)])

        def fold_resident(src, ncols, nchunk, gcol, dstbuf):
            for c in range(nchunk):
                for c0 in range(0, ncols, 2048):
                    w = min(2048, ncols - c0)
                    sg = STG.next()
                    fw.dma("sp", sg.ap[:, 0:w], src[c * 128:(c + 1) * 128, c0:c0 + w], [], [sg], sg)
                    I("dve", "tensor_scalar", [sg, PP], [dstbuf], out=dstbuf.ap[:, c, c0:c0 + w], in0=sg.ap[:, 0:w],
                      scalar1=ppc(gcol + c), scalar2=None, op0=ALU.mult)

        PP2 = fw.sbuf("PP2", [128, 8], F32)
        I("pool", "tensor_scalar", [PP], [PP2], out=PP2.ap[:, 0:8], in0=PP.ap[:, LNG:LNG + 8], scalar1=0.5, scalar2=None, op0=ALU.mult)
        fold_resident(w_in[:, 0:704], 704, 8, GMIX, WIN)
        fold_resident(w_uq, WUQ_COLS, 3, GQ, WUQ)

        def fold_scratch(src, dst, ncols, nchunk, gcol):
            for c in range(nchunk):
                for c0 in range(0, ncols, 2048):
                    w = min(2048, ncols - c0)
                    sg = STG.next()
                    sb = WPC.next()
                    fw.dma("sp", sg.ap[:, 0:w], src[c * 128:(c + 1) * 128, c0:c0 + w], [], [sg], sg)
                    I("dve", "tensor_scalar", [sg, PP], [sb], out=sb.ap[:, 0:w], in0=sg.ap[:, 0:w],
                      scalar1=ppc(gcol + c), scalar2=None, op0=ALU.mult)
                    fw.dma("sp", dst[c * 128:(c + 1) * 128, c0:c0 + w], sb.ap[:, 0:w], [sb], [], sb)

        wds_v = wds.rearrange("(j p) n -> p j n", p=128)

        def fold_all_scratch():
            for c in range(8):
                sg_ = STG.next()
                sb_ = WPC.next()
                fw.dma("sp", sg_.ap[:, 0:1024], w_in[c * 128:(c + 1) * 128, 704:WIN_COLS], [], [sg_], sg_)
                I("dve", "tensor_scalar", [sg_, PP], [sb_], out=sb_.ap[:, 0:1024], in0=sg_.ap[:, 0:1024],
                  scalar1=ppc(GMIX + c), scalar2=None, op0=ALU.mult)
                hv = sb_.ap[:, 0:1024].rearrange("p (k n) -> p k n", n=256)[:, :, 0:128]
                I("dve", "tensor_scalar", [sb_], [sb_], out=hv, in0=hv, scalar1=0.5, scalar2=None, op0=ALU.mult)
                fw.dma("sp", wcs[c * 128:(c + 1) * 128, :], sb_.ap[:, 0:1024], [sb_], [], sb_)
            fold_scratch(w_out, wos, D, 8, GOUT)
            fold_scratch(w_up, wus, DFF, 8, GFFN)
            fold_scratch(w_gate, wgs, D, 8, GPLE)
            wd_v = w_down.rearrange("(j p) n -> p j n", p=128)
            wds_v = wds.rearrange("(j p) n -> p j n", p=128)
            for j0 in range(0, 32, 4):
                sb = WPC.next()
                fw.dma("pool", sb.ap[:, :].rearrange("p (j n) -> p j n", n=D), wd_v[:, j0:j0 + 4, :], [], [sb], sb)
                fw.dma("sp", wds_v[:, j0:j0 + 4, :], sb.ap[:, :].rearrange("p (j n) -> p j n", n=D), [sb], [], sb)
            fw.barrier()


        cnt = {"ev": 0}

        def evac_eng():
            cnt["ev"] += 1
            return "act" if cnt["ev"] % 2 else "dve"

        def copy_on(eng, reads, writes, out, in_):
            if eng == "act":
                I("act", "copy", reads, writes, out=out, in_=in_)
            else:
                I(eng, "tensor_copy", reads, writes, out=out, in_=in_)

        def rstd(stb, TS, eps):
            I("act", "activation", [stb], [stb], out=stb.ap[:TS, 1:2], in_=stb.ap[:TS, 0:1], func=AF.Ln, bias=eps)
            I("act", "activation", [stb], [stb], out=stb.ap[:TS, 1:2], in_=stb.ap[:TS, 1:2], func=AF.Exp, scale=-0.5)

        def _mscols(ms, TS, k):
            g = int(ms[0].name[2:].split("_")[0])
            return STT[:TS, 32 + 8 * g:32 + 8 * g + k]

        def group_stats(srcs, TS, eps):
            ms, rs = SGRP.next()
            k = len(srcs)
            for j, (ap_, bufs_, n_, jap_, jbuf_) in enumerate(srcs):
                I("act", "activation", bufs_, [ms[j], jbuf_], out=jap_, in_=ap_, func=AF.Square,
                  scale=float(n_) ** -0.5, accum_out=ms[j].ap[:TS, 0:1])
            I("act", "activation", ms[:k], [rs], out=rs.ap[:TS, 0:k], in_=_mscols(ms, TS, k),
              func=AF.Ln, bias=eps)
            I("act", "activation", [rs], [rs], out=rs.ap[:TS, 0:k], in_=rs.ap[:TS, 0:k], func=AF.Exp, scale=-0.5)
            return rs

        def norm_a(srcs, subs, TS):
            xss = [XS.next() for _ in subs]
            rs = group_stats([(srcs[j].ap[:TS, :], [srcs[j]], D, xss[j].ap[:TS, :], xss[j]) for j in range(len(subs))], TS, 1e-6)
            for j, s in enumerate(subs):
                I("dve", "tensor_scalar", [srcs[j], rs], [xss[j]], out=xss[j].ap[:TS, :], in0=srcs[j].ap[:TS, :],
                  scalar1=rs.ap[:TS, j:j + 1], scalar2=None, op0=ALU.mult)
            return xss

        def norm_b(xss, subs, TS, dst_bufs, dst_tile):
            for j, s in enumerate(subs):
                xs = xss[j]
                tb = TB.next()
                tv = tbv(tb)
                for c in range(8):
                    I("pe", "transpose", [xs, IDB], [tb], out=tv[:, c, 0:TS], in_=xs.ap[:TS, c * 128:(c + 1) * 128],
                      identity=IDB.ap[:TS, :TS])
                copy_on(evac_eng(), [tb], [dst_bufs[j]], dst_tile[:, :, s * TS:(s + 1) * TS], tv[:, :, 0:TS])

        def norm_transpose(srcs, subs, TS, dst):
            for i0 in range(0, len(subs), 2):
                sub2 = subs[i0:i0 + 2]
                xss = norm_a(srcs[i0:i0 + 2], sub2, TS)
                norm_b(xss, sub2, TS, [dst] * len(sub2), dst.ap)

        def bcast_rstd(ps, T, n, eps):
            rs = STAT.next()
            I("act", "activation", [ps], [rs], out=rs.ap[:, 0:T], in_=ps.ap[:, 0:T], func=AF.Ln, bias=eps, scale=1.0 / n)
            I("act", "activation", [rs], [rs], out=rs.ap[:, 0:T], in_=rs.ap[:, 0:T], func=AF.Exp, scale=-0.5)
            return rs

        class Seq:
            pass

        def transposes_small(ckb, krb, cqb, s, TS):
            tb = TB.next()
            tv = tbv(tb)
            for c in range(2):
                I("pe", "transpose", [ckb, IDB], [tb], out=tv[:, c, 0:TS], in_=ckb.ap[:TS, c * 128:(c + 1) * 128],
                  identity=IDB.ap[:TS, :TS])
            if cqb is not None:
                for c in range(3):
                    I("pe", "transpose", [cqb, IDB], [tb], out=tv[:, 2 + c, 0:TS], in_=cqb.ap[:TS, c * 128:(c + 1) * 128],
                      identity=IDB.ap[:TS, :TS])
            if KSUB >= 9:
                I("pe", "transpose", [krb, IDB], [tb], out=tv[0:96, 5, 0:TS], in_=krb.ap[:TS, 0:96], identity=IDB.ap[:TS, :TS])
            if KSUB >= 8 or KSUB == 5:
                I("act", "copy", [tb], [CKVT], out=CKVT.ap[:, :, s * TS:(s + 1) * TS], in_=tv[:, 0:2, 0:TS])
            if cqb is not None and (KSUB >= 8 or KSUB == 6):
                I("dve", "tensor_copy", [tb], [CQT], out=CQT.ap[:, :, s * TS:(s + 1) * TS], in_=tv[:, 2:5, 0:TS])
            I("dve", "tensor_copy", [tb], [KRT], out=KRT.ap[64:96, s * TS:(s + 1) * TS], in_=tv[64:96, 5, 0:TS])

        def kv_expand(sq, T, nsub, TS, key0):
            for j in range(4):
                ps = G.next()
                for c in range(2):
                    I("pe", "matmul", [WUK, CKVT], [ps], ps.ap[:, 0:T], lhsT=WUK.ap[:, c, j * 128:(j + 1) * 128],
                      rhs=CKVT.ap[:, c, 0:T], start=(c == 0), stop=(c == 1))
                I("act", "copy", [ps], AT, out=KTNT[0:64, 2 * j, 0:T], in_=ps.ap[0:64, 0:T])
                I("dve", "tensor_copy", [ps], AT, out=KTNT[0:64, 2 * j + 1, 0:T], in_=ps.ap[64:128, 0:T])
            I("pool", "tensor_copy", [KRT], AT, out=KTNT[64:96, :, 0:T],
              in_=KRT.ap[64:96, 0:T].unsqueeze(1).to_broadcast([32, NH, T]))
            for s in range(nsub):
                ps = G.next()
                for c in range(2):
                    I("pe", "matmul", [WUV, CKVT], [ps], ps.ap[:TS, :], lhsT=CKVT.ap[:, c, s * TS:(s + 1) * TS],
                      rhs=WUV.ap[:, c, :], start=(c == 0), stop=(c == 1))
                copy_on(evac_eng(), [ps], [VN], VN.ap[:TS, :, s, 0:64], ps.ap[:TS, :].rearrange("p (h d) -> p h d", d=64))
            ti = key0 // 512
            fw.dma("sp", sq.kts[:, :, key0:key0 + T].rearrange("h r c -> r h c"), KTNT[0:96, :, 0:T], AT, [sq.KS[ti]], AT[0])
            kt0 = key0 // 128
            fw.dma("sp", sq.vs[:, 0:TS, kt0:kt0 + nsub, :].rearrange("h p k c -> p h k c"), VN.ap[:TS, :, 0:nsub, :],
                   [VN], [sq.VSB[ti]], VN)

        def wview(src, n0, n1):
            return src.rearrange("(c p) n -> p c n", p=128)[:, :, n0:n1]

        def conv_w_load(k):
            wc = WPC.next()
            wcv = wc.ap[:, 0:2048].rearrange("p (c n) -> p c n", n=256)
            fw.dma("sp", wcv, wview(wcs, k * 256, (k + 1) * 256), [], [wc], wc)
            return (wc, wcv)

        def conv_chunk(k, T, pre_w):
            wc, wcv = pre_w
            psa = GH.next()
            psg = GH.next()
            for c in range(8):
                I("pe", "matmul", [wc] + HTP, [psa], psa.ap[:, 0:T], lhsT=wcv[:, c, 0:128],
                  rhs=HTPT[:, c, 0:T], start=(c == 0), stop=(c == 7))
            for c in range(8):
                I("pe", "matmul", [wc] + HTP, [psg], psg.ap[:, 0:T], lhsT=wcv[:, c, 128:256],
                  rhs=HTPT[:, c, 0:T], start=(c == 0), stop=(c == 7))
            sig = TMP.next()
            I("act", "activation", [psg], [sig], out=sig.ap[:, 0:T], in_=psg.ap[:, 0:T], func=AF.Tanh, scale=0.5)
            uc = UC.next()
            I("pool", "tensor_copy", [UT], [uc], out=uc.ap[:, 0:CK - 1], in_=UT.ap[:, k, :])
            I("dve", "scalar_tensor_tensor", [psa, sig, uc], [uc], out=uc.ap[:, CK - 1:CK - 1 + T], in0=sig.ap[:, 0:T],
              scalar=1.0, in1=psa.ap[:, 0:T], op0=ALU.add, op1=ALU.mult)
            I("pool", "tensor_copy", [uc, UT], [UT], out=UT.ap[:, k, :], in_=uc.ap[:, T:T + CK - 1])
            a = ACC[k]
            I("dve", "tensor_scalar", [uc, PP], [a], out=a.ap[:, 0:T], in0=uc.ap[:, 0:T], scalar1=ppc(CW + k * CK),
              scalar2=ppc(CB + k), op0=ALU.mult, op1=ALU.add)

            def taps(j0, j1):
                def f():
                    for j in range(j0, j1):
                        I("dve", "scalar_tensor_tensor", [uc, PP, a], [a], out=a.ap[:, 0:T], in0=uc.ap[:, j:j + T],
                          scalar=ppc(CW + k * CK + j), in1=a.ap[:, 0:T], op0=ALU.mult, op1=ALU.add)
                return f
            return [taps(1, 6), taps(6, 11), taps(11, 16), taps(16, 21), taps(21, 26), taps(26, CK)]

        def conv_post(T):
            s1 = G.next()
            s2 = G.next()
            for k in range(4):
                sq = SQB.next()
                I("act", "activation", [ACC[k]], [sq], out=sq.ap[:, 0:T], in_=ACC[k].ap[:, 0:T], func=AF.Square)
                I("pe", "matmul", [ONF, ACC[k]], [s1], s1.ap[:, 0:T], lhsT=ONF.ap[:, :], rhs=ACC[k].ap[:, 0:T],
                  start=(k == 0), stop=(k == 3))
                I("pe", "matmul", [ONB, sq], [s2], s2.ap[:, 0:T], lhsT=ONB.ap[:, :], rhs=sq.ap[:, 0:T],
                  start=(k == 0), stop=(k == 3))
            mu = STAT.next()
            I("act", "activation", [s1], [mu], out=mu.ap[:, 0:T], in_=s1.ap[:, 0:T], func=AF.Copy, scale=1.0 / CCH)
            musq = TMP.next()
            I("pool", "tensor_tensor", [mu], [musq], out=musq.ap[:, 0:T], in0=mu.ap[:, 0:T], in1=mu.ap[:, 0:T], op=ALU.mult)
            var = TMP.next()
            I("dve", "scalar_tensor_tensor", [s2, musq], [var], out=var.ap[:, 0:T], in0=s2.ap[:, 0:T], scalar=1.0 / CCH,
              in1=musq.ap[:, 0:T], op0=ALU.mult, op1=ALU.subtract)
            rs = STAT.next()
            I("act", "activation", [var], [rs], out=rs.ap[:, 0:T], in_=var.ap[:, 0:T], func=AF.Ln, bias=1e-5)
            I("act", "activation", [rs], [rs], out=rs.ap[:, 0:T], in_=rs.ap[:, 0:T], func=AF.Exp, scale=-0.5)
            for k in range(4):
                a = ACC[k]
                I("dve", "tensor_tensor", [a, mu], [a], out=a.ap[:, 0:T], in0=a.ap[:, 0:T], in1=mu.ap[:, 0:T], op=ALU.subtract)
                I("pool", "tensor_tensor", [a, rs], [a], out=a.ap[:, 0:T], in0=a.ap[:, 0:T], in1=rs.ap[:, 0:T], op=ALU.mult)
                th = TMP.next()
                I("act", "activation", [a, PP2], [th], out=th.ap[:, 0:T], in_=a.ap[:, 0:T], func=AF.Tanh,
                  scale=PP2.ap[:, k:k + 1], bias=PP2.ap[:, 4 + k:5 + k])
                I("dve", "tensor_scalar", [a, PP2], [a], out=a.ap[:, 0:T], in0=a.ap[:, 0:T], scalar1=PP2.ap[:, k:k + 1],
                  scalar2=PP2.ap[:, 4 + k:5 + k], op0=ALU.mult, op1=ALU.add)
                I("dve", "scalar_tensor_tensor", [th, a], [a], out=a.ap[:, 0:T], in0=th.ap[:, 0:T], scalar=1.0,
                  in1=a.ap[:, 0:T], op0=ALU.add, op1=ALU.mult)
            s3 = G.next()
            for k in range(4):
                sq = SQB.next()
                I("act", "activation", [ACC[k]], [sq], out=sq.ap[:, 0:T], in_=ACC[k].ap[:, 0:T], func=AF.Square)
                I("pe", "matmul", [ONB, sq], [s3], s3.ap[:, 0:T], lhsT=ONB.ap[:, :], rhs=sq.ap[:, 0:T],
                  start=(k == 0), stop=(k == 3))
            rs2 = bcast_rstd(s3, T, CCH, 1e-6)
            for k in range(4):
                I("dve", "tensor_tensor", [ACC[k], rs2], [YN[k]], out=YN[k].ap[:, 0:T], in0=ACC[k].ap[:, 0:T],
                  in1=rs2.ap[:, 0:T], op=ALU.mult)

        def q_head(h, T):
            psa = G.next()
            psb_ = G.next()
            for c in range(3):
                I("pe", "matmul", [WUQ, CQT], [psa], psa.ap[0:96, 0:T], lhsT=WUQ.ap[:, c, h * 192:h * 192 + 96],
                  rhs=CQT.ap[:, c, 0:T], start=(c == 0), stop=(c == 2))
            for c in range(3):
                I("pe", "matmul", [WUQ, CQT], [psb_], psb_.ap[0:96, 0:T], lhsT=WUQ.ap[:, c, h * 192 + 96:h * 192 + 192],
                  rhs=CQT.ap[:, c, 0:T], start=(c == 0), stop=(c == 2))
            I("act", "copy", [psa], [QT[h]], out=QT[h].ap[0:64, 0:T], in_=psa.ap[0:64, 0:T])
            t1 = TMP.next()
            t2 = TMP.next()
            I("dve", "tensor_tensor", [psa, CST], [t1], out=t1.ap[64:96, 0:T], in0=psa.ap[64:96, 0:T],
              in1=CST.ap[64:96, 0, 0:T], op=ALU.mult)
            I("dve", "tensor_tensor", [psb_, CST], [t2], out=t2.ap[64:96, 0:T], in0=psb_.ap[64:96, 0:T],
              in1=CST.ap[64:96, 1, 0:T], op=ALU.mult)
            I("pool" if h % 2 == 0 else "dve", "tensor_tensor", [t1, t2, QT[h]], [QT[h]], out=QT[h].ap[64:96, 0:T],
              in0=t1.ap[64:96, 0:T], in1=t2.ap[64:96, 0:T], op=ALU.add)

        def kv_load(sq, h, blk):
            kh = KH.next()
            vh = VH.next()
            kt0 = blk[0][0]
            nkeys = sum(t[1] for t in blk)
            tis = sorted(set(t[0] // 4 for t in blk))
            fw.dma("sp", kh.ap[0:96, 0:nkeys], sq.kts[h, :, kt0 * 128:kt0 * 128 + nkeys], [sq.KS[j] for j in tis], [kh], kh)
            pv = min(t[1] for t in blk)
            assert pv == 128 or all(t[1] == pv for t in blk)
            fw.dma("sp", vh.ap[0:pv, 0:len(blk), :], sq.vs[h, 0:pv, kt0:kt0 + len(blk), :], [sq.VSB[j] for j in tis], [vh], vh)
            return (kh, vh)

        def attn_head(sq, h, T, ktl, hooks, pre, prefetch):
            o = OB.next()
            nk_tiles = len(ktl)
            blocks = [ktl[i:i + 8] for i in range(0, nk_tiles, 8)]
            loaded = [pre]
            pf = {"done": False}

            def load(bi):
                loaded.append(kv_load(sq, h, blocks[bi]))
            flat = []
            for bi, blk in enumerate(blocks):
                for li, t in enumerate(blk):
                    flat.append((bi, li, t))
            units = []
            i_ = 0
            while i_ < len(flat):
                t_ = flat[i_][2]
                if (T == 512 and i_ + 1 < len(flat) and t_[1] == 128 and t_[2] == 0 and not t_[3]
                        and flat[i_ + 1][2][1] == 128 and flat[i_ + 1][2][2] == 0 and not flat[i_ + 1][2][3]):
                    units.append([flat[i_], flat[i_ + 1]])
                    i_ += 2
                else:
                    units.append([flat[i_]])
                    i_ += 1
            seen_blocks = set([0])

            def smm(unit):
                if len(unit) == 2 and G.i % 2:
                    G.i += 1
                banks = []
                for n_, item in enumerate(unit):
                    bi, li, (kt, nk, c0, masked) = item
                    if bi + 1 < len(blocks) and (bi + 1) not in seen_blocks:
                        seen_blocks.add(bi + 1)
                        load(bi + 1)
                    if bi == len(blocks) - 1 and not pf["done"]:
                        pf["done"] = True
                        prefetch()
                    kh, vh = loaded[bi]
                    ps = G.next()
                    I("pe", "matmul", [kh, QT[h]], [ps], ps.ap[0:nk, c0:T], lhsT=kh.ap[0:96, li * 128:li * 128 + nk],
                      rhs=QT[h].ap[0:96, c0:T], start=True, stop=True)
                    banks.append(ps)
                return banks

            DEPTH = 2
            pss = [smm(units[i_]) for i_ in range(min(DEPTH, len(units)))]
            done = 0
            for ui, unit in enumerate(units):
                banks = pss.pop(0)
                pts = PTS.next()
                if len(unit) == 2:
                    gi = int(banks[0].name[1:])
                    if os.environ.get("KSPLIT"):
                        for n_ in range(2):
                            I("act", "activation", banks, [pts], out=pts.ap[:, n_ * T:(n_ + 1) * T], in_=banks[n_].ap[:, 0:T],
                              func=AF.Exp, scale=SC_ATT)
                    else:
                        I("act", "activation", banks, [pts], out=pts.ap[:, 0:2 * T].rearrange("p (b t) -> p b t", t=T),
                          in_=psG[:, gi * 512:gi * 512 + 2 * T].rearrange("p (b t) -> p b t", t=T),
                          func=AF.Exp, scale=SC_ATT)
                else:
                    bi, li, (kt, nk, c0, masked) = unit[0]
                    I("act", "activation", banks, [pts], out=pts.ap[0:nk, c0:T], in_=banks[0].ap[0:nk, c0:T], func=AF.Exp, scale=SC_ATT)
                if ui + DEPTH < len(units):
                    pss.append(smm(units[ui + DEPTH]))
                for n_, item in enumerate(unit):
                    bi, li, (kt, nk, c0, masked) = item
                    kh, vh = loaded[bi]
                    if masked and done == 0:
                        I("pool", "memset", [pts], [pts], pts.ap[64:128, c0:c0 + 64], 0.0)
                        I("pe", "matmul", [vh, pts], [o], o.ap[0:65, c0:T], lhsT=vh.ap[0:nk, li, 0:65],
                          rhs=pts.ap[0:nk, n_ * T + c0:n_ * T + T], start=True, stop=(done == len(flat) - 1))
                    elif masked:
                        I("pe", "matmul", [vh, pts], [o], o.ap[0:65, c0 + 64:T], lhsT=vh.ap[0:128, li, 0:65],
                          rhs=pts.ap[0:128, c0 + 64:T], start=False, stop=False)
                        I("pe", "matmul", [vh, pts], [o], o.ap[0:65, c0:c0 + 64], lhsT=vh.ap[0:64, li, 0:65],
                          rhs=pts.ap[0:64, c0:c0 + 64], start=False, stop=(done == len(flat) - 1))
                    else:
                        I("pe", "matmul", [vh, pts], [o], o.ap[0:65, c0:T], lhsT=vh.ap[0:nk, li, 0:65],
                          rhs=pts.ap[0:nk, n_ * T + c0:n_ * T + T], start=(done == 0), stop=(done == len(flat) - 1))
                    done += 1
                for key in sorted(k_ for k_ in list(hooks) if k_ <= done - 1 or done == len(flat)):
                    hooks.pop(key)()
            return o

        def attn_fin_a(o, T):
            I("dve", "reciprocal", [o], [RL], out=RL.ap[64:65, 0:T], in_=o.ap[64:65, 0:T])
            I("dve", "tensor_copy", [RL], [RLH], out=RLH.ap[64:65, 0:T], in_=RL.ap[64:65, 0:T])
            I("dve", "tensor_tensor", [RL, RLH], [RLL], out=RLL.ap[64:65, 0:T], in0=RL.ap[64:65, 0:T],
              in1=RLH.ap[64:65, 0:T], op=ALU.subtract)

        def attn_fin_b(o, h, T):
            bc = GH.next()
            I("pe", "matmul", [ONB, RLH], [bc], bc.ap[0:64, 0:T], lhsT=ONB.ap[64:65, 0:64], rhs=RLH.ap[64:65, 0:T],
              start=True, stop=False)
            I("pe", "matmul", [ONB, RLL], [bc], bc.ap[0:64, 0:T], lhsT=ONB.ap[64:65, 0:64], rhs=RLL.ap[64:65, 0:T],
              start=False, stop=True)
            bcs = TMP.next()
            I("act", "copy", [bc], [bcs], out=bcs.ap[0:64, 0:T], in_=bc.ap[0:64, 0:T])
            j, half = h // 2, (h % 2) * 64
            I("dve", "tensor_tensor", [o, bcs, ATT[j]], [ATT[j]], out=ATT[j].ap[half:half + 64, 0:T], in0=o.ap[0:64, 0:T],
              in1=bcs.ap[0:64, 0:T], op=ALU.mult)

        def attn_post(T):
            s4 = G.next()
            for j in range(4):
                sq = SQB.next()
                I("act", "activation", [ATT[j]], [sq], out=sq.ap[:, 0:T], in_=ATT[j].ap[:, 0:T], func=AF.Square)
                I("pe", "matmul", [ONB, sq], [s4], s4.ap[:, 0:T], lhsT=ONB.ap[:, :], rhs=sq.ap[:, 0:T],
                  start=(j == 0), stop=(j == 3))
            rsa = bcast_rstd(s4, T, 512, 1e-6)
            for j in range(4):
                I("dve", "tensor_tensor", [ATT[j], rsa], [ATT[j]], out=ATT[j].ap[:, 0:T], in0=ATT[j].ap[:, 0:T],
                  in1=rsa.ap[:, 0:T], op=ALU.mult)

        def x_load(nsub, TS, x_src, t0):
            for s in range(nsub):
                fw.dma("sp", X[s].ap[:TS, :], x_src[t0 + s * TS:t0 + (s + 1) * TS, :], [], [X[s]], X[s])

        def out_proj(T, nsub, TS, after=None):
            MIX = ATT + YN
            wps = []
            for n in range(2):
                wp = WPC.next()
                wv = wp.ap[:, :].rearrange("p (c n) -> p c n", n=512)
                fw.dma("sp", wv, wview(wos, n * 512, (n + 1) * 512), [], [wp], wp)
                wps.append((wp, wv))
            for s in range(nsub):
                for n in range(2):
                    wp, wv = wps[n]
                    ps = G.next()
                    for k in range(8):
                        I("pe", "matmul", [MIX[k], wp], [ps], ps.ap[:TS, :], lhsT=MIX[k].ap[:, s * TS:(s + 1) * TS],
                          rhs=wv[:, k, :], start=(k == 0), stop=(k == 7))
                    I("dve", "tensor_tensor", [ps, X[s]], [X[s]], out=X[s].ap[:TS, n * 512:(n + 1) * 512], in0=ps.ap[:TS, :],
                      in1=X[s].ap[:TS, n * 512:(n + 1) * 512], op=ALU.add)
                if after and s in after:
                    after[s]()

        def ffn(T, nsub, TS, slots=None, prenorm=None):
            slots = dict(slots or {})

            def run_slot(k):
                for f_ in slots.pop(k, []):
                    f_()
            if prenorm is None:
                norm_transpose(X[:nsub], list(range(nsub)), TS, HT)
            else:
                norm_b(prenorm, [2, 3], TS, [HT, HT], HT.ap)
            st_ = {}

            def up(g):
                wu = WPC.next()
                wuv = wu.ap[:, :].rearrange("p (c n) -> p c n", n=512)
                fw.dma("sp", wuv, wview(wus, g * 512, (g + 1) * 512), [], [wu], wu)
                wd = WPC.next()
                wdv = wd.ap[:, :].rearrange("p (j n) -> p j n", n=D)
                fw.dma("sp", wdv, wds_v[:, g * 4:(g + 1) * 4, :], [], [wd], wd)
                at = ATR.next()
                for jj in range(4):
                    ps = G.next()
                    for c in range(8):
                        I("pe", "matmul", [wu, HT], [ps], ps.ap[:, 0:T], lhsT=wuv[:, c, jj * 128:(jj + 1) * 128],
                          rhs=HT.ap[:, c, 0:T], start=(c == 0), stop=(c == 7))
                    r = TMP.next()
                    I("act", "activation", [ps], [r], out=r.ap[:, 0:T], in_=ps.ap[:, 0:T], func=AF.Relu)
                    I("dve", "tensor_tensor", [ps, r, at], [at], out=at.ap[:, jj, 0:T], in0=ps.ap[:, 0:T], in1=r.ap[:, 0:T],
                      op=ALU.mult)
                st_[g] = (at, wd, wdv)

            def down(g):
                at, wd, wdv = st_.pop(g)
                for n in range(2):
                    for s in range(nsub):
                        ps = G.next()
                        for jj in range(4):
                            I("pe", "matmul", [at, wd], [ps], ps.ap[:TS, :], lhsT=at.ap[:, jj, s * TS:(s + 1) * TS],
                              rhs=wdv[:, jj, n * 512:(n + 1) * 512], start=(jj == 0), stop=(jj == 3))
                        I("dve", "tensor_tensor", [ps, X[s]], [X[s]], out=X[s].ap[:TS, n * 512:(n + 1) * 512],
                          in0=ps.ap[:TS, :], in1=X[s].ap[:TS, n * 512:(n + 1) * 512], op=ALU.add)

            up(0)
            for g in range(8):
                run_slot(2 * g)
                if g + 1 < 8:
                    up(g + 1)
                run_slot(2 * g + 1)
                down(g)
            for k in sorted(slots):
                run_slot(k)

        def ple_pieces(T, nsub, TS, p_src, t0):
            subs = list(range(nsub))
            prs = [subs[i0:i0 + 2] for i0 in range(0, nsub, 2)]
            st_ = {}

            def p0():
                st_["xs0"] = norm_a([X[s] for s in prs[0]], prs[0], TS)
                fw.dma("pool", PB.ap[:TS, 0:nsub, :], p_src[t0:t0 + T, :].rearrange("(s p) d -> p s d", p=TS), [], [PB], PB)

            def p1():
                norm_b(st_.pop("xs0"), prs[0], TS, [HT] * len(prs[0]), HT.ap)
                if len(prs) > 1:
                    st_["xs1"] = norm_a([X[s] for s in prs[1]], prs[1], TS)
                for s in range(nsub):
                    tb = TB.next()
                    tv = tbv(tb)
                    for c in range(2):
                        I("pe", "transpose", [PB, IDB], [tb], out=tv[:, c, 0:TS], in_=PB.ap[:TS, s, c * 128:(c + 1) * 128],
                          identity=IDB.ap[:TS, :TS])
                    copy_on(evac_eng(), [tb], [PTR], PTR.ap[:, :, s * TS:(s + 1) * TS], tv[:, 0:2, 0:TS])
                wgl = []
                for n in range(2):
                    wg = WPC.next()
                    wgv = wg.ap[:, :].rearrange("p (c n) -> p c n", n=512)
                    fw.dma("sp", wgv, wview(wgs, n * 512, (n + 1) * 512), [], [wg], wg)
                    wgl.append((wg, wgv))
                st_["wg"] = wgl

            def p2():
                if len(prs) > 1:
                    norm_b(st_.pop("xs1"), prs[1], TS, [HT] * len(prs[1]), HT.ap)
                for n in range(2):
                    wg, wgv = st_["wg"][n]
                    for s in range(nsub):
                        ps = G.next()
                        for c in range(8):
                            I("pe", "matmul", [HT, wg], [ps], ps.ap[:TS, :], lhsT=HT.ap[:, c, s * TS:(s + 1) * TS],
                              rhs=wgv[:, c, :], start=(c == 0), stop=(c == 7))
                        sg = TMP.next()
                        I("act", "activation", [ps], [sg], out=sg.ap[:TS, :], in_=ps.ap[:TS, :], func=AF.Tanh, scale=0.5)
                        pp_ = G.next()
                        for c in range(2):
                            I("pe", "matmul", [PTR, WP], [pp_], pp_.ap[:TS, :], lhsT=PTR.ap[:, c, s * TS:(s + 1) * TS],
                              rhs=WP.ap[:, c, n * 512:(n + 1) * 512], start=(c == 0), stop=(c == 1))
                        tm = TMP.next()
                        I("dve", "scalar_tensor_tensor", [pp_, sg], [tm], out=tm.ap[:TS, :], in0=sg.ap[:TS, :], scalar=1.0,
                          in1=pp_.ap[:TS, :], op0=ALU.add, op1=ALU.mult)
                        I("pool", "tensor_tensor", [X[s], tm], [X[s]], out=X[s].ap[:TS, n * 512:(n + 1) * 512],
                          in0=X[s].ap[:TS, n * 512:(n + 1) * 512], in1=tm.ap[:TS, :], op=ALU.add)
            return [p0, p1, p2]

        def ple(T, nsub, TS, p_src, t0):
            for f_ in ple_pieces(T, nsub, TS, p_src, t0):
                f_()

        def final_store(T, nsub, TS, y_dst, t0):
            ybs, yvs = [], []
            for s in range(nsub):
                if s % 2 == 0:
                    yb = WPC.next()
                ybs.append(yb)
                yvs.append(yb.ap[:, :].bitcast(F32)[:TS, (s % 2) * D:(s % 2 + 1) * D])
            rs = group_stats([(X[s].ap[:TS, :], [X[s]], D, yvs[s], ybs[s]) for s in range(nsub)], TS, 1e-6)
            for s in range(nsub):
                I("dve", "scalar_tensor_tensor", [X[s], rs, BC], [ybs[s]], out=yvs[s], in0=X[s].ap[:TS, :],
                  scalar=rs.ap[:TS, s:s + 1], in1=BC.ap[:TS, 256:256 + D], op0=ALU.mult, op1=ALU.mult)
                fw.dma("pool", y_dst[t0 + s * TS:t0 + (s + 1) * TS, :], yvs[s], [ybs[s]], [], ybs[s])

        def front_slots(T, nsub, TS, t0, x_src, cs_src, cst_src, ckv_dst, kr_dst):
            st_ = {}

            def n_a(subs, first):
                def f():
                    if first:
                        fw.dma("sp", CS.ap[:TS, 0:nsub, :], cs_src[t0:t0 + T, :].rearrange("(s p) d -> p s d", p=TS), [], [CS], CS)
                        fw.dma("sp", CST.ap[:, :, 0:T], cst_src[:, :, t0:t0 + T], [], [CST], CST)
                    xfs = []
                    for s in subs:
                        xf = XF.next()
                        fw.dma("sp", xf.ap[:TS, :], x_src[t0 + s * TS:t0 + (s + 1) * TS, :], [], [xf], xf)
                        xfs.append(xf)
                    st_[("xs", tuple(subs))] = norm_a(xfs, subs, TS)
                return f

            def n_b(subs):
                def f():
                    norm_b(st_.pop(("xs", tuple(subs))), subs, TS, [HTP[s] for s in subs], HTPT)
                return f

            def s_a(s):
                def f():
                    psA = G.next()
                    psB = G.next()
                    for c in range(8):
                        I("pe", "matmul", [HTP[s], WIN], [psA], psA.ap[:TS, 0:320], lhsT=HTPT[:, c, s * TS:(s + 1) * TS],
                          rhs=WIN.ap[:, c, 0:320], start=(c == 0), stop=(c == 7))
                    for c in range(8):
                        I("pe", "matmul", [HTP[s], WIN], [psB], psB.ap[:TS, 0:384], lhsT=HTPT[:, c, s * TS:(s + 1) * TS],
                          rhs=WIN.ap[:, c, 320:704], start=(c == 0), stop=(c == 7))
                    st_[("ps", s)] = (psA, psB)
                return f

            def s_b(s):
                def f():
                    psA, psB = st_.pop(("ps", s))
                    cko = CKVO.next()
                    cqb = CQB.next()
                    rs = group_stats([(psA.ap[:TS, 0:KVL], [psA], KVL, cko.ap[:TS, :], cko),
                                      (psB.ap[:TS, 0:QLORA], [psB], QLORA, cqb.ap[:TS, :], cqb)], TS, 1e-6)
                    I("dve", "scalar_tensor_tensor", [psA, rs, BC], [cko], out=cko.ap[:TS, :], in0=psA.ap[:TS, 0:KVL],
                      scalar=rs.ap[:TS, 0:1], in1=BC.ap[:TS, 0:KVL], op0=ALU.mult, op1=ALU.mult)
                    fw.dma("pool", ckv_dst[t0 + s * TS:t0 + (s + 1) * TS, :], cko.ap[:TS, :], [cko], [], cko)
                    ckb = CKVB.next()
                    I("pool", "tensor_copy", [cko], [ckb], out=ckb.ap[:TS, :], in_=cko.ap[:TS, :])
                    t1 = SMALL.next()
                    t2 = SMALL.next()
                    I("dve", "tensor_tensor", [psA, CS], [t1], out=t1.ap[:TS, :], in0=psA.ap[:TS, 256:288],
                      in1=CS.ap[:TS, s, 0:32], op=ALU.mult)
                    I("dve", "tensor_tensor", [psA, CS], [t2], out=t2.ap[:TS, :], in0=psA.ap[:TS, 288:320],
                      in1=CS.ap[:TS, s, 32:64], op=ALU.mult)
                    kro = KRO.next()
                    I("pool", "tensor_tensor", [t1, t2], [kro], out=kro.ap[:TS, :], in0=t1.ap[:TS, :], in1=t2.ap[:TS, :], op=ALU.add)
                    fw.dma("pool", kr_dst[t0 + s * TS:t0 + (s + 1) * TS, :], kro.ap[:TS, :], [kro], [], kro)
                    krb = KRB.next()
                    I("pool", "tensor_copy", [kro], [krb], out=krb.ap[:TS, 64:96], in_=kro.ap[:TS, :])
                    I("dve", "tensor_scalar", [psB, rs], [cqb], out=cqb.ap[:TS, :], in0=psB.ap[:TS, 0:QLORA],
                      scalar1=rs.ap[:TS, 1:2], scalar2=None, op0=ALU.mult)
                    st_[("b", s)] = (ckb, krb, cqb)
                return f

            def s_c(s):
                def f():
                    ckb, krb, cqb = st_.pop(("b", s))
                    transposes_small(ckb, krb, cqb, s, TS)
                return f

            def q_(h):
                return lambda: q_head(h, T)

            if nsub == 4:
                return {0: [n_a([0, 1], True)], 2: [n_b([0, 1]), n_a([2, 3], False)], 4: [n_b([2, 3]), s_a(0), s_b(0)],
                        5: [s_a(1), s_b(1)], 6: [s_c(0), s_a(2), s_b(2)], 7: [s_c(1), s_a(3), s_b(3)], 8: [s_c(2)], 9: [s_c(3)],
                        10: [q_(0)], 11: [q_(1), q_(2)], 12: [q_(3)], 13: [q_(4), q_(5)], 14: [q_(6)], 15: [q_(7)]}
            subs = list(range(nsub))
            return {0: [n_a(subs, True), n_b(subs)] + [g_(s) for s in subs for g_ in (s_a, s_b, s_c)] + [q_(h) for h in range(NH)]}

        def run_slots(slots):
            for k in sorted(slots):
                for f_ in slots[k]:
                    f_()

        def mid(sq, T, nsub, TS, t0, x_src, ktl, inter=None, pre0=None):
            if inter is None:
                x_load(nsub, TS, x_src, t0)
                inter = {}
            prev = None
            nkt = len(ktl)
            pre = pre0 if pre0 is not None else kv_load(sq, 0, ktl[0:8])
            cw = conv_w_load(0)
            for h in range(NH):
                nxt = {}

                def prefetch(h=h, nxt=nxt):
                    if h + 1 < NH:
                        nxt["v"] = kv_load(sq, h + 1, ktl[0:8])
                hooks = {}
                last = nkt - 1

                def addhook(pos, fn, hooks=hooks):
                    pos = min(pos, last)
                    while pos in hooks:
                        pos += 0.01
                    hooks[pos] = fn
                if prev is not None:
                    po, ph = prev
                    addhook(0, (lambda po=po: attn_fin_a(po, T)))
                    addhook(max(6, last - 4), (lambda po=po, ph=ph: attn_fin_b(po, ph, T)))
                if h % 2 == 0:
                    tg = {}

                    def conv_a(h=h, cw=cw, tg=tg):
                        tg["g"] = conv_chunk(h // 2, T, cw)
                    addhook(3, conv_a)
                    addhook(6, (lambda tg=tg: tg["g"][0]()))
                    addhook(9, (lambda tg=tg: tg["g"][1]()))
                    addhook(12, (lambda tg=tg: tg["g"][2]()))
                    tg_prev = tg
                else:
                    addhook(2, (lambda tg=tg_prev: tg["g"][3]()))
                    addhook(5, (lambda tg=tg_prev: tg["g"][4]()))
                    addhook(8, (lambda tg=tg_prev: tg["g"][5]()))
                    if h + 1 < NH:
                        cw = conv_w_load((h + 1) // 2)
                o = attn_head(sq, h, T, ktl, hooks, pre, prefetch)
                pre = nxt.get("v")
                prev = (o, h)
                if h in inter:
                    inter[h]()
            attn_fin_a(prev[0], T)
            attn_fin_b(prev[0], prev[1], T)
            attn_post(T)
            conv_post(T)
            if nsub == 4:
                hold = {}

                def a1():
                    hold["a"] = norm_a(X[0:2], [0, 1], TS)

                def a3():
                    norm_b(hold.pop("a"), [0, 1], TS, [HT, HT], HT.ap)
                    hold["b"] = norm_a(X[2:4], [2, 3], TS)
                out_proj(T, nsub, TS, {1: a1, 3: a3})
                return hold["b"]
            out_proj(T, nsub, TS)
            return None

        def conv_state_out(dst):
            ps = G.next()
            for k in range(4):
                I("pe", "transpose", [UT, IDF], [ps], out=ps.ap[0:CK - 1, k * 128:(k + 1) * 128], in_=UT.ap[:, k, :],
                  identity=IDF.ap[:, :])
            tm = TMP.next()
            I("act", "copy", [ps], [tm], out=tm.ap[0:CK - 1, :], in_=ps.ap[0:CK - 1, :])
            fw.dma("pool", dst, tm.ap[0:CK - 1, :], [tm], [], tm)

        sp_ = Seq()
        sp_.kts, sp_.vs = kts_p, vs_p
        sp_.KS = [Buf("KSp%d" % j, None) for j in range(NT)]
        sp_.VSB = [Buf("VSp%d" % j, None) for j in range(NT)]
        I("pool", "memset", [], [UT], UT.ap[:, :, :], 0.0)

        def fp_prompt(i):
            return front_slots(512, 4, 128, i * 512, x_p, cs_p, cst_p, ckv_p, kr_p)

        run_slots(fp_prompt(0))
        kv_expand(sp_, 512, 4, 128, 0)
        fold_all_scratch()
        pre0 = None
        for i in range(NT):
            ktl = [(kt, 128, 0, False) for kt in range(4 * i)] + [(4 * i + r, 128, 128 * r, True) for r in range(4)]
            inter = None
            if i > 0:
                pp3 = ple_pieces(512, 4, 128, p_p, (i - 1) * 512)

                def after_h3(i=i):
                    final_store(512, 4, 128, y_p, (i - 1) * 512)

                def after_h6(i=i):
                    x_load(4, 128, x_p, i * 512)
                inter = {0: pp3[0], 1: pp3[1], 2: pp3[2], 3: after_h3, 6: after_h6}
            pn = mid(sp_, 512, 4, 128, i * 512, x_p, ktl, inter, pre0)
            ffn(512, 4, 128, fp_prompt(i + 1) if i + 1 < NT else None, pn)
            pre0 = None
            if i + 1 < NT:
                if i + 1 >= 2:
                    ktl_n = [(kt, 128, 0, False) for kt in range(4 * (i + 1))]
                    pre0 = kv_load(sp_, 0, ktl_n[0:8])
                kv_expand(sp_, 512, 4, 128, (i + 1) * 512)
        tail_ov = bool(with_sample and WSAMPLE)
        if not tail_ov:
            ple(512, 4, 128, p_p, (NT - 1) * 512)
            final_store(512, 4, 128, y_p, (NT - 1) * 512)
        conv_state_out(conv_p)

        if with_sample and WSAMPLE:
            ss_ = Seq()
            ss_.kts, ss_.vs = kts_s, vs_s
            ss_.KS = [Buf("KSs%d" % j, None) for j in range(3)]
            ss_.VSB = [Buf("VSs%d" % j, None) for j in range(3)]
            for blk in range(PAST // 512):
                for s in range(4):
                    r0 = blk * 512 + s * 128
                    ckb = CKVB.next()
                    fw.dma("pool", ckb.ap[:, :], c_ckv[r0:r0 + 128, :], [], [ckb], ckb)
                    krb = KRB.next()
                    fw.dma("pool", krb.ap[:, 64:96], c_kr[r0:r0 + 128, :], [], [krb], krb)
                    transposes_small(ckb, krb, None, s, 128)
                kv_expand(ss_, 512, 4, 128, blk * 512)
            tm = TMP.next()
            fw.dma("sp", tm.ap[0:CK - 1, :], s_conv, [], [tm], tm)
            ps = G.next()
            for k in range(4):
                I("pe", "transpose", [tm, IDF], [ps], out=ps.ap[:, k * 32:k * 32 + CK - 1], in_=tm.ap[0:CK - 1, k * 128:(k + 1) * 128],
                  identity=IDF.ap[0:CK - 1, 0:CK - 1])
            I("dve", "tensor_copy", [ps], [UT], out=UT.ap[:, :, :],
              in_=ps.ap[:, 0:128].rearrange("p (k t) -> p k t", t=32)[:, :, 0:CK - 1])
            run_slots(front_slots(DEC, 1, DEC, 0, x_s, cs_s, cst_s, ckv_s, kr_s))
            kv_expand(ss_, DEC, 1, DEC, PAST)
            ktl = [(kt, 128, 0, False) for kt in range(PAST // 128)] + [(PAST // 128, DEC, 0, False)]
            ppl = ple_pieces(512, 4, 128, p_p, (NT - 1) * 512)

            def tail_h3():
                final_store(512, 4, 128, y_p, (NT - 1) * 512)

            def tail_h6():
                x_load(1, DEC, x_s, 0)
            mid(ss_, DEC, 1, DEC, 0, x_s, ktl, {0: ppl[0], 1: ppl[1], 2: ppl[2], 3: tail_h3, 6: tail_h6})
            ffn(DEC, 1, DEC)
            ple(DEC, 1, DEC, p_s, 0)
            final_store(DEC, 1, DEC, y_s, 0)
            conv_state_out(conv_s)

        fw.finish()
        nc._sbuf_bytes = fw.sbuf_bytes
    return nc


def _rope_tables(pos):
    half = ROPE // 2
    inv = (10000.0 ** (-np.arange(half, dtype=np.float32) / half)).astype(np.float32)
    ang = pos.astype(np.float32)[:, None] * inv[None, :]
    cos, sin = np.cos(ang).astype(np.float32), np.sin(ang).astype(np.float32)
    cos32 = np.concatenate([cos, cos], axis=1)
    ssin32 = np.concatenate([-sin, sin], axis=1)
    tok = np.ascontiguousarray(np.concatenate([cos32, ssin32], axis=1))
    ft = np.zeros((96, 2, pos.shape[0]), np.float32)
    ft[64:96, 0, :] = cos32.T
    ft[64:96, 1, :] = ssin32.T
    return tok, ft


def _prep_shared(inp, S):
    f = lambda a: np.ascontiguousarray(np.asarray(a, dtype=np.float32))
    w_in = f(inp["w_in"])[0]
    cq, ckv, kr, conv = w_in[:, 0:384], w_in[:, 384:640], w_in[:, 640:672], w_in[:, 672:1696]
    krs = np.concatenate([kr[:, 16:32], kr[:, 0:16]], axis=1)
    conv_l = np.concatenate([np.concatenate([conv[:, k * 128:(k + 1) * 128], conv[:, 512 + k * 128:512 + (k + 1) * 128]], axis=1)
                             for k in range(4)], axis=1)
    w_in_ext = np.ascontiguousarray(np.concatenate([ckv, kr, krs, cq, conv_l], axis=1))
    w_uq = f(inp["w_uq"])[0]
    blocks = []
    for h in range(NH):
        a = w_uq[:, h * 96:(h + 1) * 96]
        b = np.concatenate([np.zeros((QLORA, 64), np.float32), a[:, 80:96], a[:, 64:80]], axis=1)
        blocks += [a, b]
    w_uq_ext = np.ascontiguousarray(np.concatenate(blocks, axis=1))
    w_ukv = f(inp["w_ukv"])[0].reshape(KVL, NH, 128)
    w_uk = np.ascontiguousarray(w_ukv[:, :, 0:64].reshape(KVL, 512))
    w_uv = np.ascontiguousarray(w_ukv[:, :, 64:128].reshape(KVL, 512))

    def chunks(v, n):
        return np.asarray(v, np.float32).reshape(n, 128).T

    gout = np.concatenate([f(inp["attn_out_g"])[0], f(inp["conv_out_g"])[0]])
    cw = f(inp["conv_w"])[0]
    cw_l = cw.T.reshape(4, 128, CK).transpose(1, 0, 2).reshape(128, 4 * CK)
    pp = np.concatenate([
        chunks(f(inp["norm_mix_g"])[0], 8), chunks(f(inp["q_norm_g"])[0], 3), chunks(f(inp["norm_ffn_g"])[0], 8),
        chunks(f(inp["norm_ple_g"])[0], 8), chunks(gout, 8), chunks(f(inp["conv_b"])[0], 4),
        chunks(f(inp["conv_ln_g"])[0], 4), chunks(f(inp["conv_ln_b"])[0], 4), cw_l], axis=1)
    assert pp.shape == (128, 171)
    bc = np.concatenate([np.broadcast_to(f(inp["kv_norm_g"])[0][None, :], (128, KVL)),
                         np.broadcast_to(f(inp["norm_final_g"])[None, :], (128, D))], axis=1)
    cs_p, cst_p = _rope_tables(np.arange(S))
    cs_s, cst_s = _rope_tables(PAST + np.arange(DEC))
    return dict(w_in=w_in_ext, w_uq=w_uq_ext, w_uk=w_uk, w_uv=w_uv, w_out=f(inp["w_out"])[0], w_up=f(inp["w_ff_up"])[0],
                w_down=f(inp["w_ff_down"])[0], w_gate=f(inp["w_ple_gate"])[0], w_pp=f(inp["w_ple_proj"])[0],
                pp=np.ascontiguousarray(pp), bc=np.ascontiguousarray(bc), cs_p=cs_p, cs_s=cs_s, cst_p=cst_p, cst_s=cst_s)


_NC_CACHE = {}


def run(inputs, S=SEQ, ncores=NCORES, with_sample=True, trace=False):
    f = lambda a: np.ascontiguousarray(np.asarray(a, dtype=np.float32))
    shared = _prep_shared(inputs, S)
    key = (S, with_sample)
    if key not in _NC_CACHE:
        _NC_CACHE[key] = build_nc(S, with_sample)
    nc = _NC_CACHE[key]
    in_maps = []
    for b in range(ncores):
        m = dict(shared)
        m["x_p"] = f(inputs["x_prompt"][b][:S])
        m["p_p"] = f(inputs["p_prompt"][0, b][:S])
        m["x_s"] = f(inputs["x_sample"][b])
        m["p_s"] = f(inputs["p_sample"][0, b])
        m["c_ckv"] = f(inputs["cache_ckv"][0, b])
        m["c_kr"] = f(inputs["cache_krope"][0, b])
        m["s_conv"] = f(inputs["state_conv"][0, b])
        in_maps.append(m)
    res = run_bass_kernel_spmd(nc, in_maps, core_ids=list(range(ncores)), trace=trace)
    r = res.results
    st = lambda k: np.stack([np.asarray(r[b][k], dtype=np.float32) for b in range(ncores)])
    outs = (st("y_p"), st("y_s"), st("ckv_p")[None], st("kr_p")[None], st("conv_p")[None],
            st("ckv_s")[None], st("kr_s")[None], st("conv_s")[None])
    return outs, res


def kernel(**inputs):
    outs, _ = run(inputs)
    return outs
```

```python
import os
import numpy as np
from contextlib import ExitStack
import concourse.bass as bass
import concourse.mybir as mybir
from concourse.bass_utils import run_bass_kernel_spmd

F32 = mybir.dt.float32
BF16 = mybir.dt.bfloat16
ALU = mybir.AluOpType
AF = mybir.ActivationFunctionType

D = 1024
NH = 8
QLORA = 384
KVL = 256
ROPE = 32
CCH = 512
CK = 31
DFF = 4096
PLE = 256
PAST = 1024
DEC = 64
SEQ = 8192
NCORES = 8
WIN_COLS = 704 + 1024
WUQ_COLS = NH * 192
SC_ATT = 96 ** -0.5
STAGE = int(os.environ.get('KSTAGE', '99'))
WSAMPLE = int(os.environ.get('KSAMPLE', '1'))
KSUB = int(os.environ.get('KSUB', '99'))


class Buf:
    def __init__(self, name, ap):
        self.name = name
        self.ap = ap
        self.last_write = None
        self.readers = {}
        self.dsem = None
        self.dcount = 0
        self.psum = False


class Eng:
    def __init__(self, name, sem, is_pe=False):
        self.name = name
        self.sem = sem
        self.count = 0
        self.waited = {}
        self.ops = []
        self.is_pe = is_pe


class Rot:
    def __init__(self, bufs):
        self.bufs = bufs
        self.i = 0

    def next(self):
        b = self.bufs[self.i % len(self.bufs)]
        self.i += 1
        return b


class FW:
    def __init__(self, nc, stack):
        self.nc = nc
        self.stack = stack
        self.engs = {}
        self.sems = {}
        for n in ["pe", "act", "dve", "pool", "sp"]:
            sem = stack.enter_context(nc.semaphore("s_" + n))
            self.engs[n] = Eng(n, sem, is_pe=(n == "pe"))
            self.sems[id(sem)] = sem
        self.dbufs = []
        self.sbuf_bytes = 0

    def tile(self, name, shape, dt):
        t = self.stack.enter_context(self.nc.sbuf_tensor(name, shape, dt))
        n = 1
        for s in shape[1:]:
            n *= s
        self.sbuf_bytes += n * (4 if dt == F32 else 2)
        return t

    def sbuf(self, name, shape, dt):
        return Buf(name, self.tile(name, shape, dt))

    def _deps(self, eng, reads, writes):
        deps = {}

        def add(k, v):
            if deps.get(k, 0) < v:
                deps[k] = v
        for b in reads:
            if b.last_write is not None:
                add(*b.last_write)
            if b.psum:
                for k, v in b.readers.items():
                    if k != id(eng.sem):
                        add(k, v)
        for b in writes:
            if b.last_write is not None:
                add(*b.last_write)
            for k, v in b.readers.items():
                add(k, v)
        out = []
        for k, v in deps.items():
            if eng.is_pe and k == id(eng.sem):
                continue
            if eng.waited.get(k, 0) >= v:
                continue
            eng.waited[k] = v
            out.append((self.sems[k], v))
        return out

    def _record(self, ev, reads, writes):
        for b in reads:
            if b.readers.get(ev[0], 0) < ev[1]:
                b.readers[ev[0]] = ev[1]
        for b in writes:
            b.last_write = ev
            b.readers = {}

    def I(self, engname, method, reads, writes, *args, **kw):
        eng = self.engs[engname]
        waits = self._deps(eng, reads, writes)
        eng.count += 1
        ev = (id(eng.sem), eng.count)
        sem = eng.sem

        def emit(e):
            for s, v in waits:
                e.wait_ge(s, v)
            getattr(e, method)(*args, **kw).then_inc(sem, 1)
        eng.ops.append(emit)
        self._record(ev, reads, writes)

    def dma(self, qname, out_ap, in_ap, reads, writes, owner, **kw):
        eng = self.engs[qname]
        if owner.dsem is None:
            owner.dsem = {}
        if qname not in owner.dsem:
            sem_ = self.stack.enter_context(self.nc.semaphore("d%s_%s" % (qname, owner.name)))
            owner.dsem[qname] = [sem_, 0]
            self.sems[id(sem_)] = sem_
            self.dbufs.append(owner.dsem[qname])
        waits = self._deps(eng, reads, writes)
        owner.dsem[qname][1] += 16
        sem = owner.dsem[qname][0]
        ev = (id(sem), owner.dsem[qname][1])

        def emit(e):
            for s, v in waits:
                e.wait_ge(s, v)
            e.dma_start(out=out_ap, in_=in_ap, **kw).then_inc(sem, 16)
        eng.ops.append(emit)
        self._record(ev, reads, writes)

    def _all_events(self):
        final = {}
        for e in self.engs.values():
            if e.count:
                final[id(e.sem)] = e.count
        for sem_, cnt_ in self.dbufs:
            final[id(sem_)] = cnt_
        return final

    def barrier(self, engines=("pe", "act", "dve", "pool", "sp")):
        final = self._all_events()
        for n in engines:
            eng = self.engs[n]
            ws = []
            for k, v in final.items():
                if eng.waited.get(k, 0) >= v:
                    continue
                if k == id(eng.sem) and eng.is_pe:
                    continue
                eng.waited[k] = v
                ws.append((self.sems[k], v))

            def emit(e, ws=ws):
                for s, v in ws:
                    e.wait_ge(s, v)
            eng.ops.append(emit)

    def finish(self):
        self.barrier(engines=("sp",))
        nc = self.nc
        engs = self.engs
        with nc.Block() as block:
            @block.tensor
            def _(e):
                for f in engs["pe"].ops:
                    f(e)

            @block.scalar
            def _(e):
                for f in engs["act"].ops:
                    f(e)

            @block.vector
            def _(e):
                for f in engs["dve"].ops:
                    f(e)

            @block.gpsimd
            def _(e):
                for f in engs["pool"].ops:
                    f(e)

            @block.sync
            def _(e):
                for f in engs["sp"].ops:
                    f(e)


def build_nc(S=SEQ, with_sample=True, dbg=False):
    nc = bass.Bass("TRN2", target_bir_lowering=False)
    NT = S // 512

    def din(name, shape, dt=F32):
        return nc.dram_tensor(name, list(shape), dt, kind="ExternalInput").ap()

    def dout(name, shape, dt=F32):
        return nc.dram_tensor(name, list(shape), dt, kind="ExternalOutput").ap()

    def dscr(name, shape, dt=BF16):
        return nc.dram_tensor(name, list(shape), dt, kind="Internal").ap()

    x_p = din("x_p", [S, D])
    p_p = din("p_p", [S, PLE])
    x_s = din("x_s", [DEC, D])
    p_s = din("p_s", [DEC, PLE])
    c_ckv = din("c_ckv", [PAST, KVL])
    c_kr = din("c_kr", [PAST, ROPE])
    s_conv = din("s_conv", [CK - 1, CCH])
    w_in = din("w_in", [D, WIN_COLS])
    w_uq = din("w_uq", [QLORA, WUQ_COLS])
    w_uk = din("w_uk", [KVL, 512])
    w_uv = din("w_uv", [KVL, 512])
    w_out = din("w_out", [D, D])
    w_up = din("w_up", [D, DFF])
    w_down = din("w_down", [DFF, D])
    w_gate = din("w_gate", [D, D])
    w_pp = din("w_pp", [PLE, D])
    pp_in = din("pp", [128, 171])
    bc_in = din("bc", [128, 256 + 1024])
    cs_p = din("cs_p", [S, 64])
    cs_s = din("cs_s", [DEC, 64])
    cst_p = din("cst_p", [96, 2, S])
    cst_s = din("cst_s", [96, 2, DEC])

    y_p = dout("y_p", [S, D])
    y_s = dout("y_s", [DEC, D])
    ckv_p = dout("ckv_p", [S, KVL])
    kr_p = dout("kr_p", [S, ROPE])
    conv_p = dout("conv_p", [CK - 1, CCH])
    ckv_s = dout("ckv_s", [DEC, KVL])
    kr_s = dout("kr_s", [DEC, ROPE])
    conv_s = dout("conv_s", [CK - 1, CCH])

    wos = dscr("wos", [D, D])
    wus = dscr("wus", [D, DFF])
    wds = dscr("wds", [DFF, D])
    wgs = dscr("wgs", [D, D])
    wcs = dscr("wcs", [D, 1024])
    kts_p = dscr("kts_p", [NH, 96, S])
    vs_p = dscr("vs_p", [NH, 128, S // 128, 65])
    SK = PAST + 128
    kts_s = dscr("kts_s", [NH, 96, SK])
    vs_s = dscr("vs_s", [NH, 128, SK // 128, 65])

    st = ExitStack()
    with st:
        fw = FW(nc, st)
        I = fw.I

        XT = fw.tile("X", [128, 4, D], F32)
        X = [Buf("X%d" % s, XT[:, s, :]) for s in range(4)]
        HT = fw.sbuf("HT", [128, 8, 512], BF16)
        XS = Rot([fw.sbuf("XS%d" % i, [128, D], BF16) for i in range(2)])
        XF = Rot([fw.sbuf("XF%d" % i, [128, D], F32) for i in range(2)])
        HTPT = fw.tile("HTP", [128, 8, 512], BF16)
        HTP = [Buf("HTP%d" % s_, HTPT[:, :, s_ * 128:(s_ + 1) * 128]) for s_ in range(4)]
        KRT = fw.sbuf("KRT", [96, 512], BF16)
        STT = fw.tile("STT", [128, 64], F32)
        STATS = Rot([Buf("st%d" % i, STT[:, 2 * i:2 * i + 2]) for i in range(16)])
        SGRP = Rot([([Buf("sg%d_%d" % (g, j), STT[:, 32 + 8 * g + j:32 + 8 * g + j + 1]) for j in range(4)],
                     Buf("sr%d" % g, STT[:, 32 + 8 * g + 4:32 + 8 * g + 8])) for g in range(4)])
        SMT = fw.tile("SMT", [128, 4, 32], F32)
        SMALL = Rot([Buf("sm%d" % i, SMT[:, i, :]) for i in range(4)])
        CKVO = Rot([fw.sbuf("CKVO%d" % i, [128, KVL], F32) for i in range(2)])
        KRO = Rot([fw.sbuf("KRO%d" % i, [128, ROPE], F32) for i in range(2)])
        CKVB = Rot([fw.sbuf("CKVB%d" % i, [128, KVL], BF16) for i in range(2)])
        KRB = Rot([fw.sbuf("KRB%d" % i, [128, 96], BF16) for i in range(2)])
        CQB = Rot([fw.sbuf("CQB%d" % i, [128, QLORA], BF16) for i in range(2)])
        CKVT = fw.sbuf("CKVT", [128, 2, 512], BF16)
        CQT = fw.sbuf("CQT", [128, 3, 512], BF16)
        CS = fw.sbuf("CS", [128, 4, 64], F32)
        CST = fw.sbuf("CST", [96, 2, 512], F32)
        KTNT = fw.tile("KTN", [128, 8, 512], BF16)
        AT = [Buf("AT%d" % i, KTNT[:, 4 * i:4 * i + 4, :]) for i in range(2)]
        ATR = Rot(AT)
        VN = fw.sbuf("VN", [128, 8, 4, 65], BF16)
        QTT = fw.tile("QT", [96, 8, 512], BF16)
        QT = [Buf("QT%d" % h, QTT[:, h, :]) for h in range(NH)]
        UC = Rot([fw.sbuf("UC%d" % i, [128, 512 + CK - 1], F32) for i in range(2)])
        UT = fw.sbuf("UT", [128, 4, CK - 1], F32)
        ACCT = fw.tile("ACC", [128, 4, 512], F32)
        ACC = [Buf("ACC%d" % k, ACCT[:, k, :]) for k in range(4)]
        YNT = fw.tile("YN", [128, 4, 512], BF16)
        YN = [Buf("YN%d" % k, YNT[:, k, :]) for k in range(4)]
        TMP = Rot([fw.sbuf("TMP%d" % i, [128, 512], F32) for i in range(4)])
        STAT = Rot([fw.sbuf("STAT%d" % i, [128, 512], F32) for i in range(2)])
        KH = Rot([fw.sbuf("KH%d" % i, [96, 1024], BF16) for i in range(3)])
        VH = Rot([fw.sbuf("VH%d" % i, [128, 8, 65], BF16) for i in range(3)])
        PTS = Rot([fw.sbuf("PTS%d" % i, [128, 1024], BF16) for i in range(3)])
        RL = fw.sbuf("RL", [65, 512], F32)
        RLH = fw.sbuf("RLH", [65, 512], BF16)
        RLL = fw.sbuf("RLL", [65, 512], BF16)
        ATTT = fw.tile("ATT", [128, 4, 512], BF16)
        ATT = [Buf("ATT%d" % j, ATTT[:, j, :]) for j in range(4)]
        SQB = Rot([fw.sbuf("SQB%d" % i, [128, 512], BF16) for i in range(2)])
        WPC = Rot([fw.sbuf("WPC%d" % i, [128, 4096], BF16) for i in range(4)])
        PB = fw.sbuf("PB", [128, 4, PLE], BF16)
        PTR = fw.sbuf("PTR", [128, 2, 512], BF16)
        WIN = fw.sbuf("WIN", [128, 8, 704], BF16)
        WUQ = fw.sbuf("WUQ", [128, 3, WUQ_COLS], BF16)
        WUK = fw.sbuf("WUK", [128, 2, 512], BF16)
        WUV = fw.sbuf("WUV", [128, 2, 512], BF16)
        WP = fw.sbuf("WP", [128, 2, D], BF16)
        PP = fw.sbuf("PP", [128, 171], F32)
        BC = fw.sbuf("BC", [128, 256 + 1024], F32)
        IDB = fw.sbuf("IDB", [128, 128], BF16)
        IDF = fw.sbuf("IDF", [128, 128], F32)
        ONB = fw.sbuf("ONB", [128, 128], BF16)
        ONF = fw.sbuf("ONF", [128, 128], F32)
        psG = st.enter_context(nc.psum_tensor("psG", [128, 2048], F32))
        psb = [psG[:, i * 512:(i + 1) * 512] for i in range(4)] + \
              [st.enter_context(nc.psum_tensor("ps%d" % i, [128, 512], F32)) for i in range(4, 8)]
        G = Rot([Buf("G%d" % i, psb[i]) for i in range(4)])
        OB = Rot([Buf("O%d" % i, psb[4 + i]) for i in range(2)])
        TB = Rot([Buf("T%d" % i, psb[6 + i]) for i in range(2)])
        for b_ in G.bufs + OB.bufs + TB.bufs:
            b_.psum = True
        GH = Rot(TB.bufs)

        def tbv(b):
            return b.ap[:, :].bitcast(BF16).rearrange("p (c t) -> p c t", t=128)

        GMIX, GQ, GFFN, GPLE, GOUT, CB, LNG, LNB, CW = 0, 8, 11, 19, 27, 35, 39, 43, 47

        def ppc(c):
            return PP.ap[:, c:c + 1]

        fw.dma("sp", PP.ap[:, :], pp_in, [], [PP], PP)
        fw.dma("sp", BC.ap[:, :], bc_in, [], [BC], BC)
        I("pool", "memset", [], [IDF], IDF.ap[:, :], 0.0)
        I("pool", "affine_select", [IDF], [IDF], out=IDF.ap[:, :], in_=IDF.ap[:, :], pattern=[[-1, 128]],
          compare_op=ALU.not_equal, fill=1.0, base=0, channel_multiplier=1)
        I("dve", "tensor_copy", [IDF], [IDB], out=IDB.ap[:, :], in_=IDF.ap[:, :])
        I("pool", "memset", [], [ONB], ONB.ap[:, :], 1.0)
        I("pool", "memset", [], [ONF], ONF.ap[:, :], 1.0)
        I("pool", "memset", [], [VN], VN.ap[:, :, :, :], 1.0)
        for kb in KRB.bufs:
            I("pool", "memset", [], [kb], kb.ap[:, :], 0.0)
        fw.dma("pool", WUK.ap[:, :, :], w_uk.rearrange("(c p) n -> p c n", p=128), [], [WUK], WUK)
        fw.dma("pool", WUV.ap[:, :, :], w_uv.rearrange("(c p) n -> p c n", p=128), [], [WUV], WUV)
        fw.dma("pool", WP.ap[:, :, :], w_pp.rearrange("(c p) n -> p c n", p=128), [], [WP], WP)
        I("pool", "tensor_scalar", [WP], [WP], out=WP.ap[:, :, :], in0=WP.ap[:, :, :], scalar1=0.5, scalar2=None, op0=ALU.mult)
        XFLAT = XT[:, :, :].rearrange("p s d -> p (s d)")
        STG = Rot([Buf("STG%d" % i, XFLAT[:, 2048 * i:2048 * (i + 1)]) for i in range(2)])

        def fold_resident(src, ncols, nchunk, gcol, dstbuf):
            for c in range(nchunk):
                for c0 in range(0, ncols, 2048):
                    w = min(2048, ncols - c0)
                    sg = STG.next()
                    fw.dma("sp", sg.ap[:, 0:w], src[c * 128:(c + 1) * 128, c0:c0 + w], [], [sg], sg)
                    I("dve", "tensor_scalar", [sg, PP], [dstbuf], out=dstbuf.ap[:, c, c0:c0 + w], in0=sg.ap[:, 0:w],
                      scalar1=ppc(gcol + c), scalar2=None, op0=ALU.mult)

        PP2 = fw.sbuf("PP2", [128, 8], F32)
        I("pool", "tensor_scalar", [PP], [PP2], out=PP2.ap[:, 0:8], in0=PP.ap[:, LNG:LNG + 8], scalar1=0.5, scalar2=None, op0=ALU.mult)
        fold_resident(w_in[:, 0:704], 704, 8, GMIX, WIN)
        fold_resident(w_uq, WUQ_COLS, 3, GQ, WUQ)

        def fold_scratch(src, dst, ncols, nchunk, gcol):
            for c in range(nchunk):
                for c0 in range(0, ncols, 2048):
                    w = min(2048, ncols - c0)
                    sg = STG.next()
                    sb = WPC.next()
                    fw.dma("sp", sg.ap[:, 0:w], src[c * 128:(c + 1) * 128, c0:c0 + w], [], [sg], sg)
                    I("dve", "tensor_scalar", [sg, PP], [sb], out=sb.ap[:, 0:w], in0=sg.ap[:, 0:w],
                      scalar1=ppc(gcol + c), scalar2=None, op0=ALU.mult)
                    fw.dma("sp", dst[c * 128:(c + 1) * 128, c0:c0 + w], sb.ap[:, 0:w], [sb], [], sb)

        wds_v = wds.rearrange("(j p) n -> p j n", p=128)

        def fold_all_scratch():
            for c in range(8):
                sg_ = STG.next()
                sb_ = WPC.next()
                fw.dma("sp", sg_.ap[:, 0:1024], w_in[c * 128:(c + 1) * 128, 704:WIN_COLS], [], [sg_], sg_)
                I("dve", "tensor_scalar", [sg_, PP], [sb_], out=sb_.ap[:, 0:1024], in0=sg_.ap[:, 0:1024],
                  scalar1=ppc(GMIX + c), scalar2=None, op0=ALU.mult)
                hv = sb_.ap[:, 0:1024].rearrange("p (k n) -> p k n", n=256)[:, :, 0:128]
                I("dve", "tensor_scalar", [sb_], [sb_], out=hv, in0=hv, scalar1=0.5, scalar2=None, op0=ALU.mult)
                fw.dma("sp", wcs[c * 128:(c + 1) * 128, :], sb_.ap[:, 0:1024], [sb_], [], sb_)
            fold_scratch(w_out, wos, D, 8, GOUT)
            fold_scratch(w_up, wus, DFF, 8, GFFN)
            fold_scratch(w_gate, wgs, D, 8, GPLE)
            wd_v = w_down.rearrange("(j p) n -> p j n", p=128)
            wds_v = wds.rearrange("(j p) n -> p j n", p=128)
            for j0 in range(0, 32, 4):
                sb = WPC.next()
                fw.dma("pool", sb.ap[:, :].rearrange("p (j n) -> p j n", n=D), wd_v[:, j0:j0 + 4, :], [], [sb], sb)
                fw.dma("sp", wds_v[:, j0:j0 + 4, :], sb.ap[:, :].rearrange("p (j n) -> p j n", n=D), [sb], [], sb)
            fw.barrier()


        cnt = {"ev": 0}

        def evac_eng():
            cnt["ev"] += 1
            return "act" if cnt["ev"] % 2 else "dve"

        def copy_on(eng, reads, writes, out, in_):
            if eng == "act":
                I("act", "copy", reads, writes, out=out, in_=in_)
            else:
                I(eng, "tensor_copy", reads, writes, out=out, in_=in_)

        def rstd(stb, TS, eps):
            I("act", "activation", [stb], [stb], out=stb.ap[:TS, 1:2], in_=stb.ap[:TS, 0:1], func=AF.Ln, bias=eps)
            I("act", "activation", [stb], [stb], out=stb.ap[:TS, 1:2], in_=stb.ap[:TS, 1:2], func=AF.Exp, scale=-0.5)

        def _mscols(ms, TS, k):
            g = int(ms[0].name[2:].split("_")[0])
            return STT[:TS, 32 + 8 * g:32 + 8 * g + k]

        def group_stats(srcs, TS, eps):
            ms, rs = SGRP.next()
            k = len(srcs)
            for j, (ap_, bufs_, n_, jap_, jbuf_) in enumerate(srcs):
                I("act", "activation", bufs_, [ms[j], jbuf_], out=jap_, in_=ap_, func=AF.Square,
                  scale=float(n_) ** -0.5, accum_out=ms[j].ap[:TS, 0:1])
            I("act", "activation", ms[:k], [rs], out=rs.ap[:TS, 0:k], in_=_mscols(ms, TS, k),
              func=AF.Ln, bias=eps)
            I("act", "activation", [rs], [rs], out=rs.ap[:TS, 0:k], in_=rs.ap[:TS, 0:k], func=AF.Exp, scale=-0.5)
            return rs

        def norm_a(srcs, subs, TS):
            xss = [XS.next() for _ in subs]
            rs = group_stats([(srcs[j].ap[:TS, :], [srcs[j]], D, xss[j].ap[:TS, :], xss[j]) for j in range(len(subs))], TS, 1e-6)
            for j, s in enumerate(subs):
                I("dve", "tensor_scalar", [srcs[j], rs], [xss[j]], out=xss[j].ap[:TS, :], in0=srcs[j].ap[:TS, :],
                  scalar1=rs.ap[:TS, j:j + 1], scalar2=None, op0=ALU.mult)
            return xss

        def norm_b(xss, subs, TS, dst_bufs, dst_tile):
            for j, s in enumerate(subs):
                xs = xss[j]
                tb = TB.next()
                tv = tbv(tb)
                for c in range(8):
                    I("pe", "transpose", [xs, IDB], [tb], out=tv[:, c, 0:TS], in_=xs.ap[:TS, c * 128:(c + 1) * 128],
                      identity=IDB.ap[:TS, :TS])
                copy_on(evac_eng(), [tb], [dst_bufs[j]], dst_tile[:, :, s * TS:(s + 1) * TS], tv[:, :, 0:TS])

        def norm_transpose(srcs, subs, TS, dst):
            for i0 in range(0, len(subs), 2):
                sub2 = subs[i0:i0 + 2]
                xss = norm_a(srcs[i0:i0 + 2], sub2, TS)
                norm_b(xss, sub2, TS, [dst] * len(sub2), dst.ap)

        def bcast_rstd(ps, T, n, eps):
            rs = STAT.next()
            I("act", "activation", [ps], [rs], out=rs.ap[:, 0:T], in_=ps.ap[:, 0:T], func=AF.Ln, bias=eps, scale=1.0 / n)
            I("act", "activation", [rs], [rs], out=rs.ap[:, 0:T], in_=rs.ap[:, 0:T], func=AF.Exp, scale=-0.5)
            return rs

        class Seq:
            pass

        def transposes_small(ckb, krb, cqb, s, TS):
            tb = TB.next()
            tv = tbv(tb)
            for c in range(2):
                I("pe", "transpose", [ckb, IDB], [tb], out=tv[:, c, 0:TS], in_=ckb.ap[:TS, c * 128:(c + 1) * 128],
                  identity=IDB.ap[:TS, :TS])
            if cqb is not None:
                for c in range(3):
                    I("pe", "transpose", [cqb, IDB], [tb], out=tv[:, 2 + c, 0:TS], in_=cqb.ap[:TS, c * 128:(c + 1) * 128],
                      identity=IDB.ap[:TS, :TS])
            if KSUB >= 9:
                I("pe", "transpose", [krb, IDB], [tb], out=tv[0:96, 5, 0:TS], in_=krb.ap[:TS, 0:96], identity=IDB.ap[:TS, :TS])
            if KSUB >= 8 or KSUB == 5:
                I("act", "copy", [tb], [CKVT], out=CKVT.ap[:, :, s * TS:(s + 1) * TS], in_=tv[:, 0:2, 0:TS])
            if cqb is not None and (KSUB >= 8 or KSUB == 6):
                I("dve", "tensor_copy", [tb], [CQT], out=CQT.ap[:, :, s * TS:(s + 1) * TS], in_=tv[:, 2:5, 0:TS])
            I("dve", "tensor_copy", [tb], [KRT], out=KRT.ap[64:96, s * TS:(s + 1) * TS], in_=tv[64:96, 5, 0:TS])

        def kv_expand(sq, T, nsub, TS, key0):
            for j in range(4):
                ps = G.next()
                for c in range(2):
                    I("pe", "matmul", [WUK, CKVT], [ps], ps.ap[:, 0:T], lhsT=WUK.ap[:, c, j * 128:(j + 1) * 128],
                      rhs=CKVT.ap[:, c, 0:T], start=(c == 0), stop=(c == 1))
                I("act", "copy", [ps], AT, out=KTNT[0:64, 2 * j, 0:T], in_=ps.ap[0:64, 0:T])
                I("dve", "tensor_copy", [ps], AT, out=KTNT[0:64, 2 * j + 1, 0:T], in_=ps.ap[64:128, 0:T])
            I("pool", "tensor_copy", [KRT], AT, out=KTNT[64:96, :, 0:T],
              in_=KRT.ap[64:96, 0:T].unsqueeze(1).to_broadcast([32, NH, T]))
            for s in range(nsub):
                ps = G.next()
                for c in range(2):
                    I("pe", "matmul", [WUV, CKVT], [ps], ps.ap[:TS, :], lhsT=CKVT.ap[:, c, s * TS:(s + 1) * TS],
                      rhs=WUV.ap[:, c, :], start=(c == 0), stop=(c == 1))
                copy_on(evac_eng(), [ps], [VN], VN.ap[:TS, :, s, 0:64], ps.ap[:TS, :].rearrange("p (h d) -> p h d", d=64))
            ti = key0 // 512
            fw.dma("sp", sq.kts[:, :, key0:key0 + T].rearrange("h r c -> r h c"), KTNT[0:96, :, 0:T], AT, [sq.KS[ti]], AT[0])
            kt0 = key0 // 128
            fw.dma("sp", sq.vs[:, 0:TS, kt0:kt0 + nsub, :].rearrange("h p k c -> p h k c"), VN.ap[:TS, :, 0:nsub, :],
                   [VN], [sq.VSB[ti]], VN)

        def wview(src, n0, n1):
            return src.rearrange("(c p) n -> p c n", p=128)[:, :, n0:n1]

        def conv_w_load(k):
            wc = WPC.next()
            wcv = wc.ap[:, 0:2048].rearrange("p (c n) -> p c n", n=256)
            fw.dma("sp", wcv, wview(wcs, k * 256, (k + 1) * 256), [], [wc], wc)
            return (wc, wcv)

        def conv_chunk(k, T, pre_w):
            wc, wcv = pre_w
            psa = GH.next()
            psg = GH.next()
            for c in range(8):
                I("pe", "matmul", [wc] + HTP, [psa], psa.ap[:, 0:T], lhsT=wcv[:, c, 0:128],
                  rhs=HTPT[:, c, 0:T], start=(c == 0), stop=(c == 7))
            for c in range(8):
                I("pe", "matmul", [wc] + HTP, [psg], psg.ap[:, 0:T], lhsT=wcv[:, c, 128:256],
                  rhs=HTPT[:, c, 0:T], start=(c == 0), stop=(c == 7))
            sig = TMP.next()
            I("act", "activation", [psg], [sig], out=sig.ap[:, 0:T], in_=psg.ap[:, 0:T], func=AF.Tanh, scale=0.5)
            uc = UC.next()
            I("pool", "tensor_copy", [UT], [uc], out=uc.ap[:, 0:CK - 1], in_=UT.ap[:, k, :])
            I("dve", "scalar_tensor_tensor", [psa, sig, uc], [uc], out=uc.ap[:, CK - 1:CK - 1 + T], in0=sig.ap[:, 0:T],
              scalar=1.0, in1=psa.ap[:, 0:T], op0=ALU.add, op1=ALU.mult)
            I("pool", "tensor_copy", [uc, UT], [UT], out=UT.ap[:, k, :], in_=uc.ap[:, T:T + CK - 1])
            a = ACC[k]
            I("dve", "tensor_scalar", [uc, PP], [a], out=a.ap[:, 0:T], in0=uc.ap[:, 0:T], scalar1=ppc(CW + k * CK),
              scalar2=ppc(CB + k), op0=ALU.mult, op1=ALU.add)

            def taps(j0, j1):
                def f():
                    for j in range(j0, j1):
                        I("dve", "scalar_tensor_tensor", [uc, PP, a], [a], out=a.ap[:, 0:T], in0=uc.ap[:, j:j + T],
                          scalar=ppc(CW + k * CK + j), in1=a.ap[:, 0:T], op0=ALU.mult, op1=ALU.add)
                return f
            return [taps(1, 8), taps(8, 16), taps(16, 24), taps(24, CK)]

        def conv_post(T):
            s1 = G.next()
            s2 = G.next()
            for k in range(4):
                sq = SQB.next()
                I("act", "activation", [ACC[k]], [sq], out=sq.ap[:, 0:T], in_=ACC[k].ap[:, 0:T], func=AF.Square)
                I("pe", "matmul", [ONF, ACC[k]], [s1], s1.ap[:, 0:T], lhsT=ONF.ap[:, :], rhs=ACC[k].ap[:, 0:T],
                  start=(k == 0), stop=(k == 3))
                I("pe", "matmul", [ONB, sq], [s2], s2.ap[:, 0:T], lhsT=ONB.ap[:, :], rhs=sq.ap[:, 0:T],
                  start=(k == 0), stop=(k == 3))
            mu = STAT.next()
            I("act", "activation", [s1], [mu], out=mu.ap[:, 0:T], in_=s1.ap[:, 0:T], func=AF.Copy, scale=1.0 / CCH)
            musq = TMP.next()
            I("pool", "tensor_tensor", [mu], [musq], out=musq.ap[:, 0:T], in0=mu.ap[:, 0:T], in1=mu.ap[:, 0:T], op=ALU.mult)
            var = TMP.next()
            I("dve", "scalar_tensor_tensor", [s2, musq], [var], out=var.ap[:, 0:T], in0=s2.ap[:, 0:T], scalar=1.0 / CCH,
              in1=musq.ap[:, 0:T], op0=ALU.mult, op1=ALU.subtract)
            rs = STAT.next()
            I("act", "activation", [var], [rs], out=rs.ap[:, 0:T], in_=var.ap[:, 0:T], func=AF.Ln, bias=1e-5)
            I("act", "activation", [rs], [rs], out=rs.ap[:, 0:T], in_=rs.ap[:, 0:T], func=AF.Exp, scale=-0.5)
            for k in range(4):
                a = ACC[k]
                I("dve", "tensor_tensor", [a, mu], [a], out=a.ap[:, 0:T], in0=a.ap[:, 0:T], in1=mu.ap[:, 0:T], op=ALU.subtract)
                I("pool", "tensor_tensor", [a, rs], [a], out=a.ap[:, 0:T], in0=a.ap[:, 0:T], in1=rs.ap[:, 0:T], op=ALU.mult)
                th = TMP.next()
                I("act", "activation", [a, PP2], [th], out=th.ap[:, 0:T], in_=a.ap[:, 0:T], func=AF.Tanh,
                  scale=PP2.ap[:, k:k + 1], bias=PP2.ap[:, 4 + k:5 + k])
                I("dve", "tensor_scalar", [a, PP2], [a], out=a.ap[:, 0:T], in0=a.ap[:, 0:T], scalar1=PP2.ap[:, k:k + 1],
                  scalar2=PP2.ap[:, 4 + k:5 + k], op0=ALU.mult, op1=ALU.add)
                I("dve", "scalar_tensor_tensor", [th, a], [a], out=a.ap[:, 0:T], in0=th.ap[:, 0:T], scalar=1.0,
                  in1=a.ap[:, 0:T], op0=ALU.add, op1=ALU.mult)
            s3 = G.next()
            for k in range(4):
                sq = SQB.next()
                I("act", "activation", [ACC[k]], [sq], out=sq.ap[:, 0:T], in_=ACC[k].ap[:, 0:T], func=AF.Square)
                I("pe", "matmul", [ONB, sq], [s3], s3.ap[:, 0:T], lhsT=ONB.ap[:, :], rhs=sq.ap[:, 0:T],
                  start=(k == 0), stop=(k == 3))
            rs2 = bcast_rstd(s3, T, CCH, 1e-6)
            for k in range(4):
                I("dve", "tensor_tensor", [ACC[k], rs2], [YN[k]], out=YN[k].ap[:, 0:T], in0=ACC[k].ap[:, 0:T],
                  in1=rs2.ap[:, 0:T], op=ALU.mult)

        def q_head(h, T):
            psa = G.next()
            psb_ = G.next()
            for c in range(3):
                I("pe", "matmul", [WUQ, CQT], [psa], psa.ap[0:96, 0:T], lhsT=WUQ.ap[:, c, h * 192:h * 192 + 96],
                  rhs=CQT.ap[:, c, 0:T], start=(c == 0), stop=(c == 2))
            for c in range(3):
                I("pe", "matmul", [WUQ, CQT], [psb_], psb_.ap[0:96, 0:T], lhsT=WUQ.ap[:, c, h * 192 + 96:h * 192 + 192],
                  rhs=CQT.ap[:, c, 0:T], start=(c == 0), stop=(c == 2))
            I("act", "copy", [psa], [QT[h]], out=QT[h].ap[0:64, 0:T], in_=psa.ap[0:64, 0:T])
            t1 = TMP.next()
            t2 = TMP.next()
            I("dve", "tensor_tensor", [psa, CST], [t1], out=t1.ap[64:96, 0:T], in0=psa.ap[64:96, 0:T],
              in1=CST.ap[64:96, 0, 0:T], op=ALU.mult)
            I("dve", "tensor_tensor", [psb_, CST], [t2], out=t2.ap[64:96, 0:T], in0=psb_.ap[64:96, 0:T],
              in1=CST.ap[64:96, 1, 0:T], op=ALU.mult)
            I("pool" if h % 2 == 0 else "dve", "tensor_tensor", [t1, t2, QT[h]], [QT[h]], out=QT[h].ap[64:96, 0:T],
              in0=t1.ap[64:96, 0:T], in1=t2.ap[64:96, 0:T], op=ALU.add)

        def kv_load(sq, h, blk):
            kh = KH.next()
            vh = VH.next()
            kt0 = blk[0][0]
            nkeys = sum(t[1] for t in blk)
            tis = sorted(set(t[0] // 4 for t in blk))
            fw.dma("sp", kh.ap[0:96, 0:nkeys], sq.kts[h, :, kt0 * 128:kt0 * 128 + nkeys], [sq.KS[j] for j in tis], [kh], kh)
            pv = min(t[1] for t in blk)
            assert pv == 128 or all(t[1] == pv for t in blk)
            fw.dma("sp", vh.ap[0:pv, 0:len(blk), :], sq.vs[h, 0:pv, kt0:kt0 + len(blk), :], [sq.VSB[j] for j in tis], [vh], vh)
            return (kh, vh)

        def attn_head(sq, h, T, ktl, hooks, pre, prefetch):
            o = OB.next()
            nk_tiles = len(ktl)
            blocks = [ktl[i:i + 8] for i in range(0, nk_tiles, 8)]
            loaded = [pre]
            pf = {"done": False}

            def load(bi):
                loaded.append(kv_load(sq, h, blocks[bi]))
            flat = []
            for bi, blk in enumerate(blocks):
                for li, t in enumerate(blk):
                    flat.append((bi, li, t))
            units = []
            i_ = 0
            while i_ < len(flat):
                t_ = flat[i_][2]
                if (T == 512 and i_ + 1 < len(flat) and t_[1] == 128 and t_[2] == 0 and not t_[3]
                        and flat[i_ + 1][2][1] == 128 and flat[i_ + 1][2][2] == 0 and not flat[i_ + 1][2][3]):
                    units.append([flat[i_], flat[i_ + 1]])
                    i_ += 2
                else:
                    units.append([flat[i_]])
                    i_ += 1
            seen_blocks = set([0])

            def smm(unit):
                if len(unit) == 2 and G.i % 2:
                    G.i += 1
                banks = []
                for n_, item in enumerate(unit):
                    bi, li, (kt, nk, c0, masked) = item
                    if bi + 1 < len(blocks) and (bi + 1) not in seen_blocks:
                        seen_blocks.add(bi + 1)
                        load(bi + 1)
                    if bi == len(blocks) - 1 and not pf["done"]:
                        pf["done"] = True
                        prefetch()
                    kh, vh = loaded[bi]
                    ps = G.next()
                    I("pe", "matmul", [kh, QT[h]], [ps], ps.ap[0:nk, c0:T], lhsT=kh.ap[0:96, li * 128:li * 128 + nk],
                      rhs=QT[h].ap[0:96, c0:T], start=True, stop=True)
                    banks.append(ps)
                return banks

            DEPTH = 2
            pss = [smm(units[i_]) for i_ in range(min(DEPTH, len(units)))]
            done = 0
            for ui, unit in enumerate(units):
                banks = pss.pop(0)
                pts = PTS.next()
                if len(unit) == 2:
                    gi = int(banks[0].name[1:])
                    if os.environ.get("KSPLIT"):
                        for n_ in range(2):
                            I("act", "activation", banks, [pts], out=pts.ap[:, n_ * T:(n_ + 1) * T], in_=banks[n_].ap[:, 0:T],
                              func=AF.Exp, scale=SC_ATT)
                    else:
                        I("act", "activation", banks, [pts], out=pts.ap[:, 0:2 * T].rearrange("p (b t) -> p b t", t=T),
                          in_=psG[:, gi * 512:gi * 512 + 2 * T].rearrange("p (b t) -> p b t", t=T),
                          func=AF.Exp, scale=SC_ATT)
                else:
                    bi, li, (kt, nk, c0, masked) = unit[0]
                    I("act", "activation", banks, [pts], out=pts.ap[0:nk, c0:T], in_=banks[0].ap[0:nk, c0:T], func=AF.Exp, scale=SC_ATT)
                if ui + DEPTH < len(units):
                    pss.append(smm(units[ui + DEPTH]))
                for n_, item in enumerate(unit):
                    bi, li, (kt, nk, c0, masked) = item
                    kh, vh = loaded[bi]
                    if masked and done == 0:
                        I("pool", "memset", [pts], [pts], pts.ap[64:128, c0:c0 + 64], 0.0)
                        I("pe", "matmul", [vh, pts], [o], o.ap[0:65, c0:T], lhsT=vh.ap[0:nk, li, 0:65],
                          rhs=pts.ap[0:nk, n_ * T + c0:n_ * T + T], start=True, stop=(done == len(flat) - 1))
                    elif masked:
                        I("pe", "matmul", [vh, pts], [o], o.ap[0:65, c0 + 64:T], lhsT=vh.ap[0:128, li, 0:65],
                          rhs=pts.ap[0:128, c0 + 64:T], start=False, stop=False)
                        I("pe", "matmul", [vh, pts], [o], o.ap[0:65, c0:c0 + 64], lhsT=vh.ap[0:64, li, 0:65],
                          rhs=pts.ap[0:64, c0:c0 + 64], start=False, stop=(done == len(flat) - 1))
                    else:
                        I("pe", "matmul", [vh, pts], [o], o.ap[0:65, c0:T], lhsT=vh.ap[0:nk, li, 0:65],
                          rhs=pts.ap[0:nk, n_ * T + c0:n_ * T + T], start=(done == 0), stop=(done == len(flat) - 1))
                    done += 1
                for key in sorted(k_ for k_ in list(hooks) if k_ <= done - 1 or done == len(flat)):
                    hooks.pop(key)()
            return o

        def attn_fin_a(o, T):
            I("dve", "reciprocal", [o], [RL], out=RL.ap[64:65, 0:T], in_=o.ap[64:65, 0:T])
            I("dve", "tensor_copy", [RL], [RLH], out=RLH.ap[64:65, 0:T], in_=RL.ap[64:65, 0:T])
            I("dve", "tensor_tensor", [RL, RLH], [RLL], out=RLL.ap[64:65, 0:T], in0=RL.ap[64:65, 0:T],
              in1=RLH.ap[64:65, 0:T], op=ALU.subtract)

        def attn_fin_b(o, h, T):
            bc = GH.next()
            I("pe", "matmul", [ONB, RLH], [bc], bc.ap[0:64, 0:T], lhsT=ONB.ap[64:65, 0:64], rhs=RLH.ap[64:65, 0:T],
              start=True, stop=False)
            I("pe", "matmul", [ONB, RLL], [bc], bc.ap[0:64, 0:T], lhsT=ONB.ap[64:65, 0:64], rhs=RLL.ap[64:65, 0:T],
              start=False, stop=True)
            bcs = TMP.next()
            I("dve", "tensor_copy", [bc], [bcs], out=bcs.ap[0:64, 0:T], in_=bc.ap[0:64, 0:T])
            j, half = h // 2, (h % 2) * 64
            I("dve", "tensor_tensor", [o, bcs, ATT[j]], [ATT[j]], out=ATT[j].ap[half:half + 64, 0:T], in0=o.ap[0:64, 0:T],
              in1=bcs.ap[0:64, 0:T], op=ALU.mult)

        def attn_post(T):
            s4 = G.next()
            for j in range(4):
                sq = SQB.next()
                I("act", "activation", [ATT[j]], [sq], out=sq.ap[:, 0:T], in_=ATT[j].ap[:, 0:T], func=AF.Square)
                I("pe", "matmul", [ONB, sq], [s4], s4.ap[:, 0:T], lhsT=ONB.ap[:, :], rhs=sq.ap[:, 0:T],
                  start=(j == 0), stop=(j == 3))
            rsa = bcast_rstd(s4, T, 512, 1e-6)
            for j in range(4):
                I("dve", "tensor_tensor", [ATT[j], rsa], [ATT[j]], out=ATT[j].ap[:, 0:T], in0=ATT[j].ap[:, 0:T],
                  in1=rsa.ap[:, 0:T], op=ALU.mult)

        def x_load(nsub, TS, x_src, t0):
            for s in range(nsub):
                fw.dma("sp", X[s].ap[:TS, :], x_src[t0 + s * TS:t0 + (s + 1) * TS, :], [], [X[s]], X[s])

        def out_proj(T, nsub, TS, after=None):
            MIX = ATT + YN
            wps = []
            for n in range(2):
                wp = WPC.next()
                wv = wp.ap[:, :].rearrange("p (c n) -> p c n", n=512)
                fw.dma("sp", wv, wview(wos, n * 512, (n + 1) * 512), [], [wp], wp)
                wps.append((wp, wv))
            for s in range(nsub):
                for n in range(2):
                    wp, wv = wps[n]
                    ps = G.next()
                    for k in range(8):
                        I("pe", "matmul", [MIX[k], wp], [ps], ps.ap[:TS, :], lhsT=MIX[k].ap[:, s * TS:(s + 1) * TS],
                          rhs=wv[:, k, :], start=(k == 0), stop=(k == 7))
                    I("dve", "tensor_tensor", [ps, X[s]], [X[s]], out=X[s].ap[:TS, n * 512:(n + 1) * 512], in0=ps.ap[:TS, :],
                      in1=X[s].ap[:TS, n * 512:(n + 1) * 512], op=ALU.add)
                if after and s in after:
                    after[s]()

        def ffn(T, nsub, TS, slots=None, prenorm=None):
            slots = dict(slots or {})

            def run_slot(k):
                for f_ in slots.pop(k, []):
                    f_()
            if prenorm is None:
                norm_transpose(X[:nsub], list(range(nsub)), TS, HT)
            else:
                norm_b(prenorm, [2, 3], TS, [HT, HT], HT.ap)
            st_ = {}

            def up(g):
                wu = WPC.next()
                wuv = wu.ap[:, :].rearrange("p (c n) -> p c n", n=512)
                fw.dma("sp", wuv, wview(wus, g * 512, (g + 1) * 512), [], [wu], wu)
                wd = WPC.next()
                wdv = wd.ap[:, :].rearrange("p (j n) -> p j n", n=D)
                fw.dma("sp", wdv, wds_v[:, g * 4:(g + 1) * 4, :], [], [wd], wd)
                at = ATR.next()
                for jj in range(4):
                    ps = G.next()
                    for c in range(8):
                        I("pe", "matmul", [wu, HT], [ps], ps.ap[:, 0:T], lhsT=wuv[:, c, jj * 128:(jj + 1) * 128],
                          rhs=HT.ap[:, c, 0:T], start=(c == 0), stop=(c == 7))
                    r = TMP.next()
                    I("act", "activation", [ps], [r], out=r.ap[:, 0:T], in_=ps.ap[:, 0:T], func=AF.Relu)
                    I("dve", "tensor_tensor", [ps, r, at], [at], out=at.ap[:, jj, 0:T], in0=ps.ap[:, 0:T], in1=r.ap[:, 0:T],
                      op=ALU.mult)
                st_[g] = (at, wd, wdv)

            def down(g):
                at, wd, wdv = st_.pop(g)
                for n in range(2):
                    for s in range(nsub):
                        ps = G.next()
                        for jj in range(4):
                            I("pe", "matmul", [at, wd], [ps], ps.ap[:TS, :], lhsT=at.ap[:, jj, s * TS:(s + 1) * TS],
                              rhs=wdv[:, jj, n * 512:(n + 1) * 512], start=(jj == 0), stop=(jj == 3))
                        I("dve", "tensor_tensor", [ps, X[s]], [X[s]], out=X[s].ap[:TS, n * 512:(n + 1) * 512],
                          in0=ps.ap[:TS, :], in1=X[s].ap[:TS, n * 512:(n + 1) * 512], op=ALU.add)

            up(0)
            for g in range(8):
                run_slot(2 * g)
                if g + 1 < 8:
                    up(g + 1)
                run_slot(2 * g + 1)
                down(g)
            for k in sorted(slots):
                run_slot(k)

        def ple_pieces(T, nsub, TS, p_src, t0):
            subs = list(range(nsub))
            prs = [subs[i0:i0 + 2] for i0 in range(0, nsub, 2)]
            st_ = {}

            def p0():
                st_["xs0"] = norm_a([X[s] for s in prs[0]], prs[0], TS)
                fw.dma("pool", PB.ap[:TS, 0:nsub, :], p_src[t0:t0 + T, :].rearrange("(s p) d -> p s d", p=TS), [], [PB], PB)

            def p1():
                norm_b(st_.pop("xs0"), prs[0], TS, [HT] * len(prs[0]), HT.ap)
                if len(prs) > 1:
                    st_["xs1"] = norm_a([X[s] for s in prs[1]], prs[1], TS)
                for s in range(nsub):
                    tb = TB.next()
                    tv = tbv(tb)
                    for c in range(2):
                        I("pe", "transpose", [PB, IDB], [tb], out=tv[:, c, 0:TS], in_=PB.ap[:TS, s, c * 128:(c + 1) * 128],
                          identity=IDB.ap[:TS, :TS])
                    copy_on(evac_eng(), [tb], [PTR], PTR.ap[:, :, s * TS:(s + 1) * TS], tv[:, 0:2, 0:TS])
                wgl = []
                for n in range(2):
                    wg = WPC.next()
                    wgv = wg.ap[:, :].rearrange("p (c n) -> p c n", n=512)
                    fw.dma("sp", wgv, wview(wgs, n * 512, (n + 1) * 512), [], [wg], wg)
                    wgl.append((wg, wgv))
                st_["wg"] = wgl

            def p2():
                if len(prs) > 1:
                    norm_b(st_.pop("xs1"), prs[1], TS, [HT] * len(prs[1]), HT.ap)
                for n in range(2):
                    wg, wgv = st_["wg"][n]
                    for s in range(nsub):
                        ps = G.next()
                        for c in range(8):
                            I("pe", "matmul", [HT, wg], [ps], ps.ap[:TS, :], lhsT=HT.ap[:, c, s * TS:(s + 1) * TS],
                              rhs=wgv[:, c, :], start=(c == 0), stop=(c == 7))
                        sg = TMP.next()
                        I("act", "activation", [ps], [sg], out=sg.ap[:TS, :], in_=ps.ap[:TS, :], func=AF.Tanh, scale=0.5)
                        pp_ = G.next()
                        for c in range(2):
                            I("pe", "matmul", [PTR, WP], [pp_], pp_.ap[:TS, :], lhsT=PTR.ap[:, c, s * TS:(s + 1) * TS],
                              rhs=WP.ap[:, c, n * 512:(n + 1) * 512], start=(c == 0), stop=(c == 1))
                        tm = TMP.next()
                        I("dve", "scalar_tensor_tensor", [pp_, sg], [tm], out=tm.ap[:TS, :], in0=sg.ap[:TS, :], scalar=1.0,
                          in1=pp_.ap[:TS, :], op0=ALU.add, op1=ALU.mult)
                        I("pool", "tensor_tensor", [X[s], tm], [X[s]], out=X[s].ap[:TS, n * 512:(n + 1) * 512],
                          in0=X[s].ap[:TS, n * 512:(n + 1) * 512], in1=tm.ap[:TS, :], op=ALU.add)
            return [p0, p1, p2]

        def ple(T, nsub, TS, p_src, t0):
            for f_ in ple_pieces(T, nsub, TS, p_src, t0):
                f_()

        def final_store(T, nsub, TS, y_dst, t0):
            ybs, yvs = [], []
            for s in range(nsub):
                if s % 2 == 0:
                    yb = WPC.next()
                ybs.append(yb)
                yvs.append(yb.ap[:, :].bitcast(F32)[:TS, (s % 2) * D:(s % 2 + 1) * D])
            rs = group_stats([(X[s].ap[:TS, :], [X[s]], D, yvs[s], ybs[s]) for s in range(nsub)], TS, 1e-6)
            for s in range(nsub):
                I("dve", "scalar_tensor_tensor", [X[s], rs, BC], [ybs[s]], out=yvs[s], in0=X[s].ap[:TS, :],
                  scalar=rs.ap[:TS, s:s + 1], in1=BC.ap[:TS, 256:256 + D], op0=ALU.mult, op1=ALU.mult)
                fw.dma("pool", y_dst[t0 + s * TS:t0 + (s + 1) * TS, :], yvs[s], [ybs[s]], [], ybs[s])

        def front_slots(T, nsub, TS, t0, x_src, cs_src, cst_src, ckv_dst, kr_dst):
            st_ = {}

            def n_a(subs, first):
                def f():
                    if first:
                        fw.dma("sp", CS.ap[:TS, 0:nsub, :], cs_src[t0:t0 + T, :].rearrange("(s p) d -> p s d", p=TS), [], [CS], CS)
                        fw.dma("sp", CST.ap[:, :, 0:T], cst_src[:, :, t0:t0 + T], [], [CST], CST)
                    xfs = []
                    for s in subs:
                        xf = XF.next()
                        fw.dma("sp", xf.ap[:TS, :], x_src[t0 + s * TS:t0 + (s + 1) * TS, :], [], [xf], xf)
                        xfs.append(xf)
                    st_[("xs", tuple(subs))] = norm_a(xfs, subs, TS)
                return f

            def n_b(subs):
                def f():
                    norm_b(st_.pop(("xs", tuple(subs))), subs, TS, [HTP[s] for s in subs], HTPT)
                return f

            def s_a(s):
                def f():
                    psA = G.next()
                    psB = G.next()
                    for c in range(8):
                        I("pe", "matmul", [HTP[s], WIN], [psA], psA.ap[:TS, 0:320], lhsT=HTPT[:, c, s * TS:(s + 1) * TS],
                          rhs=WIN.ap[:, c, 0:320], start=(c == 0), stop=(c == 7))
                    for c in range(8):
                        I("pe", "matmul", [HTP[s], WIN], [psB], psB.ap[:TS, 0:384], lhsT=HTPT[:, c, s * TS:(s + 1) * TS],
                          rhs=WIN.ap[:, c, 320:704], start=(c == 0), stop=(c == 7))
                    st_[("ps", s)] = (psA, psB)
                return f

            def s_b(s):
                def f():
                    psA, psB = st_.pop(("ps", s))
                    cko = CKVO.next()
                    cqb = CQB.next()
                    rs = group_stats([(psA.ap[:TS, 0:KVL], [psA], KVL, cko.ap[:TS, :], cko),
                                      (psB.ap[:TS, 0:QLORA], [psB], QLORA, cqb.ap[:TS, :], cqb)], TS, 1e-6)
                    I("dve", "scalar_tensor_tensor", [psA, rs, BC], [cko], out=cko.ap[:TS, :], in0=psA.ap[:TS, 0:KVL],
                      scalar=rs.ap[:TS, 0:1], in1=BC.ap[:TS, 0:KVL], op0=ALU.mult, op1=ALU.mult)
                    fw.dma("pool", ckv_dst[t0 + s * TS:t0 + (s + 1) * TS, :], cko.ap[:TS, :], [cko], [], cko)
                    ckb = CKVB.next()
                    I("pool", "tensor_copy", [cko], [ckb], out=ckb.ap[:TS, :], in_=cko.ap[:TS, :])
                    t1 = SMALL.next()
                    t2 = SMALL.next()
                    I("dve", "tensor_tensor", [psA, CS], [t1], out=t1.ap[:TS, :], in0=psA.ap[:TS, 256:288],
                      in1=CS.ap[:TS, s, 0:32], op=ALU.mult)
                    I("dve", "tensor_tensor", [psA, CS], [t2], out=t2.ap[:TS, :], in0=psA.ap[:TS, 288:320],
                      in1=CS.ap[:TS, s, 32:64], op=ALU.mult)
                    kro = KRO.next()
                    I("pool", "tensor_tensor", [t1, t2], [kro], out=kro.ap[:TS, :], in0=t1.ap[:TS, :], in1=t2.ap[:TS, :], op=ALU.add)
                    fw.dma("pool", kr_dst[t0 + s * TS:t0 + (s + 1) * TS, :], kro.ap[:TS, :], [kro], [], kro)
                    krb = KRB.next()
                    I("pool", "tensor_copy", [kro], [krb], out=krb.ap[:TS, 64:96], in_=kro.ap[:TS, :])
                    I("dve", "tensor_scalar", [psB, rs], [cqb], out=cqb.ap[:TS, :], in0=psB.ap[:TS, 0:QLORA],
                      scalar1=rs.ap[:TS, 1:2], scalar2=None, op0=ALU.mult)
                    st_[("b", s)] = (ckb, krb, cqb)
                return f

            def s_c(s):
                def f():
                    ckb, krb, cqb = st_.pop(("b", s))
                    transposes_small(ckb, krb, cqb, s, TS)
                return f

            def q_(h):
                return lambda: q_head(h, T)

            if nsub == 4:
                return {0: [n_a([0, 1], True)], 2: [n_b([0, 1]), n_a([2, 3], False)], 4: [n_b([2, 3]), s_a(0), s_b(0)],
                        5: [s_a(1), s_b(1)], 6: [s_c(0), s_a(2), s_b(2)], 7: [s_c(1), s_a(3), s_b(3)], 8: [s_c(2)], 9: [s_c(3)],
                        10: [q_(0)], 11: [q_(1), q_(2)], 12: [q_(3)], 13: [q_(4), q_(5)], 14: [q_(6)], 15: [q_(7)]}
            subs = list(range(nsub))
            return {0: [n_a(subs, True), n_b(subs)] + [g_(s) for s in subs for g_ in (s_a, s_b, s_c)] + [q_(h) for h in range(NH)]}

        def run_slots(slots):
            for k in sorted(slots):
                for f_ in slots[k]:
                    f_()

        def mid(sq, T, nsub, TS, t0, x_src, ktl, inter=None, pre0=None):
            if inter is None:
                x_load(nsub, TS, x_src, t0)
                inter = {}
            prev = None
            nkt = len(ktl)
            pre = pre0 if pre0 is not None else kv_load(sq, 0, ktl[0:8])
            cw = conv_w_load(0)
            for h in range(NH):
                nxt = {}

                def prefetch(h=h, nxt=nxt):
                    if h + 1 < NH:
                        nxt["v"] = kv_load(sq, h + 1, ktl[0:8])
                hooks = {}
                last = nkt - 1

                def addhook(pos, fn, hooks=hooks):
                    pos = min(pos, last)
                    while pos in hooks:
                        pos += 0.01
                    hooks[pos] = fn
                if prev is not None:
                    po, ph = prev
                    addhook(0, (lambda po=po: attn_fin_a(po, T)))
                    addhook(max(6, last - 4), (lambda po=po, ph=ph: attn_fin_b(po, ph, T)))
                if h % 2 == 0:
                    tg = {}

                    def conv_a(h=h, cw=cw, tg=tg):
                        tg["g"] = conv_chunk(h // 2, T, cw)
                    addhook(3, conv_a)
                    addhook(7, (lambda tg=tg: tg["g"][0]()))
                    addhook(11, (lambda tg=tg: tg["g"][1]()))
                    tg_prev = tg
                else:
                    addhook(3, (lambda tg=tg_prev: tg["g"][2]()))
                    addhook(7, (lambda tg=tg_prev: tg["g"][3]()))
                    if h + 1 < NH:
                        cw = conv_w_load((h + 1) // 2)
                o = attn_head(sq, h, T, ktl, hooks, pre, prefetch)
                pre = nxt.get("v")
                prev = (o, h)
                if h in inter:
                    inter[h]()
            attn_fin_a(prev[0], T)
            attn_fin_b(prev[0], prev[1], T)
            attn_post(T)
            conv_post(T)
            if nsub == 4:
                hold = {}

                def a1():
                    hold["a"] = norm_a(X[0:2], [0, 1], TS)

                def a3():
                    norm_b(hold.pop("a"), [0, 1], TS, [HT, HT], HT.ap)
                    hold["b"] = norm_a(X[2:4], [2, 3], TS)
                out_proj(T, nsub, TS, {1: a1, 3: a3})
                return hold["b"]
            out_proj(T, nsub, TS)
            return None

        def conv_state_out(dst):
            ps = G.next()
            for k in range(4):
                I("pe", "transpose", [UT, IDF], [ps], out=ps.ap[0:CK - 1, k * 128:(k + 1) * 128], in_=UT.ap[:, k, :],
                  identity=IDF.ap[:, :])
            tm = TMP.next()
            I("act", "copy", [ps], [tm], out=tm.ap[0:CK - 1, :], in_=ps.ap[0:CK - 1, :])
            fw.dma("pool", dst, tm.ap[0:CK - 1, :], [tm], [], tm)

        sp_ = Seq()
        sp_.kts, sp_.vs = kts_p, vs_p
        sp_.KS = [Buf("KSp%d" % j, None) for j in range(NT)]
        sp_.VSB = [Buf("VSp%d" % j, None) for j in range(NT)]
        I("pool", "memset", [], [UT], UT.ap[:, :, :], 0.0)

        def fp_prompt(i):
            return front_slots(512, 4, 128, i * 512, x_p, cs_p, cst_p, ckv_p, kr_p)

        run_slots(fp_prompt(0))
        kv_expand(sp_, 512, 4, 128, 0)
        fold_all_scratch()
        pre0 = None
        for i in range(NT):
            ktl = [(kt, 128, 0, False) for kt in range(4 * i)] + [(4 * i + r, 128, 128 * r, True) for r in range(4)]
            inter = None
            if i > 0:
                pp3 = ple_pieces(512, 4, 128, p_p, (i - 1) * 512)

                def after_h3(i=i):
                    final_store(512, 4, 128, y_p, (i - 1) * 512)

                def after_h6(i=i):
                    x_load(4, 128, x_p, i * 512)
                inter = {0: pp3[0], 1: pp3[1], 2: pp3[2], 3: after_h3, 6: after_h6}
            pn = mid(sp_, 512, 4, 128, i * 512, x_p, ktl, inter, pre0)
            ffn(512, 4, 128, fp_prompt(i + 1) if i + 1 < NT else None, pn)
            pre0 = None
            if i + 1 < NT:
                if i + 1 >= 2:
                    ktl_n = [(kt, 128, 0, False) for kt in range(4 * (i + 1))]
                    pre0 = kv_load(sp_, 0, ktl_n[0:8])
                kv_expand(sp_, 512, 4, 128, (i + 1) * 512)
        ple(512, 4, 128, p_p, (NT - 1) * 512)
        final_store(512, 4, 128, y_p, (NT - 1) * 512)
        conv_state_out(conv_p)

        if with_sample and WSAMPLE:
            ss_ = Seq()
            ss_.kts, ss_.vs = kts_s, vs_s
            ss_.KS = [Buf("KSs%d" % j, None) for j in range(3)]
            ss_.VSB = [Buf("VSs%d" % j, None) for j in range(3)]
            for blk in range(PAST // 512):
                for s in range(4):
                    r0 = blk * 512 + s * 128
                    ckb = CKVB.next()
                    fw.dma("pool", ckb.ap[:, :], c_ckv[r0:r0 + 128, :], [], [ckb], ckb)
                    krb = KRB.next()
                    fw.dma("pool", krb.ap[:, 64:96], c_kr[r0:r0 + 128, :], [], [krb], krb)
                    transposes_small(ckb, krb, None, s, 128)
                kv_expand(ss_, 512, 4, 128, blk * 512)
            tm = TMP.next()
            fw.dma("sp", tm.ap[0:CK - 1, :], s_conv, [], [tm], tm)
            ps = G.next()
            for k in range(4):
                I("pe", "transpose", [tm, IDF], [ps], out=ps.ap[:, k * 32:k * 32 + CK - 1], in_=tm.ap[0:CK - 1, k * 128:(k + 1) * 128],
                  identity=IDF.ap[0:CK - 1, 0:CK - 1])
            I("dve", "tensor_copy", [ps], [UT], out=UT.ap[:, :, :],
              in_=ps.ap[:, 0:128].rearrange("p (k t) -> p k t", t=32)[:, :, 0:CK - 1])
            run_slots(front_slots(DEC, 1, DEC, 0, x_s, cs_s, cst_s, ckv_s, kr_s))
            kv_expand(ss_, DEC, 1, DEC, PAST)
            ktl = [(kt, 128, 0, False) for kt in range(PAST // 128)] + [(PAST // 128, DEC, 0, False)]
            mid(ss_, DEC, 1, DEC, 0, x_s, ktl)
            ffn(DEC, 1, DEC)
            ple(DEC, 1, DEC, p_s, 0)
            final_store(DEC, 1, DEC, y_s, 0)
            conv_state_out(conv_s)

        fw.finish()
        nc._sbuf_bytes = fw.sbuf_bytes
    return nc


def _rope_tables(pos):
    half = ROPE // 2
    inv = (10000.0 ** (-np.arange(half, dtype=np.float32) / half)).astype(np.float32)
    ang = pos.astype(np.float32)[:, None] * inv[None, :]
    cos, sin = np.cos(ang).astype(np.float32), np.sin(ang).astype(np.float32)
    cos32 = np.concatenate([cos, cos], axis=1)
    ssin32 = np.concatenate([-sin, sin], axis=1)
    tok = np.ascontiguousarray(np.concatenate([cos32, ssin32], axis=1))
    ft = np.zeros((96, 2, pos.shape[0]), np.float32)
    ft[64:96, 0, :] = cos32.T
    ft[64:96, 1, :] = ssin32.T
    return tok, ft


def _prep_shared(inp, S):
    f = lambda a: np.ascontiguousarray(np.asarray(a, dtype=np.float32))
    w_in = f(inp["w_in"])[0]
    cq, ckv, kr, conv = w_in[:, 0:384], w_in[:, 384:640], w_in[:, 640:672], w_in[:, 672:1696]
    krs = np.concatenate([kr[:, 16:32], kr[:, 0:16]], axis=1)
    conv_l = np.concatenate([np.concatenate([conv[:, k * 128:(k + 1) * 128], conv[:, 512 + k * 128:512 + (k + 1) * 128]], axis=1)
                             for k in range(4)], axis=1)
    w_in_ext = np.ascontiguousarray(np.concatenate([ckv, kr, krs, cq, conv_l], axis=1))
    w_uq = f(inp["w_uq"])[0]
    blocks = []
    for h in range(NH):
        a = w_uq[:, h * 96:(h + 1) * 96]
        b = np.concatenate([np.zeros((QLORA, 64), np.float32), a[:, 80:96], a[:, 64:80]], axis=1)
        blocks += [a, b]
    w_uq_ext = np.ascontiguousarray(np.concatenate(blocks, axis=1))
    w_ukv = f(inp["w_ukv"])[0].reshape(KVL, NH, 128)
    w_uk = np.ascontiguousarray(w_ukv[:, :, 0:64].reshape(KVL, 512))
    w_uv = np.ascontiguousarray(w_ukv[:, :, 64:128].reshape(KVL, 512))

    def chunks(v, n):
        return np.asarray(v, np.float32).reshape(n, 128).T

    gout = np.concatenate([f(inp["attn_out_g"])[0], f(inp["conv_out_g"])[0]])
    cw = f(inp["conv_w"])[0]
    cw_l = cw.T.reshape(4, 128, CK).transpose(1, 0, 2).reshape(128, 4 * CK)
    pp = np.concatenate([
        chunks(f(inp["norm_mix_g"])[0], 8), chunks(f(inp["q_norm_g"])[0], 3), chunks(f(inp["norm_ffn_g"])[0], 8),
        chunks(f(inp["norm_ple_g"])[0], 8), chunks(gout, 8), chunks(f(inp["conv_b"])[0], 4),
        chunks(f(inp["conv_ln_g"])[0], 4), chunks(f(inp["conv_ln_b"])[0], 4), cw_l], axis=1)
    assert pp.shape == (128, 171)
    bc = np.concatenate([np.broadcast_to(f(inp["kv_norm_g"])[0][None, :], (128, KVL)),
                         np.broadcast_to(f(inp["norm_final_g"])[None, :], (128, D))], axis=1)
    cs_p, cst_p = _rope_tables(np.arange(S))
    cs_s, cst_s = _rope_tables(PAST + np.arange(DEC))
    return dict(w_in=w_in_ext, w_uq=w_uq_ext, w_uk=w_uk, w_uv=w_uv, w_out=f(inp["w_out"])[0], w_up=f(inp["w_ff_up"])[0],
                w_down=f(inp["w_ff_down"])[0], w_gate=f(inp["w_ple_gate"])[0], w_pp=f(inp["w_ple_proj"])[0],
                pp=np.ascontiguousarray(pp), bc=np.ascontiguousarray(bc), cs_p=cs_p, cs_s=cs_s, cst_p=cst_p, cst_s=cst_s)


_NC_CACHE = {}


def run(inputs, S=SEQ, ncores=NCORES, with_sample=True, trace=False):
    f = lambda a: np.ascontiguousarray(np.asarray(a, dtype=np.float32))
    shared = _prep_shared(inputs, S)
    key = (S, with_sample)
    if key not in _NC_CACHE:
        _NC_CACHE[key] = build_nc(S, with_sample)
    nc = _NC_CACHE[key]
    in_maps = []
    for b in range(ncores):
        m = dict(shared)
        m["x_p"] = f(inputs["x_prompt"][b][:S])
        m["p_p"] = f(inputs["p_prompt"][0, b][:S])
        m["x_s"] = f(inputs["x_sample"][b])
        m["p_s"] = f(inputs["p_sample"][0, b])
        m["c_ckv"] = f(inputs["cache_ckv"][0, b])
        m["c_kr"] = f(inputs["cache_krope"][0, b])
        m["s_conv"] = f(inputs["state_conv"][0, b])
        in_maps.append(m)
    res = run_bass_kernel_spmd(nc, in_maps, core_ids=list(range(ncores)), trace=trace)
    r = res.results
    st = lambda k: np.stack([np.asarray(r[b][k], dtype=np.float32) for b in range(ncores)])
    outs = (st("y_p"), st("y_s"), st("ckv_p")[None], st("kr_p")[None], st("conv_p")[None],
            st("ckv_s")[None], st("kr_s")[None], st("conv_s")[None])
    return outs, res


def kernel(**inputs):
    outs, _ = run(inputs)
    return outs
```

```python
import os
import numpy as np
from contextlib import ExitStack
import concourse.bass as bass
import concourse.mybir as mybir
from concourse.bass_utils import run_bass_kernel_spmd

F32 = mybir.dt.float32
BF16 = mybir.dt.bfloat16
ALU = mybir.AluOpType
AF = mybir.ActivationFunctionType

D = 1024
NH = 8
QLORA = 384
KVL = 256
ROPE = 32
CCH = 512
CK = 31
DFF = 4096
PLE = 256
PAST = 1024
DEC = 64
SEQ = 8192
NCORES = 8
WIN_COLS = 704 + 1024
WUQ_COLS = NH * 192
SC_ATT = 96 ** -0.5
STAGE = int(os.environ.get('KSTAGE', '99'))
WSAMPLE = int(os.environ.get('KSAMPLE', '1'))
KSUB = int(os.environ.get('KSUB', '99'))


class Buf:
    def __init__(self, name, ap):
        self.name = name
        self.ap = ap
        self.last_write = None
        self.readers = {}
        self.dsem = None
        self.dcount = 0
        self.psum = False


class Eng:
    def __init__(self, name, sem, is_pe=False):
        self.name = name
        self.sem = sem
        self.count = 0
        self.waited = {}
        self.ops = []
        self.is_pe = is_pe


class Rot:
    def __init__(self, bufs):
        self.bufs = bufs
        self.i = 0

    def next(self):
        b = self.bufs[self.i % len(self.bufs)]
        self.i += 1
        return b


class FW:
    def __init__(self, nc, stack):
        self.nc = nc
        self.stack = stack
        self.engs = {}
        self.sems = {}
        for n in ["pe", "act", "dve", "pool", "sp"]:
            sem = stack.enter_context(nc.semaphore("s_" + n))
            self.engs[n] = Eng(n, sem, is_pe=(n == "pe"))
            self.sems[id(sem)] = sem
        self.dbufs = []
        self.sbuf_bytes = 0

    def tile(self, name, shape, dt):
        t = self.stack.enter_context(self.nc.sbuf_tensor(name, shape, dt))
        n = 1
        for s in shape[1:]:
            n *= s
        self.sbuf_bytes += n * (4 if dt == F32 else 2)
        return t

    def sbuf(self, name, shape, dt):
        return Buf(name, self.tile(name, shape, dt))

    def _deps(self, eng, reads, writes):
        deps = {}

        def add(k, v):
            if deps.get(k, 0) < v:
                deps[k] = v
        for b in reads:
            if b.last_write is not None:
                add(*b.last_write)
            if b.psum:
                for k, v in b.readers.items():
                    if k != id(eng.sem):
                        add(k, v)
        for b in writes:
            if b.last_write is not None:
                add(*b.last_write)
            for k, v in b.readers.items():
                add(k, v)
        out = []
        for k, v in deps.items():
            if eng.is_pe and k == id(eng.sem):
                continue
            if eng.waited.get(k, 0) >= v:
                continue
            eng.waited[k] = v
            out.append((self.sems[k], v))
        return out

    def _record(self, ev, reads, writes):
        for b in reads:
            if b.readers.get(ev[0], 0) < ev[1]:
                b.readers[ev[0]] = ev[1]
        for b in writes:
            b.last_write = ev
            b.readers = {}

    def I(self, engname, method, reads, writes, *args, **kw):
        eng = self.engs[engname]
        waits = self._deps(eng, reads, writes)
        eng.count += 1
        ev = (id(eng.sem), eng.count)
        sem = eng.sem

        def emit(e):
            for s, v in waits:
                e.wait_ge(s, v)
            getattr(e, method)(*args, **kw).then_inc(sem, 1)
        eng.ops.append(emit)
        self._record(ev, reads, writes)

    def dma(self, qname, out_ap, in_ap, reads, writes, owner, **kw):
        eng = self.engs[qname]
        if owner.dsem is None:
            owner.dsem = {}
        if qname not in owner.dsem:
            sem_ = self.stack.enter_context(self.nc.semaphore("d%s_%s" % (qname, owner.name)))
            owner.dsem[qname] = [sem_, 0]
            self.sems[id(sem_)] = sem_
            self.dbufs.append(owner.dsem[qname])
        waits = self._deps(eng, reads, writes)
        owner.dsem[qname][1] += 16
        sem = owner.dsem[qname][0]
        ev = (id(sem), owner.dsem[qname][1])

        def emit(e):
            for s, v in waits:
                e.wait_ge(s, v)
            e.dma_start(out=out_ap, in_=in_ap, **kw).then_inc(sem, 16)
        eng.ops.append(emit)
        self._record(ev, reads, writes)

    def _all_events(self):
        final = {}
        for e in self.engs.values():
            if e.count:
                final[id(e.sem)] = e.count
        for sem_, cnt_ in self.dbufs:
            final[id(sem_)] = cnt_
        return final

    def barrier(self, engines=("pe", "act", "dve", "pool", "sp")):
        final = self._all_events()
        for n in engines:
            eng = self.engs[n]
            ws = []
            for k, v in final.items():
                if eng.waited.get(k, 0) >= v:
                    continue
                if k == id(eng.sem) and eng.is_pe:
                    continue
                eng.waited[k] = v
                ws.append((self.sems[k], v))

            def emit(e, ws=ws):
                for s, v in ws:
                    e.wait_ge(s, v)
            eng.ops.append(emit)

    def finish(self):
        self.barrier(engines=("sp",))
        nc = self.nc
        engs = self.engs
        with nc.Block() as block:
            @block.tensor
            def _(e):
                for f in engs["pe"].ops:
                    f(e)

            @block.scalar
            def _(e):
                for f in engs["act"].ops:
                    f(e)

            @block.vector
            def _(e):
                for f in engs["dve"].ops:
                    f(e)

            @block.gpsimd
            def _(e):
                for f in engs["pool"].ops:
                    f(e)

            @block.sync
            def _(e):
                for f in engs["sp"].ops:
                    f(e)


def build_nc(S=SEQ, with_sample=True, dbg=False):
    nc = bass.Bass("TRN2", target_bir_lowering=False)
    NT = S // 512

    def din(name, shape, dt=F32):
        return nc.dram_tensor(name, list(shape), dt, kind="ExternalInput").ap()

    def dout(name, shape, dt=F32):
        return nc.dram_tensor(name, list(shape), dt, kind="ExternalOutput").ap()

    def dscr(name, shape, dt=BF16):
        return nc.dram_tensor(name, list(shape), dt, kind="Internal").ap()

    x_p = din("x_p", [S, D])
    p_p = din("p_p", [S, PLE])
    x_s = din("x_s", [DEC, D])
    p_s = din("p_s", [DEC, PLE])
    c_ckv = din("c_ckv", [PAST, KVL])
    c_kr = din("c_kr", [PAST, ROPE])
    s_conv = din("s_conv", [CK - 1, CCH])
    w_in = din("w_in", [D, WIN_COLS])
    w_uq = din("w_uq", [QLORA, WUQ_COLS])
    w_uk = din("w_uk", [KVL, 512])
    w_uv = din("w_uv", [KVL, 512])
    w_out = din("w_out", [D, D])
    w_up = din("w_up", [D, DFF])
    w_down = din("w_down", [DFF, D])
    w_gate = din("w_gate", [D, D])
    w_pp = din("w_pp", [PLE, D])
    pp_in = din("pp", [128, 171])
    bc_in = din("bc", [128, 256 + 1024])
    cs_p = din("cs_p", [S, 64])
    cs_s = din("cs_s", [DEC, 64])
    cst_p = din("cst_p", [96, 2, S])
    cst_s = din("cst_s", [96, 2, DEC])

    y_p = dout("y_p", [S, D])
    y_s = dout("y_s", [DEC, D])
    ckv_p = dout("ckv_p", [S, KVL])
    kr_p = dout("kr_p", [S, ROPE])
    conv_p = dout("conv_p", [CK - 1, CCH])
    ckv_s = dout("ckv_s", [DEC, KVL])
    kr_s = dout("kr_s", [DEC, ROPE])
    conv_s = dout("conv_s", [CK - 1, CCH])

    wos = dscr("wos", [D, D])
    wus = dscr("wus", [D, DFF])
    wds = dscr("wds", [DFF, D])
    wgs = dscr("wgs", [D, D])
    wcs = dscr("wcs", [D, 1024])
    kts_p = dscr("kts_p", [NH, 96, S])
    vs_p = dscr("vs_p", [NH, 128, S // 128, 65])
    SK = PAST + 128
    kts_s = dscr("kts_s", [NH, 96, SK])
    vs_s = dscr("vs_s", [NH, 128, SK // 128, 65])

    st = ExitStack()
    with st:
        fw = FW(nc, st)
        I = fw.I

        XT = fw.tile("X", [128, 4, D], F32)
        X = [Buf("X%d" % s, XT[:, s, :]) for s in range(4)]
        HT = fw.sbuf("HT", [128, 8, 512], BF16)
        XS = Rot([fw.sbuf("XS%d" % i, [128, D], BF16) for i in range(2)])
        XF = Rot([fw.sbuf("XF%d" % i, [128, D], F32) for i in range(2)])
        HTPT = fw.tile("HTP", [128, 8, 512], BF16)
        HTP = [Buf("HTP%d" % s_, HTPT[:, :, s_ * 128:(s_ + 1) * 128]) for s_ in range(4)]
        KRT = fw.sbuf("KRT", [96, 512], BF16)
        STT = fw.tile("STT", [128, 64], F32)
        STATS = Rot([Buf("st%d" % i, STT[:, 2 * i:2 * i + 2]) for i in range(16)])
        SGRP = Rot([([Buf("sg%d_%d" % (g, j), STT[:, 32 + 8 * g + j:32 + 8 * g + j + 1]) for j in range(4)],
                     Buf("sr%d" % g, STT[:, 32 + 8 * g + 4:32 + 8 * g + 8])) for g in range(4)])
        SMT = fw.tile("SMT", [128, 4, 32], F32)
        SMALL = Rot([Buf("sm%d" % i, SMT[:, i, :]) for i in range(4)])
        CKVO = Rot([fw.sbuf("CKVO%d" % i, [128, KVL], F32) for i in range(2)])
        KRO = Rot([fw.sbuf("KRO%d" % i, [128, ROPE], F32) for i in range(2)])
        CKVB = Rot([fw.sbuf("CKVB%d" % i, [128, KVL], BF16) for i in range(2)])
        KRB = Rot([fw.sbuf("KRB%d" % i, [128, 96], BF16) for i in range(2)])
        CQB = Rot([fw.sbuf("CQB%d" % i, [128, QLORA], BF16) for i in range(2)])
        CKVT = fw.sbuf("CKVT", [128, 2, 512], BF16)
        CQT = fw.sbuf("CQT", [128, 3, 512], BF16)
        CS = fw.sbuf("CS", [128, 4, 64], F32)
        CST = fw.sbuf("CST", [96, 2, 512], F32)
        KTNT = fw.tile("KTN", [128, 8, 512], BF16)
        AT = [Buf("AT%d" % i, KTNT[:, 4 * i:4 * i + 4, :]) for i in range(2)]
        ATR = Rot(AT)
        VN = fw.sbuf("VN", [128, 8, 4, 65], BF16)
        QTT = fw.tile("QT", [96, 8, 512], BF16)
        QT = [Buf("QT%d" % h, QTT[:, h, :]) for h in range(NH)]
        UC = Rot([fw.sbuf("UC%d" % i, [128, 512 + CK - 1], F32) for i in range(2)])
        UT = fw.sbuf("UT", [128, 4, CK - 1], F32)
        ACCT = fw.tile("ACC", [128, 4, 512], F32)
        ACC = [Buf("ACC%d" % k, ACCT[:, k, :]) for k in range(4)]
        YNT = fw.tile("YN", [128, 4, 512], BF16)
        YN = [Buf("YN%d" % k, YNT[:, k, :]) for k in range(4)]
        TMP = Rot([fw.sbuf("TMP%d" % i, [128, 512], F32) for i in range(4)])
        STAT = Rot([fw.sbuf("STAT%d" % i, [128, 512], F32) for i in range(2)])
        KH = Rot([fw.sbuf("KH%d" % i, [96, 1024], BF16) for i in range(3)])
        VH = Rot([fw.sbuf("VH%d" % i, [128, 8, 65], BF16) for i in range(3)])
        PTS = Rot([fw.sbuf("PTS%d" % i, [128, 1024], BF16) for i in range(3)])
        RL = fw.sbuf("RL", [65, 512], F32)
        RLH = fw.sbuf("RLH", [65, 512], BF16)
        RLL = fw.sbuf("RLL", [65, 512], BF16)
        ATTT = fw.tile("ATT", [128, 4, 512], BF16)
        ATT = [Buf("ATT%d" % j, ATTT[:, j, :]) for j in range(4)]
        SQB = Rot([fw.sbuf("SQB%d" % i, [128, 512], BF16) for i in range(2)])
        WPC = Rot([fw.sbuf("WPC%d" % i, [128, 4096], BF16) for i in range(4)])
        PB = fw.sbuf("PB", [128, 4, PLE], BF16)
        PTR = fw.sbuf("PTR", [128, 2, 512], BF16)
        WIN = fw.sbuf("WIN", [128, 8, 704], BF16)
        WUQ = fw.sbuf("WUQ", [128, 3, WUQ_COLS], BF16)
        WUK = fw.sbuf("WUK", [128, 2, 512], BF16)
        WUV = fw.sbuf("WUV", [128, 2, 512], BF16)
        WP = fw.sbuf("WP", [128, 2, D], BF16)
        PP = fw.sbuf("PP", [128, 171], F32)
        BC = fw.sbuf("BC", [128, 256 + 1024], F32)
        IDB = fw.sbuf("IDB", [128, 128], BF16)
        IDF = fw.sbuf("IDF", [128, 128], F32)
        ONB = fw.sbuf("ONB", [128, 128], BF16)
        ONF = fw.sbuf("ONF", [128, 128], F32)
        psG = st.enter_context(nc.psum_tensor("psG", [128, 2048], F32))
        psb = [psG[:, i * 512:(i + 1) * 512] for i in range(4)] + \
              [st.enter_context(nc.psum_tensor("ps%d" % i, [128, 512], F32)) for i in range(4, 8)]
        G = Rot([Buf("G%d" % i, psb[i]) for i in range(4)])
        OB = Rot([Buf("O%d" % i, psb[4 + i]) for i in range(2)])
        TB = Rot([Buf("T%d" % i, psb[6 + i]) for i in range(2)])
        for b_ in G.bufs + OB.bufs + TB.bufs:
            b_.psum = True
        GH = Rot(TB.bufs)

        def tbv(b):
            return b.ap[:, :].bitcast(BF16).rearrange("p (c t) -> p c t", t=128)

        GMIX, GQ, GFFN, GPLE, GOUT, CB, LNG, LNB, CW = 0, 8, 11, 19, 27, 35, 39, 43, 47

        def ppc(c):
            return PP.ap[:, c:c + 1]

        fw.dma("sp", PP.ap[:, :], pp_in, [], [PP], PP)
        fw.dma("sp", BC.ap[:, :], bc_in, [], [BC], BC)
        I("pool", "memset", [], [IDF], IDF.ap[:, :], 0.0)
        I("pool", "affine_select", [IDF], [IDF], out=IDF.ap[:, :], in_=IDF.ap[:, :], pattern=[[-1, 128]],
          compare_op=ALU.not_equal, fill=1.0, base=0, channel_multiplier=1)
        I("dve", "tensor_copy", [IDF], [IDB], out=IDB.ap[:, :], in_=IDF.ap[:, :])
        I("pool", "memset", [], [ONB], ONB.ap[:, :], 1.0)
        I("pool", "memset", [], [ONF], ONF.ap[:, :], 1.0)
        I("pool", "memset", [], [VN], VN.ap[:, :, :, :], 1.0)
        for kb in KRB.bufs:
            I("pool", "memset", [], [kb], kb.ap[:, :], 0.0)
        fw.dma("pool", WUK.ap[:, :, :], w_uk.rearrange("(c p) n -> p c n", p=128), [], [WUK], WUK)
        fw.dma("pool", WUV.ap[:, :, :], w_uv.rearrange("(c p) n -> p c n", p=128), [], [WUV], WUV)
        fw.dma("pool", WP.ap[:, :, :], w_pp.rearrange("(c p) n -> p c n", p=128), [], [WP], WP)
        I("pool", "tensor_scalar", [WP], [WP], out=WP.ap[:, :, :], in0=WP.ap[:, :, :], scalar1=0.5, scalar2=None, op0=ALU.mult)
        XFLAT = XT[:, :, :].rearrange("p s d -> p (s d)")
        STG = Rot([Buf("STG%d" % i, XFLAT[:, 2048 * i:2048 * (i + 1)]) for i in range(2)])

        def fold_resident(src, ncols, nchunk, gcol, dstbuf):
            for c in range(nchunk):
                for c0 in range(0, ncols, 2048):
                    w = min(2048, ncols - c0)
                    sg = STG.next()
                    fw.dma("sp", sg.ap[:, 0:w], src[c * 128:(c + 1) * 128, c0:c0 + w], [], [sg], sg)
                    I("dve", "tensor_scalar", [sg, PP], [dstbuf], out=dstbuf.ap[:, c, c0:c0 + w], in0=sg.ap[:, 0:w],
                      scalar1=ppc(gcol + c), scalar2=None, op0=ALU.mult)

        PP2 = fw.sbuf("PP2", [128, 8], F32)
        I("pool", "tensor_scalar", [PP], [PP2], out=PP2.ap[:, 0:8], in0=PP.ap[:, LNG:LNG + 8], scalar1=0.5, scalar2=None, op0=ALU.mult)
        fold_resident(w_in[:, 0:704], 704, 8, GMIX, WIN)
        fold_resident(w_uq, WUQ_COLS, 3, GQ, WUQ)

        def fold_scratch(src, dst, ncols, nchunk, gcol):
            for c in range(nchunk):
                for c0 in range(0, ncols, 2048):
                    w = min(2048, ncols - c0)
                    sg = STG.next()
                    sb = WPC.next()
                    fw.dma("sp", sg.ap[:, 0:w], src[c * 128:(c + 1) * 128, c0:c0 + w], [], [sg], sg)
                    I("dve", "tensor_scalar", [sg, PP], [sb], out=sb.ap[:, 0:w], in0=sg.ap[:, 0:w],
                      scalar1=ppc(gcol + c), scalar2=None, op0=ALU.mult)
                    fw.dma("sp", dst[c * 128:(c + 1) * 128, c0:c0 + w], sb.ap[:, 0:w], [sb], [], sb)

        wds_v = wds.rearrange("(j p) n -> p j n", p=128)

        def fold_all_scratch():
            for c in range(8):
                sg_ = STG.next()
                sb_ = WPC.next()
                fw.dma("sp", sg_.ap[:, 0:1024], w_in[c * 128:(c + 1) * 128, 704:WIN_COLS], [], [sg_], sg_)
                I("dve", "tensor_scalar", [sg_, PP], [sb_], out=sb_.ap[:, 0:1024], in0=sg_.ap[:, 0:1024],
                  scalar1=ppc(GMIX + c), scalar2=None, op0=ALU.mult)
                hv = sb_.ap[:, 0:1024].rearrange("p (k n) -> p k n", n=256)[:, :, 0:128]
                I("dve", "tensor_scalar", [sb_], [sb_], out=hv, in0=hv, scalar1=0.5, scalar2=None, op0=ALU.mult)
                fw.dma("sp", wcs[c * 128:(c + 1) * 128, :], sb_.ap[:, 0:1024], [sb_], [], sb_)
            fold_scratch(w_out, wos, D, 8, GOUT)
            fold_scratch(w_up, wus, DFF, 8, GFFN)
            fold_scratch(w_gate, wgs, D, 8, GPLE)
            wd_v = w_down.rearrange("(j p) n -> p j n", p=128)
            wds_v = wds.rearrange("(j p) n -> p j n", p=128)
            for j0 in range(0, 32, 4):
                sb = WPC.next()
                fw.dma("pool", sb.ap[:, :].rearrange("p (j n) -> p j n", n=D), wd_v[:, j0:j0 + 4, :], [], [sb], sb)
                fw.dma("sp", wds_v[:, j0:j0 + 4, :], sb.ap[:, :].rearrange("p (j n) -> p j n", n=D), [sb], [], sb)
            fw.barrier()


        cnt = {"ev": 0}

        def evac_eng():
            cnt["ev"] += 1
            return "act" if cnt["ev"] % 2 else "dve"

        def copy_on(eng, reads, writes, out, in_):
            if eng == "act":
                I("act", "copy", reads, writes, out=out, in_=in_)
            else:
                I(eng, "tensor_copy", reads, writes, out=out, in_=in_)

        def rstd(stb, TS, eps):
            I("act", "activation", [stb], [stb], out=stb.ap[:TS, 1:2], in_=stb.ap[:TS, 0:1], func=AF.Ln, bias=eps)
            I("act", "activation", [stb], [stb], out=stb.ap[:TS, 1:2], in_=stb.ap[:TS, 1:2], func=AF.Exp, scale=-0.5)

        def _mscols(ms, TS, k):
            g = int(ms[0].name[2:].split("_")[0])
            return STT[:TS, 32 + 8 * g:32 + 8 * g + k]

        def group_stats(srcs, TS, eps):
            ms, rs = SGRP.next()
            k = len(srcs)
            for j, (ap_, bufs_, n_, jap_, jbuf_) in enumerate(srcs):
                I("act", "activation", bufs_, [ms[j], jbuf_], out=jap_, in_=ap_, func=AF.Square,
                  scale=float(n_) ** -0.5, accum_out=ms[j].ap[:TS, 0:1])
            I("act", "activation", ms[:k], [rs], out=rs.ap[:TS, 0:k], in_=_mscols(ms, TS, k),
              func=AF.Ln, bias=eps)
            I("act", "activation", [rs], [rs], out=rs.ap[:TS, 0:k], in_=rs.ap[:TS, 0:k], func=AF.Exp, scale=-0.5)
            return rs

        def norm_a(srcs, subs, TS):
            xss = [XS.next() for _ in subs]
            rs = group_stats([(srcs[j].ap[:TS, :], [srcs[j]], D, xss[j].ap[:TS, :], xss[j]) for j in range(len(subs))], TS, 1e-6)
            for j, s in enumerate(subs):
                I("dve", "tensor_scalar", [srcs[j], rs], [xss[j]], out=xss[j].ap[:TS, :], in0=srcs[j].ap[:TS, :],
                  scalar1=rs.ap[:TS, j:j + 1], scalar2=None, op0=ALU.mult)
            return xss

        def norm_b(xss, subs, TS, dst_bufs, dst_tile):
            for j, s in enumerate(subs):
                xs = xss[j]
                tb = TB.next()
                tv = tbv(tb)
                for c in range(8):
                    I("pe", "transpose", [xs, IDB], [tb], out=tv[:, c, 0:TS], in_=xs.ap[:TS, c * 128:(c + 1) * 128],
                      identity=IDB.ap[:TS, :TS])
                copy_on(evac_eng(), [tb], [dst_bufs[j]], dst_tile[:, :, s * TS:(s + 1) * TS], tv[:, :, 0:TS])

        def norm_transpose(srcs, subs, TS, dst):
            for i0 in range(0, len(subs), 2):
                sub2 = subs[i0:i0 + 2]
                xss = norm_a(srcs[i0:i0 + 2], sub2, TS)
                norm_b(xss, sub2, TS, [dst] * len(sub2), dst.ap)

        def bcast_rstd(ps, T, n, eps):
            rs = STAT.next()
            I("act", "activation", [ps], [rs], out=rs.ap[:, 0:T], in_=ps.ap[:, 0:T], func=AF.Ln, bias=eps, scale=1.0 / n)
            I("act", "activation", [rs], [rs], out=rs.ap[:, 0:T], in_=rs.ap[:, 0:T], func=AF.Exp, scale=-0.5)
            return rs

        class Seq:
            pass

        def transposes_small(ckb, krb, cqb, s, TS):
            tb = TB.next()
            tv = tbv(tb)
            for c in range(2):
                I("pe", "transpose", [ckb, IDB], [tb], out=tv[:, c, 0:TS], in_=ckb.ap[:TS, c * 128:(c + 1) * 128],
                  identity=IDB.ap[:TS, :TS])
            if cqb is not None:
                for c in range(3):
                    I("pe", "transpose", [cqb, IDB], [tb], out=tv[:, 2 + c, 0:TS], in_=cqb.ap[:TS, c * 128:(c + 1) * 128],
                      identity=IDB.ap[:TS, :TS])
            if KSUB >= 9:
                I("pe", "transpose", [krb, IDB], [tb], out=tv[0:96, 5, 0:TS], in_=krb.ap[:TS, 0:96], identity=IDB.ap[:TS, :TS])
            if KSUB >= 8 or KSUB == 5:
                I("act", "copy", [tb], [CKVT], out=CKVT.ap[:, :, s * TS:(s + 1) * TS], in_=tv[:, 0:2, 0:TS])
            if cqb is not None and (KSUB >= 8 or KSUB == 6):
                I("dve", "tensor_copy", [tb], [CQT], out=CQT.ap[:, :, s * TS:(s + 1) * TS], in_=tv[:, 2:5, 0:TS])
            I("dve", "tensor_copy", [tb], [KRT], out=KRT.ap[64:96, s * TS:(s + 1) * TS], in_=tv[64:96, 5, 0:TS])

        def kv_expand(sq, T, nsub, TS, key0):
            for j in range(4):
                ps = G.next()
                for c in range(2):
                    I("pe", "matmul", [WUK, CKVT], [ps], ps.ap[:, 0:T], lhsT=WUK.ap[:, c, j * 128:(j + 1) * 128],
                      rhs=CKVT.ap[:, c, 0:T], start=(c == 0), stop=(c == 1))
                I("act", "copy", [ps], AT, out=KTNT[0:64, 2 * j, 0:T], in_=ps.ap[0:64, 0:T])
                I("dve", "tensor_copy", [ps], AT, out=KTNT[0:64, 2 * j + 1, 0:T], in_=ps.ap[64:128, 0:T])
            I("pool", "tensor_copy", [KRT], AT, out=KTNT[64:96, :, 0:T],
              in_=KRT.ap[64:96, 0:T].unsqueeze(1).to_broadcast([32, NH, T]))
            for s in range(nsub):
                ps = G.next()
                for c in range(2):
                    I("pe", "matmul", [WUV, CKVT], [ps], ps.ap[:TS, :], lhsT=CKVT.ap[:, c, s * TS:(s + 1) * TS],
                      rhs=WUV.ap[:, c, :], start=(c == 0), stop=(c == 1))
                copy_on(evac_eng(), [ps], [VN], VN.ap[:TS, :, s, 0:64], ps.ap[:TS, :].rearrange("p (h d) -> p h d", d=64))
            ti = key0 // 512
            fw.dma("sp", sq.kts[:, :, key0:key0 + T].rearrange("h r c -> r h c"), KTNT[0:96, :, 0:T], AT, [sq.KS[ti]], AT[0])
            kt0 = key0 // 128
            fw.dma("sp", sq.vs[:, 0:TS, kt0:kt0 + nsub, :].rearrange("h p k c -> p h k c"), VN.ap[:TS, :, 0:nsub, :],
                   [VN], [sq.VSB[ti]], VN)

        def wview(src, n0, n1):
            return src.rearrange("(c p) n -> p c n", p=128)[:, :, n0:n1]

        def conv_w_load(k):
            wc = WPC.next()
            wcv = wc.ap[:, 0:2048].rearrange("p (c n) -> p c n", n=256)
            fw.dma("sp", wcv, wview(wcs, k * 256, (k + 1) * 256), [], [wc], wc)
            return (wc, wcv)

        def conv_chunk(k, T, pre_w):
            wc, wcv = pre_w
            psa = GH.next()
            psg = GH.next()
            for c in range(8):
                I("pe", "matmul", [wc] + HTP, [psa], psa.ap[:, 0:T], lhsT=wcv[:, c, 0:128],
                  rhs=HTPT[:, c, 0:T], start=(c == 0), stop=(c == 7))
            for c in range(8):
                I("pe", "matmul", [wc] + HTP, [psg], psg.ap[:, 0:T], lhsT=wcv[:, c, 128:256],
                  rhs=HTPT[:, c, 0:T], start=(c == 0), stop=(c == 7))
            sig = TMP.next()
            I("act", "activation", [psg], [sig], out=sig.ap[:, 0:T], in_=psg.ap[:, 0:T], func=AF.Tanh, scale=0.5)
            uc = UC.next()
            I("pool", "tensor_copy", [UT], [uc], out=uc.ap[:, 0:CK - 1], in_=UT.ap[:, k, :])
            I("dve", "scalar_tensor_tensor", [psa, sig, uc], [uc], out=uc.ap[:, CK - 1:CK - 1 + T], in0=sig.ap[:, 0:T],
              scalar=1.0, in1=psa.ap[:, 0:T], op0=ALU.add, op1=ALU.mult)
            I("pool", "tensor_copy", [uc, UT], [UT], out=UT.ap[:, k, :], in_=uc.ap[:, T:T + CK - 1])
            a = ACC[k]
            I("dve", "tensor_scalar", [uc, PP], [a], out=a.ap[:, 0:T], in0=uc.ap[:, 0:T], scalar1=ppc(CW + k * CK),
              scalar2=ppc(CB + k), op0=ALU.mult, op1=ALU.add)

            def taps(j0, j1):
                def f():
                    for j in range(j0, j1):
                        I("dve", "scalar_tensor_tensor", [uc, PP, a], [a], out=a.ap[:, 0:T], in0=uc.ap[:, j:j + T],
                          scalar=ppc(CW + k * CK + j), in1=a.ap[:, 0:T], op0=ALU.mult, op1=ALU.add)
                return f
            return [taps(1, 8), taps(8, 16), taps(16, 24), taps(24, CK)]

        def conv_post(T):
            s1 = G.next()
            s2 = G.next()
            for k in range(4):
                sq = SQB.next()
                I("act", "activation", [ACC[k]], [sq], out=sq.ap[:, 0:T], in_=ACC[k].ap[:, 0:T], func=AF.Square)
                I("pe", "matmul", [ONF, ACC[k]], [s1], s1.ap[:, 0:T], lhsT=ONF.ap[:, :], rhs=ACC[k].ap[:, 0:T],
                  start=(k == 0), stop=(k == 3))
                I("pe", "matmul", [ONB, sq], [s2], s2.ap[:, 0:T], lhsT=ONB.ap[:, :], rhs=sq.ap[:, 0:T],
                  start=(k == 0), stop=(k == 3))
            mu = STAT.next()
            I("act", "activation", [s1], [mu], out=mu.ap[:, 0:T], in_=s1.ap[:, 0:T], func=AF.Copy, scale=1.0 / CCH)
            musq = TMP.next()
            I("pool", "tensor_tensor", [mu], [musq], out=musq.ap[:, 0:T], in0=mu.ap[:, 0:T], in1=mu.ap[:, 0:T], op=ALU.mult)
            var = TMP.next()
            I("dve", "scalar_tensor_tensor", [s2, musq], [var], out=var.ap[:, 0:T], in0=s2.ap[:, 0:T], scalar=1.0 / CCH,
              in1=musq.ap[:, 0:T], op0=ALU.mult, op1=ALU.subtract)
            rs = STAT.next()
            I("act", "activation", [var], [rs], out=rs.ap[:, 0:T], in_=var.ap[:, 0:T], func=AF.Ln, bias=1e-5)
            I("act", "activation", [rs], [rs], out=rs.ap[:, 0:T], in_=rs.ap[:, 0:T], func=AF.Exp, scale=-0.5)
            for k in range(4):
                a = ACC[k]
                I("dve", "tensor_tensor", [a, mu], [a], out=a.ap[:, 0:T], in0=a.ap[:, 0:T], in1=mu.ap[:, 0:T], op=ALU.subtract)
                I("pool", "tensor_tensor", [a, rs], [a], out=a.ap[:, 0:T], in0=a.ap[:, 0:T], in1=rs.ap[:, 0:T], op=ALU.mult)
                th = TMP.next()
                I("act", "activation", [a, PP2], [th], out=th.ap[:, 0:T], in_=a.ap[:, 0:T], func=AF.Tanh,
                  scale=PP2.ap[:, k:k + 1], bias=PP2.ap[:, 4 + k:5 + k])
                I("dve", "tensor_scalar", [a, PP2], [a], out=a.ap[:, 0:T], in0=a.ap[:, 0:T], scalar1=PP2.ap[:, k:k + 1],
                  scalar2=PP2.ap[:, 4 + k:5 + k], op0=ALU.mult, op1=ALU.add)
                I("dve", "scalar_tensor_tensor", [th, a], [a], out=a.ap[:, 0:T], in0=th.ap[:, 0:T], scalar=1.0,
                  in1=a.ap[:, 0:T], op0=ALU.add, op1=ALU.mult)
            s3 = G.next()
            for k in range(4):
                sq = SQB.next()
                I("act", "activation", [ACC[k]], [sq], out=sq.ap[:, 0:T], in_=ACC[k].ap[:, 0:T], func=AF.Square)
                I("pe", "matmul", [ONB, sq], [s3], s3.ap[:, 0:T], lhsT=ONB.ap[:, :], rhs=sq.ap[:, 0:T],
                  start=(k == 0), stop=(k == 3))
            rs2 = bcast_rstd(s3, T, CCH, 1e-6)
            for k in range(4):
                I("dve", "tensor_tensor", [ACC[k], rs2], [YN[k]], out=YN[k].ap[:, 0:T], in0=ACC[k].ap[:, 0:T],
                  in1=rs2.ap[:, 0:T], op=ALU.mult)

        def q_head(h, T):
            psa = G.next()
            psb_ = G.next()
            for c in range(3):
                I("pe", "matmul", [WUQ, CQT], [psa], psa.ap[0:96, 0:T], lhsT=WUQ.ap[:, c, h * 192:h * 192 + 96],
                  rhs=CQT.ap[:, c, 0:T], start=(c == 0), stop=(c == 2))
            for c in range(3):
                I("pe", "matmul", [WUQ, CQT], [psb_], psb_.ap[0:96, 0:T], lhsT=WUQ.ap[:, c, h * 192 + 96:h * 192 + 192],
                  rhs=CQT.ap[:, c, 0:T], start=(c == 0), stop=(c == 2))
            I("act", "copy", [psa], [QT[h]], out=QT[h].ap[0:64, 0:T], in_=psa.ap[0:64, 0:T])
            t1 = TMP.next()
            t2 = TMP.next()
            I("dve", "tensor_tensor", [psa, CST], [t1], out=t1.ap[64:96, 0:T], in0=psa.ap[64:96, 0:T],
              in1=CST.ap[64:96, 0, 0:T], op=ALU.mult)
            I("dve", "tensor_tensor", [psb_, CST], [t2], out=t2.ap[64:96, 0:T], in0=psb_.ap[64:96, 0:T],
              in1=CST.ap[64:96, 1, 0:T], op=ALU.mult)
            I("pool" if h % 2 == 0 else "dve", "tensor_tensor", [t1, t2, QT[h]], [QT[h]], out=QT[h].ap[64:96, 0:T],
              in0=t1.ap[64:96, 0:T], in1=t2.ap[64:96, 0:T], op=ALU.add)

        def kv_load(sq, h, blk):
            kh = KH.next()
            vh = VH.next()
            kt0 = blk[0][0]
            nkeys = sum(t[1] for t in blk)
            tis = sorted(set(t[0] // 4 for t in blk))
            fw.dma("sp", kh.ap[0:96, 0:nkeys], sq.kts[h, :, kt0 * 128:kt0 * 128 + nkeys], [sq.KS[j] for j in tis], [kh], kh)
            pv = min(t[1] for t in blk)
            assert pv == 128 or all(t[1] == pv for t in blk)
            fw.dma("sp", vh.ap[0:pv, 0:len(blk), :], sq.vs[h, 0:pv, kt0:kt0 + len(blk), :], [sq.VSB[j] for j in tis], [vh], vh)
            return (kh, vh)

        def attn_head(sq, h, T, ktl, hooks, pre, prefetch):
            o = OB.next()
            nk_tiles = len(ktl)
            blocks = [ktl[i:i + 8] for i in range(0, nk_tiles, 8)]
            loaded = [pre]
            pf = {"done": False}

            def load(bi):
                loaded.append(kv_load(sq, h, blocks[bi]))
            flat = []
            for bi, blk in enumerate(blocks):
                for li, t in enumerate(blk):
                    flat.append((bi, li, t))
            units = []
            i_ = 0
            while i_ < len(flat):
                t_ = flat[i_][2]
                if (T == 512 and i_ + 1 < len(flat) and t_[1] == 128 and t_[2] == 0 and not t_[3]
                        and flat[i_ + 1][2][1] == 128 and flat[i_ + 1][2][2] == 0 and not flat[i_ + 1][2][3]):
                    units.append([flat[i_], flat[i_ + 1]])
                    i_ += 2
                else:
                    units.append([flat[i_]])
                    i_ += 1
            seen_blocks = set([0])

            def smm(unit):
                if len(unit) == 2 and G.i % 2:
                    G.i += 1
                banks = []
                for n_, item in enumerate(unit):
                    bi, li, (kt, nk, c0, masked) = item
                    if bi + 1 < len(blocks) and (bi + 1) not in seen_blocks:
                        seen_blocks.add(bi + 1)
                        load(bi + 1)
                    if bi == len(blocks) - 1 and not pf["done"]:
                        pf["done"] = True
                        prefetch()
                    kh, vh = loaded[bi]
                    ps = G.next()
                    I("pe", "matmul", [kh, QT[h]], [ps], ps.ap[0:nk, c0:T], lhsT=kh.ap[0:96, li * 128:li * 128 + nk],
                      rhs=QT[h].ap[0:96, c0:T], start=True, stop=True)
                    banks.append(ps)
                return banks

            DEPTH = 2
            pss = [smm(units[i_]) for i_ in range(min(DEPTH, len(units)))]
            done = 0
            for ui, unit in enumerate(units):
                banks = pss.pop(0)
                pts = PTS.next()
                if len(unit) == 2:
                    gi = int(banks[0].name[1:])
                    if os.environ.get("KSPLIT"):
                        for n_ in range(2):
                            I("act", "activation", banks, [pts], out=pts.ap[:, n_ * T:(n_ + 1) * T], in_=banks[n_].ap[:, 0:T],
                              func=AF.Exp, scale=SC_ATT)
                    else:
                        I("act", "activation", banks, [pts], out=pts.ap[:, 0:2 * T].rearrange("p (b t) -> p b t", t=T),
                          in_=psG[:, gi * 512:gi * 512 + 2 * T].rearrange("p (b t) -> p b t", t=T),
                          func=AF.Exp, scale=SC_ATT)
                else:
                    bi, li, (kt, nk, c0, masked) = unit[0]
                    I("act", "activation", banks, [pts], out=pts.ap[0:nk, c0:T], in_=banks[0].ap[0:nk, c0:T], func=AF.Exp, scale=SC_ATT)
                if ui + DEPTH < len(units):
                    pss.append(smm(units[ui + DEPTH]))
                for n_, item in enumerate(unit):
                    bi, li, (kt, nk, c0, masked) = item
                    kh, vh = loaded[bi]
                    if masked and done == 0:
                        I("pool", "memset", [pts], [pts], pts.ap[64:128, c0:c0 + 64], 0.0)
                        I("pe", "matmul", [vh, pts], [o], o.ap[0:65, c0:T], lhsT=vh.ap[0:nk, li, 0:65],
                          rhs=pts.ap[0:nk, n_ * T + c0:n_ * T + T], start=True, stop=(done == len(flat) - 1))
                    elif masked:
                        I("pe", "matmul", [vh, pts], [o], o.ap[0:65, c0 + 64:T], lhsT=vh.ap[0:128, li, 0:65],
                          rhs=pts.ap[0:128, c0 + 64:T], start=False, stop=False)
                        I("pe", "matmul", [vh, pts], [o], o.ap[0:65, c0:c0 + 64], lhsT=vh.ap[0:64, li, 0:65],
                          rhs=pts.ap[0:64, c0:c0 + 64], start=False, stop=(done == len(flat) - 1))
                    else:
                        I("pe", "matmul", [vh, pts], [o], o.ap[0:65, c0:T], lhsT=vh.ap[0:nk, li, 0:65],
                          rhs=pts.ap[0:nk, n_ * T + c0:n_ * T + T], start=(done == 0), stop=(done == len(flat) - 1))
                    done += 1
                for key in sorted(k_ for k_ in list(hooks) if k_ <= done - 1 or done == len(flat)):
                    hooks.pop(key)()
            return o

        def attn_fin_a(o, T):
            I("dve", "reciprocal", [o], [RL], out=RL.ap[64:65, 0:T], in_=o.ap[64:65, 0:T])
            I("dve", "tensor_copy", [RL], [RLH], out=RLH.ap[64:65, 0:T], in_=RL.ap[64:65, 0:T])
            I("dve", "tensor_tensor", [RL, RLH], [RLL], out=RLL.ap[64:65, 0:T], in0=RL.ap[64:65, 0:T],
              in1=RLH.ap[64:65, 0:T], op=ALU.subtract)

        def attn_fin_b(o, h, T):
            bc = GH.next()
            I("pe", "matmul", [ONB, RLH], [bc], bc.ap[0:64, 0:T], lhsT=ONB.ap[64:65, 0:64], rhs=RLH.ap[64:65, 0:T],
              start=True, stop=False)
            I("pe", "matmul", [ONB, RLL], [bc], bc.ap[0:64, 0:T], lhsT=ONB.ap[64:65, 0:64], rhs=RLL.ap[64:65, 0:T],
              start=False, stop=True)
            bcs = TMP.next()
            I("dve", "tensor_copy", [bc], [bcs], out=bcs.ap[0:64, 0:T], in_=bc.ap[0:64, 0:T])
            j, half = h // 2, (h % 2) * 64
            I("dve", "tensor_tensor", [o, bcs, ATT[j]], [ATT[j]], out=ATT[j].ap[half:half + 64, 0:T], in0=o.ap[0:64, 0:T],
              in1=bcs.ap[0:64, 0:T], op=ALU.mult)

        def attn_post(T):
            s4 = G.next()
            for j in range(4):
                sq = SQB.next()
                I("act", "activation", [ATT[j]], [sq], out=sq.ap[:, 0:T], in_=ATT[j].ap[:, 0:T], func=AF.Square)
                I("pe", "matmul", [ONB, sq], [s4], s4.ap[:, 0:T], lhsT=ONB.ap[:, :], rhs=sq.ap[:, 0:T],
                  start=(j == 0), stop=(j == 3))
            rsa = bcast_rstd(s4, T, 512, 1e-6)
            for j in range(4):
                I("dve", "tensor_tensor", [ATT[j], rsa], [ATT[j]], out=ATT[j].ap[:, 0:T], in0=ATT[j].ap[:, 0:T],
                  in1=rsa.ap[:, 0:T], op=ALU.mult)

        def x_load(nsub, TS, x_src, t0):
            for s in range(nsub):
                fw.dma("sp", X[s].ap[:TS, :], x_src[t0 + s * TS:t0 + (s + 1) * TS, :], [], [X[s]], X[s])

        def out_proj(T, nsub, TS, after=None):
            MIX = ATT + YN
            wps = []
            for n in range(2):
                wp = WPC.next()
                wv = wp.ap[:, :].rearrange("p (c n) -> p c n", n=512)
                fw.dma("sp", wv, wview(wos, n * 512, (n + 1) * 512), [], [wp], wp)
                wps.append((wp, wv))
            for s in range(nsub):
                for n in range(2):
                    wp, wv = wps[n]
                    ps = G.next()
                    for k in range(8):
                        I("pe", "matmul", [MIX[k], wp], [ps], ps.ap[:TS, :], lhsT=MIX[k].ap[:, s * TS:(s + 1) * TS],
                          rhs=wv[:, k, :], start=(k == 0), stop=(k == 7))
                    I("dve", "tensor_tensor", [ps, X[s]], [X[s]], out=X[s].ap[:TS, n * 512:(n + 1) * 512], in0=ps.ap[:TS, :],
                      in1=X[s].ap[:TS, n * 512:(n + 1) * 512], op=ALU.add)
                if after and s in after:
                    after[s]()

        def ffn(T, nsub, TS, slots=None, prenorm=None):
            slots = dict(slots or {})

            def run_slot(k):
                for f_ in slots.pop(k, []):
                    f_()
            if prenorm is None:
                norm_transpose(X[:nsub], list(range(nsub)), TS, HT)
            else:
                norm_b(prenorm, [2, 3], TS, [HT, HT], HT.ap)
            st_ = {}

            def up(g):
                wu = WPC.next()
                wuv = wu.ap[:, :].rearrange("p (c n) -> p c n", n=512)
                fw.dma("sp", wuv, wview(wus, g * 512, (g + 1) * 512), [], [wu], wu)
                wd = WPC.next()
                wdv = wd.ap[:, :].rearrange("p (j n) -> p j n", n=D)
                fw.dma("sp", wdv, wds_v[:, g * 4:(g + 1) * 4, :], [], [wd], wd)
                at = ATR.next()
                for jj in range(4):
                    ps = G.next()
                    for c in range(8):
                        I("pe", "matmul", [wu, HT], [ps], ps.ap[:, 0:T], lhsT=wuv[:, c, jj * 128:(jj + 1) * 128],
                          rhs=HT.ap[:, c, 0:T], start=(c == 0), stop=(c == 7))
                    r = TMP.next()
                    I("act", "activation", [ps], [r], out=r.ap[:, 0:T], in_=ps.ap[:, 0:T], func=AF.Relu)
                    I("dve", "tensor_tensor", [ps, r, at], [at], out=at.ap[:, jj, 0:T], in0=ps.ap[:, 0:T], in1=r.ap[:, 0:T],
                      op=ALU.mult)
                st_[g] = (at, wd, wdv)

            def down(g):
                at, wd, wdv = st_.pop(g)
                for n in range(2):
                    for s in range(nsub):
                        ps = G.next()
                        for jj in range(4):
                            I("pe", "matmul", [at, wd], [ps], ps.ap[:TS, :], lhsT=at.ap[:, jj, s * TS:(s + 1) * TS],
                              rhs=wdv[:, jj, n * 512:(n + 1) * 512], start=(jj == 0), stop=(jj == 3))
                        I("dve", "tensor_tensor", [ps, X[s]], [X[s]], out=X[s].ap[:TS, n * 512:(n + 1) * 512],
                          in0=ps.ap[:TS, :], in1=X[s].ap[:TS, n * 512:(n + 1) * 512], op=ALU.add)

            up(0)
            for g in range(8):
                run_slot(2 * g)
                if g + 1 < 8:
                    up(g + 1)
                run_slot(2 * g + 1)
                down(g)
            for k in sorted(slots):
                run_slot(k)

        def ple_pieces(T, nsub, TS, p_src, t0):
            subs = list(range(nsub))
            prs = [subs[i0:i0 + 2] for i0 in range(0, nsub, 2)]
            st_ = {}

            def p0():
                st_["xs0"] = norm_a([X[s] for s in prs[0]], prs[0], TS)
                fw.dma("pool", PB.ap[:TS, 0:nsub, :], p_src[t0:t0 + T, :].rearrange("(s p) d -> p s d", p=TS), [], [PB], PB)

            def p1():
                norm_b(st_.pop("xs0"), prs[0], TS, [HT] * len(prs[0]), HT.ap)
                if len(prs) > 1:
                    st_["xs1"] = norm_a([X[s] for s in prs[1]], prs[1], TS)
                for s in range(nsub):
                    tb = TB.next()
                    tv = tbv(tb)
                    for c in range(2):
                        I("pe", "transpose", [PB, IDB], [tb], out=tv[:, c, 0:TS], in_=PB.ap[:TS, s, c * 128:(c + 1) * 128],
                          identity=IDB.ap[:TS, :TS])
                    copy_on("dve", [tb], [PTR], PTR.ap[:, :, s * TS:(s + 1) * TS], tv[:, 0:2, 0:TS])
                wgl = []
                for n in range(2):
                    wg = WPC.next()
                    wgv = wg.ap[:, :].rearrange("p (c n) -> p c n", n=512)
                    fw.dma("sp", wgv, wview(wgs, n * 512, (n + 1) * 512), [], [wg], wg)
                    wgl.append((wg, wgv))
                st_["wg"] = wgl

            def p2():
                if len(prs) > 1:
                    norm_b(st_.pop("xs1"), prs[1], TS, [HT] * len(prs[1]), HT.ap)
                for n in range(2):
                    wg, wgv = st_["wg"][n]
                    for s in range(nsub):
                        ps = G.next()
                        for c in range(8):
                            I("pe", "matmul", [HT, wg], [ps], ps.ap[:TS, :], lhsT=HT.ap[:, c, s * TS:(s + 1) * TS],
                              rhs=wgv[:, c, :], start=(c == 0), stop=(c == 7))
                        sg = TMP.next()
                        I("act", "activation", [ps], [sg], out=sg.ap[:TS, :], in_=ps.ap[:TS, :], func=AF.Tanh, scale=0.5)
                        pp_ = G.next()
                        for c in range(2):
                            I("pe", "matmul", [PTR, WP], [pp_], pp_.ap[:TS, :], lhsT=PTR.ap[:, c, s * TS:(s + 1) * TS],
                              rhs=WP.ap[:, c, n * 512:(n + 1) * 512], start=(c == 0), stop=(c == 1))
                        tm = TMP.next()
                        I("dve", "scalar_tensor_tensor", [pp_, sg], [tm], out=tm.ap[:TS, :], in0=sg.ap[:TS, :], scalar=1.0,
                          in1=pp_.ap[:TS, :], op0=ALU.add, op1=ALU.mult)
                        I("pool", "tensor_tensor", [X[s], tm], [X[s]], out=X[s].ap[:TS, n * 512:(n + 1) * 512],
                          in0=X[s].ap[:TS, n * 512:(n + 1) * 512], in1=tm.ap[:TS, :], op=ALU.add)
            return [p0, p1, p2]

        def ple(T, nsub, TS, p_src, t0):
            for f_ in ple_pieces(T, nsub, TS, p_src, t0):
                f_()

        def final_store(T, nsub, TS, y_dst, t0):
            ybs, yvs = [], []
            for s in range(nsub):
                if s % 2 == 0:
                    yb = WPC.next()
                ybs.append(yb)
                yvs.append(yb.ap[:, :].bitcast(F32)[:TS, (s % 2) * D:(s % 2 + 1) * D])
            rs = group_stats([(X[s].ap[:TS, :], [X[s]], D, yvs[s], ybs[s]) for s in range(nsub)], TS, 1e-6)
            for s in range(nsub):
                I("dve", "scalar_tensor_tensor", [X[s], rs, BC], [ybs[s]], out=yvs[s], in0=X[s].ap[:TS, :],
                  scalar=rs.ap[:TS, s:s + 1], in1=BC.ap[:TS, 256:256 + D], op0=ALU.mult, op1=ALU.mult)
                fw.dma("pool", y_dst[t0 + s * TS:t0 + (s + 1) * TS, :], yvs[s], [ybs[s]], [], ybs[s])

        def front_slots(T, nsub, TS, t0, x_src, cs_src, cst_src, ckv_dst, kr_dst):
            st_ = {}

            def n_a(subs, first):
                def f():
                    if first:
                        fw.dma("sp", CS.ap[:TS, 0:nsub, :], cs_src[t0:t0 + T, :].rearrange("(s p) d -> p s d", p=TS), [], [CS], CS)
                        fw.dma("sp", CST.ap[:, :, 0:T], cst_src[:, :, t0:t0 + T], [], [CST], CST)
                    xfs = []
                    for s in subs:
                        xf = XF.next()
                        fw.dma("sp", xf.ap[:TS, :], x_src[t0 + s * TS:t0 + (s + 1) * TS, :], [], [xf], xf)
                        xfs.append(xf)
                    st_[("xs", tuple(subs))] = norm_a(xfs, subs, TS)
                return f

            def n_b(subs):
                def f():
                    norm_b(st_.pop(("xs", tuple(subs))), subs, TS, [HTP[s] for s in subs], HTPT)
                return f

            def s_a(s):
                def f():
                    psA = G.next()
                    psB = G.next()
                    for c in range(8):
                        I("pe", "matmul", [HTP[s], WIN], [psA], psA.ap[:TS, 0:320], lhsT=HTPT[:, c, s * TS:(s + 1) * TS],
                          rhs=WIN.ap[:, c, 0:320], start=(c == 0), stop=(c == 7))
                    for c in range(8):
                        I("pe", "matmul", [HTP[s], WIN], [psB], psB.ap[:TS, 0:384], lhsT=HTPT[:, c, s * TS:(s + 1) * TS],
                          rhs=WIN.ap[:, c, 320:704], start=(c == 0), stop=(c == 7))
                    st_[("ps", s)] = (psA, psB)
                return f

            def s_b(s):
                def f():
                    psA, psB = st_.pop(("ps", s))
                    cko = CKVO.next()
                    cqb = CQB.next()
                    rs = group_stats([(psA.ap[:TS, 0:KVL], [psA], KVL, cko.ap[:TS, :], cko),
                                      (psB.ap[:TS, 0:QLORA], [psB], QLORA, cqb.ap[:TS, :], cqb)], TS, 1e-6)
                    I("dve", "scalar_tensor_tensor", [psA, rs, BC], [cko], out=cko.ap[:TS, :], in0=psA.ap[:TS, 0:KVL],
                      scalar=rs.ap[:TS, 0:1], in1=BC.ap[:TS, 0:KVL], op0=ALU.mult, op1=ALU.mult)
                    fw.dma("pool", ckv_dst[t0 + s * TS:t0 + (s + 1) * TS, :], cko.ap[:TS, :], [cko], [], cko)
                    ckb = CKVB.next()
                    I("pool", "tensor_copy", [cko], [ckb], out=ckb.ap[:TS, :], in_=cko.ap[:TS, :])
                    t1 = SMALL.next()
                    t2 = SMALL.next()
                    I("dve", "tensor_tensor", [psA, CS], [t1], out=t1.ap[:TS, :], in0=psA.ap[:TS, 256:288],
                      in1=CS.ap[:TS, s, 0:32], op=ALU.mult)
                    I("dve", "tensor_tensor", [psA, CS], [t2], out=t2.ap[:TS, :], in0=psA.ap[:TS, 288:320],
                      in1=CS.ap[:TS, s, 32:64], op=ALU.mult)
                    kro = KRO.next()
                    I("pool", "tensor_tensor", [t1, t2], [kro], out=kro.ap[:TS, :], in0=t1.ap[:TS, :], in1=t2.ap[:TS, :], op=ALU.add)
                    fw.dma("pool", kr_dst[t0 + s * TS:t0 + (s + 1) * TS, :], kro.ap[:TS, :], [kro], [], kro)
                    krb = KRB.next()
                    I("pool", "tensor_copy", [kro], [krb], out=krb.ap[:TS, 64:96], in_=kro.ap[:TS, :])
                    I("dve", "tensor_scalar", [psB, rs], [cqb], out=cqb.ap[:TS, :], in0=psB.ap[:TS, 0:QLORA],
                      scalar1=rs.ap[:TS, 1:2], scalar2=None, op0=ALU.mult)
                    st_[("b", s)] = (ckb, krb, cqb)
                return f

            def s_c(s):
                def f():
                    ckb, krb, cqb = st_.pop(("b", s))
                    transposes_small(ckb, krb, cqb, s, TS)
                return f

            def q_(h):
                return lambda: q_head(h, T)

            if nsub == 4:
                return {0: [n_a([0, 1], True)], 2: [n_b([0, 1]), n_a([2, 3], False)], 4: [n_b([2, 3]), s_a(0), s_b(0)],
                        5: [s_a(1), s_b(1)], 6: [s_c(0), s_a(2), s_b(2)], 7: [s_c(1), s_a(3), s_b(3)], 8: [s_c(2)], 9: [s_c(3)],
                        10: [q_(0)], 11: [q_(1), q_(2)], 12: [q_(3)], 13: [q_(4), q_(5)], 14: [q_(6)], 15: [q_(7)]}
            subs = list(range(nsub))
            return {0: [n_a(subs, True), n_b(subs)] + [g_(s) for s in subs for g_ in (s_a, s_b, s_c)] + [q_(h) for h in range(NH)]}

        def run_slots(slots):
            for k in sorted(slots):
                for f_ in slots[k]:
                    f_()

        def mid(sq, T, nsub, TS, t0, x_src, ktl, inter=None, pre0=None):
            if inter is None:
                x_load(nsub, TS, x_src, t0)
                inter = {}
            prev = None
            nkt = len(ktl)
            pre = pre0 if pre0 is not None else kv_load(sq, 0, ktl[0:8])
            cw = conv_w_load(0)
            for h in range(NH):
                nxt = {}

                def prefetch(h=h, nxt=nxt):
                    if h + 1 < NH:
                        nxt["v"] = kv_load(sq, h + 1, ktl[0:8])
                hooks = {}
                last = nkt - 1

                def addhook(pos, fn, hooks=hooks):
                    pos = min(pos, last)
                    while pos in hooks:
                        pos += 0.01
                    hooks[pos] = fn
                if prev is not None:
                    po, ph = prev
                    addhook(0, (lambda po=po: attn_fin_a(po, T)))
                    addhook(max(6, last - 4), (lambda po=po, ph=ph: attn_fin_b(po, ph, T)))
                if h % 2 == 0:
                    tg = {}

                    def conv_a(h=h, cw=cw, tg=tg):
                        tg["g"] = conv_chunk(h // 2, T, cw)
                    addhook(3, conv_a)
                    addhook(7, (lambda tg=tg: tg["g"][0]()))
                    addhook(11, (lambda tg=tg: tg["g"][1]()))
                    tg_prev = tg
                else:
                    addhook(3, (lambda tg=tg_prev: tg["g"][2]()))
                    addhook(7, (lambda tg=tg_prev: tg["g"][3]()))
                    if h + 1 < NH:
                        cw = conv_w_load((h + 1) // 2)
                o = attn_head(sq, h, T, ktl, hooks, pre, prefetch)
                pre = nxt.get("v")
                prev = (o, h)
                if h in inter:
                    inter[h]()
            attn_fin_a(prev[0], T)
            attn_fin_b(prev[0], prev[1], T)
            attn_post(T)
            conv_post(T)
            if nsub == 4:
                hold = {}

                def a1():
                    hold["a"] = norm_a(X[0:2], [0, 1], TS)

                def a3():
                    norm_b(hold.pop("a"), [0, 1], TS, [HT, HT], HT.ap)
                    hold["b"] = norm_a(X[2:4], [2, 3], TS)
                out_proj(T, nsub, TS, {1: a1, 3: a3})
                return hold["b"]
            out_proj(T, nsub, TS)
            return None

        def conv_state_out(dst):
            ps = G.next()
            for k in range(4):
                I("pe", "transpose", [UT, IDF], [ps], out=ps.ap[0:CK - 1, k * 128:(k + 1) * 128], in_=UT.ap[:, k, :],
                  identity=IDF.ap[:, :])
            tm = TMP.next()
            I("act", "copy", [ps], [tm], out=tm.ap[0:CK - 1, :], in_=ps.ap[0:CK - 1, :])
            fw.dma("pool", dst, tm.ap[0:CK - 1, :], [tm], [], tm)

        sp_ = Seq()
        sp_.kts, sp_.vs = kts_p, vs_p
        sp_.KS = [Buf("KSp%d" % j, None) for j in range(NT)]
        sp_.VSB = [Buf("VSp%d" % j, None) for j in range(NT)]
        I("pool", "memset", [], [UT], UT.ap[:, :, :], 0.0)

        def fp_prompt(i):
            return front_slots(512, 4, 128, i * 512, x_p, cs_p, cst_p, ckv_p, kr_p)

        run_slots(fp_prompt(0))
        kv_expand(sp_, 512, 4, 128, 0)
        fold_all_scratch()
        pre0 = None
        for i in range(NT):
            ktl = [(kt, 128, 0, False) for kt in range(4 * i)] + [(4 * i + r, 128, 128 * r, True) for r in range(4)]
            inter = None
            if i > 0:
                pp3 = ple_pieces(512, 4, 128, p_p, (i - 1) * 512)

                def after_h3(i=i):
                    final_store(512, 4, 128, y_p, (i - 1) * 512)

                def after_h6(i=i):
                    x_load(4, 128, x_p, i * 512)
                inter = {0: pp3[0], 1: pp3[1], 2: pp3[2], 3: after_h3, 6: after_h6}
            pn = mid(sp_, 512, 4, 128, i * 512, x_p, ktl, inter, pre0)
            ffn(512, 4, 128, fp_prompt(i + 1) if i + 1 < NT else None, pn)
            pre0 = None
            if i + 1 < NT:
                if i + 1 >= 2:
                    ktl_n = [(kt, 128, 0, False) for kt in range(4 * (i + 1))]
                    pre0 = kv_load(sp_, 0, ktl_n[0:8])
                kv_expand(sp_, 512, 4, 128, (i + 1) * 512)
        ple(512, 4, 128, p_p, (NT - 1) * 512)
        final_store(512, 4, 128, y_p, (NT - 1) * 512)
        conv_state_out(conv_p)

        if with_sample and WSAMPLE:
            ss_ = Seq()
            ss_.kts, ss_.vs = kts_s, vs_s
            ss_.KS = [Buf("KSs%d" % j, None) for j in range(3)]
            ss_.VSB = [Buf("VSs%d" % j, None) for j in range(3)]
            for blk in range(PAST // 512):
                for s in range(4):
                    r0 = blk * 512 + s * 128
                    ckb = CKVB.next()
                    fw.dma("pool", ckb.ap[:, :], c_ckv[r0:r0 + 128, :], [], [ckb], ckb)
                    krb = KRB.next()
                    fw.dma("pool", krb.ap[:, 64:96], c_kr[r0:r0 + 128, :], [], [krb], krb)
                    transposes_small(ckb, krb, None, s, 128)
                kv_expand(ss_, 512, 4, 128, blk * 512)
            tm = TMP.next()
            fw.dma("sp", tm.ap[0:CK - 1, :], s_conv, [], [tm], tm)
            ps = G.next()
            for k in range(4):
                I("pe", "transpose", [tm, IDF], [ps], out=ps.ap[:, k * 32:k * 32 + CK - 1], in_=tm.ap[0:CK - 1, k * 128:(k + 1) * 128],
                  identity=IDF.ap[0:CK - 1, 0:CK - 1])
            I("dve", "tensor_copy", [ps], [UT], out=UT.ap[:, :, :],
              in_=ps.ap[:, 0:128].rearrange("p (k t) -> p k t", t=32)[:, :, 0:CK - 1])
            run_slots(front_slots(DEC, 1, DEC, 0, x_s, cs_s, cst_s, ckv_s, kr_s))
            kv_expand(ss_, DEC, 1, DEC, PAST)
            ktl = [(kt, 128, 0, False) for kt in range(PAST // 128)] + [(PAST // 128, DEC, 0, False)]
            mid(ss_, DEC, 1, DEC, 0, x_s, ktl)
            ffn(DEC, 1, DEC)
            ple(DEC, 1, DEC, p_s, 0)
            final_store(DEC, 1, DEC, y_s, 0)
            conv_state_out(conv_s)

        fw.finish()
        nc._sbuf_bytes = fw.sbuf_bytes
    return nc


def _rope_tables(pos):
    half = ROPE // 2
    inv = (10000.0 ** (-np.arange(half, dtype=np.float32) / half)).astype(np.float32)
    ang = pos.astype(np.float32)[:, None] * inv[None, :]
    cos, sin = np.cos(ang).astype(np.float32), np.sin(ang).astype(np.float32)
    cos32 = np.concatenate([cos, cos], axis=1)
    ssin32 = np.concatenate([-sin, sin], axis=1)
    tok = np.ascontiguousarray(np.concatenate([cos32, ssin32], axis=1))
    ft = np.zeros((96, 2, pos.shape[0]), np.float32)
    ft[64:96, 0, :] = cos32.T
    ft[64:96, 1, :] = ssin32.T
    return tok, ft


def _prep_shared(inp, S):
    f = lambda a: np.ascontiguousarray(np.asarray(a, dtype=np.float32))
    w_in = f(inp["w_in"])[0]
    cq, ckv, kr, conv = w_in[:, 0:384], w_in[:, 384:640], w_in[:, 640:672], w_in[:, 672:1696]
    krs = np.concatenate([kr[:, 16:32], kr[:, 0:16]], axis=1)
    conv_l = np.concatenate([np.concatenate([conv[:, k * 128:(k + 1) * 128], conv[:, 512 + k * 128:512 + (k + 1) * 128]], axis=1)
                             for k in range(4)], axis=1)
    w_in_ext = np.ascontiguousarray(np.concatenate([ckv, kr, krs, cq, conv_l], axis=1))
    w_uq = f(inp["w_uq"])[0]
    blocks = []
    for h in range(NH):
        a = w_uq[:, h * 96:(h + 1) * 96]
        b = np.concatenate([np.zeros((QLORA, 64), np.float32), a[:, 80:96], a[:, 64:80]], axis=1)
        blocks += [a, b]
    w_uq_ext = np.ascontiguousarray(np.concatenate(blocks, axis=1))
    w_ukv = f(inp["w_ukv"])[0].reshape(KVL, NH, 128)
    w_uk = np.ascontiguousarray(w_ukv[:, :, 0:64].reshape(KVL, 512))
    w_uv = np.ascontiguousarray(w_ukv[:, :, 64:128].reshape(KVL, 512))

    def chunks(v, n):
        return np.asarray(v, np.float32).reshape(n, 128).T

    gout = np.concatenate([f(inp["attn_out_g"])[0], f(inp["conv_out_g"])[0]])
    cw = f(inp["conv_w"])[0]
    cw_l = cw.T.reshape(4, 128, CK).transpose(1, 0, 2).reshape(128, 4 * CK)
    pp = np.concatenate([
        chunks(f(inp["norm_mix_g"])[0], 8), chunks(f(inp["q_norm_g"])[0], 3), chunks(f(inp["norm_ffn_g"])[0], 8),
        chunks(f(inp["norm_ple_g"])[0], 8), chunks(gout, 8), chunks(f(inp["conv_b"])[0], 4),
        chunks(f(inp["conv_ln_g"])[0], 4), chunks(f(inp["conv_ln_b"])[0], 4), cw_l], axis=1)
    assert pp.shape == (128, 171)
    bc = np.concatenate([np.broadcast_to(f(inp["kv_norm_g"])[0][None, :], (128, KVL)),
                         np.broadcast_to(f(inp["norm_final_g"])[None, :], (128, D))], axis=1)
    cs_p, cst_p = _rope_tables(np.arange(S))
    cs_s, cst_s = _rope_tables(PAST + np.arange(DEC))
    return dict(w_in=w_in_ext, w_uq=w_uq_ext, w_uk=w_uk, w_uv=w_uv, w_out=f(inp["w_out"])[0], w_up=f(inp["w_ff_up"])[0],
                w_down=f(inp["w_ff_down"])[0], w_gate=f(inp["w_ple_gate"])[0], w_pp=f(inp["w_ple_proj"])[0],
                pp=np.ascontiguousarray(pp), bc=np.ascontiguousarray(bc), cs_p=cs_p, cs_s=cs_s, cst_p=cst_p, cst_s=cst_s)


_NC_CACHE = {}


def run(inputs, S=SEQ, ncores=NCORES, with_sample=True, trace=False):
    f = lambda a: np.ascontiguousarray(np.asarray(a, dtype=np.float32))
    shared = _prep_shared(inputs, S)
    key = (S, with_sample)
    if key not in _NC_CACHE:
        _NC_CACHE[key] = build_nc(S, with_sample)
    nc = _NC_CACHE[key]
    in_maps = []
    for b in range(ncores):
        m = dict(shared)
        m["x_p"] = f(inputs["x_prompt"][b][:S])
        m["p_p"] = f(inputs["p_prompt"][0, b][:S])
        m["x_s"] = f(inputs["x_sample"][b])
        m["p_s"] = f(inputs["p_sample"][0, b])
        m["c_ckv"] = f(inputs["cache_ckv"][0, b])
        m["c_kr"] = f(inputs["cache_krope"][0, b])
        m["s_conv"] = f(inputs["state_conv"][0, b])
        in_maps.append(m)
    res = run_bass_kernel_spmd(nc, in_maps, core_ids=list(range(ncores)), trace=trace)
    r = res.results
    st = lambda k: np.stack([np.asarray(r[b][k], dtype=np.float32) for b in range(ncores)])
    outs = (st("y_p"), st("y_s"), st("ckv_p")[None], st("kr_p")[None], st("conv_p")[None],
            st("ckv_s")[None], st("kr_s")[None], st("conv_s")[None])
    return outs, res


def kernel(**inputs):
    outs, _ = run(inputs)
    return outs
```
